# Optimizing a Trainium2 kernel written in Bass

```python
import jax, jax.numpy as jnp
from jax import lax
import numpy as np

D_MODEL = 1024
BATCH = 16
SEQ = 256
DEPTH = 1
DEC_BATCH = 2
DEC_SEQ = 4096
PAST_LEN = 256

GRID_W = 64
NH_M = 4
DH_M = D_MODEL // NH_M
D_MLSTM = NH_M * DH_M
CHUNK = 64
N_Q = 8
N_KV = 2
G_Q = N_Q // N_KV
DH_A = 128
D_ATT_Q = N_Q * DH_A
D_ATT_KV = N_KV * DH_A
N_FREQ = DH_A // 4
ROPE_BASE = 10000.0
D_FF = 4 * D_MODEL
Q_BLOCK = 128
EPS = 1e-6
ALPHA = (2 * DEPTH) ** 0.25
BETA = (8 * DEPTH) ** -0.25
SPLITS = (D_MLSTM, D_MLSTM, D_MLSTM, D_MLSTM, 4 * NH_M, D_ATT_Q, D_ATT_KV, D_ATT_KV, D_MODEL, D_MODEL)
D_IN = sum(SPLITS)

kernel_name = 'hybrid_mlstm_gqa_dit_step'


def _split(x, sizes):
    idx = np.cumsum(sizes)[:-1].tolist()
    return jnp.split(x, idx, axis=-1)


def layer_norm(x, g, b):
    xf = x.astype(jnp.float32)
    mu = jnp.mean(xf, -1, keepdims=True)
    var = jnp.mean(jnp.square(xf - mu), -1, keepdims=True)
    return ((xf - mu) * lax.rsqrt(var + EPS)).astype(x.dtype) * g + b


def rms_norm(x, g):
    xf = x.astype(jnp.float32)
    return (xf * lax.rsqrt(jnp.mean(jnp.square(xf), -1, keepdims=True) + EPS)).astype(x.dtype) * g


def axial_rope_tables(n_tokens):
    rows = n_tokens // GRID_W
    row = jnp.repeat(jnp.arange(rows), GRID_W)
    col = jnp.tile(jnp.arange(GRID_W), rows)
    inv = ROPE_BASE ** (-jnp.arange(N_FREQ, dtype=jnp.float32) / N_FREQ)
    ang = jnp.stack([row, col], -1).astype(jnp.float32)[..., None] * inv
    ang = jnp.broadcast_to(ang[:, :, None, :], (n_tokens, 2, 2, N_FREQ))
    return jnp.cos(ang), jnp.sin(ang)


def apply_rope(x, cos, sin):
    xs = x.reshape(*x.shape[:-1], 2, 2, N_FREQ)
    rot = jnp.concatenate([-xs[..., 1:, :], xs[..., :1, :]], axis=-2)
    out = xs * cos[None, :, None] + rot * sin[None, :, None]
    return out.reshape(x.shape).astype(x.dtype)


def block_attention(q, k, v):
    B, T = q.shape[:2]
    nb = T // Q_BLOCK
    qb = jnp.moveaxis(q.reshape(B, nb, Q_BLOCK, N_KV, G_Q, DH_A), 1, 0)
    scale = DH_A ** -0.5

    def one_block(qblk):
        s = jnp.einsum('bqhgd,bkhd->bhgqk', qblk, k).astype(jnp.float32) * scale
        p = jax.nn.softmax(s, axis=-1).astype(v.dtype)
        return jnp.einsum('bhgqk,bkhd->bqhgd', p, v)

    o = lax.map(one_block, qb)
    return jnp.moveaxis(o, 0, 1).reshape(B, T, D_ATT_Q)


def mlstm_chunked(q, k, v, ig, lf, C0, n0, m0):
    B, H, T, _ = q.shape
    nc = T // CHUNK

    def to_chunks(a):
        return jnp.moveaxis(a.reshape(B, H, nc, CHUNK, *a.shape[3:]), 2, 0)

    causal = jnp.tril(jnp.ones((CHUNK, CHUNK), bool))

    def step(carry, xs):
        C, n, m = carry
        qc, kc, vc, ic, fc = xs
        b = jnp.cumsum(fc, -1)
        dmat = jnp.where(causal, b[..., :, None] - b[..., None, :] + ic[..., None, :], -jnp.inf)
        g_inter = b + m[..., None]
        m_row = jnp.maximum(g_inter, jnp.max(dmat, -1))
        s = jnp.einsum('bhid,bhjd->bhij', qc, kc) * jnp.exp(dmat - m_row[..., None])
        w_inter = jnp.exp(g_inter - m_row)
        num = w_inter[..., None] * jnp.einsum('bhid,bhde->bhie', qc, C) + jnp.einsum('bhij,bhje->bhie', s, vc)
        den = w_inter * jnp.einsum('bhid,bhd->bhi', qc, n) + jnp.sum(s, -1)
        h = num / jnp.maximum(jnp.abs(den), jnp.exp(-m_row))[..., None]
        b_last = b[..., -1]
        w_end = b_last[..., None] - b + ic
        m_new = jnp.maximum(b_last + m, jnp.max(w_end, -1))
        decay = jnp.exp(b_last + m - m_new)
        wk = jnp.exp(w_end - m_new[..., None])[..., None] * kc
        C_new = decay[..., None, None] * C + jnp.einsum('bhjd,bhje->bhde', wk, vc)
        n_new = decay[..., None] * n + jnp.sum(wk, -2)
        return (C_new, n_new, m_new), h

    carry0 = (C0.astype(jnp.float32), n0.astype(jnp.float32), m0.astype(jnp.float32))
    (C, n, m), hs = lax.scan(step, carry0, (to_chunks(q), to_chunks(k), to_chunks(v), to_chunks(ig), to_chunks(lf)))
    h = jnp.moveaxis(hs, 0, 2).reshape(B, H, T, -1)
    return h, C, n, m


def mlstm_bidir(q, k, v, gates, C0, n0, m0):
    B, T = q.shape[:2]
    tr = lambda a: jnp.transpose(a.astype(jnp.float32), (0, 2, 1, 3))
    qf, kf, vf = tr(q), tr(k) * (DH_M ** -0.5), tr(v)
    g4 = jnp.transpose(gates.astype(jnp.float32).reshape(B, T, 4, NH_M), (2, 0, 3, 1))
    ig_f, lf_f, ig_b, lf_b = g4[0], jax.nn.log_sigmoid(g4[1]), g4[2], jax.nn.log_sigmoid(g4[3])
    hf, Cf, nf, mf = mlstm_chunked(qf, kf, vf, ig_f, lf_f, C0[:, 0], n0[:, 0], m0[:, 0])
    flip = lambda a: jnp.flip(a, axis=2)
    hb, Cb, nb, mb = mlstm_chunked(flip(qf), flip(kf), flip(vf), flip(ig_b), flip(lf_b), C0[:, 1], n0[:, 1], m0[:, 1])
    h = jnp.transpose(hf + flip(hb), (0, 2, 1, 3)).astype(q.dtype)
    return h, jnp.stack([Cf, Cb], 1), jnp.stack([nf, nb], 1), jnp.stack([mf, mb], 1)


def mixer(h, w_in, b_gates, mlstm_norm_g, q_norm_g, k_norm_g, w_bm, w_ba, w_out, C0, n0, m0, ctx_kv, rope):
    B, T, _ = h.shape
    qm, km, vm, om, gates, qa, ka, va, gm, ga = _split(h @ w_in, SPLITS)
    shp_m = (B, T, NH_M, DH_M)
    hm, C, n, m = mlstm_bidir(qm.reshape(shp_m), km.reshape(shp_m), vm.reshape(shp_m), gates + b_gates, C0, n0, m0)
    hm = rms_norm(hm, mlstm_norm_g.reshape(NH_M, DH_M)).reshape(B, T, D_MLSTM) * jax.nn.sigmoid(om)
    qa = rms_norm(qa.reshape(B, T, N_Q, DH_A), q_norm_g)
    ka = rms_norm(ka.reshape(B, T, N_KV, DH_A), k_norm_g)
    va = va.reshape(B, T, N_KV, DH_A)
    if rope is None:
        keys, vals = ka, va
    else:
        qa = apply_rope(qa, *rope)
        ka = apply_rope(ka, *rope)
        keys = jnp.concatenate([ka, ctx_kv[0].astype(ka.dtype)], axis=1)
        vals = jnp.concatenate([va, ctx_kv[1].astype(va.dtype)], axis=1)
    ha = block_attention(qa.reshape(B, T, N_KV, G_Q, DH_A), keys, vals)
    merged = jax.nn.sigmoid(gm) * (hm @ w_bm) + jax.nn.sigmoid(ga) * (ha @ w_ba)
    return merged @ w_out, (ka, va), (C, n, m)


def trunk_layer(x, mod, w_in, b_gates, mlstm_norm_g, q_norm_g, k_norm_g, w_bm, w_ba, w_out,
                ln1_g, ln1_b, w_up, w_down, ln2_g, ln2_b, C0, n0, m0, ctx_kv, rope):
    sh1, sc1, g1, sh2, sc2, g2 = jnp.split(mod[:, None, :], 6, axis=-1)
    h = x * (1 + sc1) + sh1
    mix, kv, st = mixer(h, w_in, b_gates, mlstm_norm_g, q_norm_g, k_norm_g, w_bm, w_ba, w_out, C0, n0, m0, ctx_kv, rope)
    x = layer_norm(ALPHA * x + g1 * mix, ln1_g, ln1_b)
    h = x * (1 + sc2) + sh2
    ff = jnp.square(jax.nn.relu(h @ w_up)) @ w_down
    x = layer_norm(ALPHA * x + g2 * ff, ln2_g, ln2_b)
    return x, kv, st


def setup_inputs(seed: int = 0) -> dict:
    key = jax.random.key(seed)
    ks = jax.random.split(key, 26)
    nrm = lambda k, shape, s: jax.random.normal(k, shape, jnp.float32) * s
    col_scale = jnp.asarray(np.concatenate(
        [np.full(s, BETA if i in (2, 7) else 1.0, np.float32) for i, s in enumerate(SPLITS)]))
    gate_offset = jnp.repeat(jnp.array([0.0, 3.0, 0.0, 3.0], jnp.float32), NH_M)
    return {
        'x_prompt': nrm(ks[0], (BATCH, SEQ, D_MODEL), 1.0),
        'x_sample': nrm(ks[1], (DEC_BATCH, DEC_SEQ, D_MODEL), 1.0),
        'cache_k': nrm(ks[2], (DEC_BATCH, DEPTH, PAST_LEN, N_KV, DH_A), 1.0),
        'cache_v': nrm(ks[3], (DEC_BATCH, DEPTH, PAST_LEN, N_KV, DH_A), 0.5),
        'state_C': nrm(ks[4], (DEC_BATCH, DEPTH, 2, NH_M, DH_M, DH_M), 0.05),
        'state_n': nrm(ks[5], (DEC_BATCH, DEPTH, 2, NH_M, DH_M), 0.05),
        'state_m': nrm(ks[6], (DEC_BATCH, DEPTH, 2, NH_M), 0.5),
        'c': nrm(ks[7], (DEC_BATCH, D_MODEL), 1.0),
        'c_ctx': nrm(ks[8], (D_MODEL,), 1.0),
        'w_mod': nrm(ks[9], (DEPTH, D_MODEL, 6 * D_MODEL), D_MODEL ** -0.5),
        'b_mod': nrm(ks[10], (DEPTH, 6 * D_MODEL), 0.02),
        'w_in': nrm(ks[11], (DEPTH, D_MODEL, D_IN), D_MODEL ** -0.5) * col_scale,
        'b_gates': nrm(ks[12], (DEPTH, 4 * NH_M), 0.1) + gate_offset,
        'mlstm_norm_g': 1.0 + nrm(ks[13], (DEPTH, D_MLSTM), 0.02),
        'q_norm_g': 1.0 + nrm(ks[14], (DEPTH, DH_A), 0.02),
        'k_norm_g': 1.0 + nrm(ks[15], (DEPTH, DH_A), 0.02),
        'w_bm': nrm(ks[16], (DEPTH, D_MLSTM, D_MODEL), D_MLSTM ** -0.5),
        'w_ba': nrm(ks[17], (DEPTH, D_ATT_Q, D_MODEL), D_ATT_Q ** -0.5),
        'w_out': nrm(ks[18], (DEPTH, D_MODEL, D_MODEL), BETA * D_MODEL ** -0.5),
        'ln1_g': 1.0 + nrm(ks[19], (DEPTH, D_MODEL), 0.02),
        'ln1_b': nrm(ks[20], (DEPTH, D_MODEL), 0.02),
        'w_up': nrm(ks[21], (DEPTH, D_MODEL, D_FF), D_MODEL ** -0.5),
        'w_down': nrm(ks[22], (DEPTH, D_FF, D_MODEL), BETA * D_FF ** -0.5),
        'ln2_g': 1.0 + nrm(ks[23], (DEPTH, D_MODEL), 0.02),
        'ln2_b': nrm(ks[24], (DEPTH, D_MODEL), 0.02),
    }


def reference(x_prompt, x_sample, cache_k, cache_v, state_C, state_n, state_m, c, c_ctx,
              w_mod, b_mod, w_in, b_gates, mlstm_norm_g, q_norm_g, k_norm_g, w_bm, w_ba, w_out,
              ln1_g, ln1_b, w_up, w_down, ln2_g, ln2_b):
    B = x_prompt.shape[0]
    zC = jnp.zeros((B, 2, NH_M, DH_M, DH_M), jnp.float32)
    zn = jnp.zeros((B, 2, NH_M, DH_M), jnp.float32)
    zm = jnp.zeros((B, 2, NH_M), jnp.float32)
    rope = axial_rope_tables(x_sample.shape[1])
    xp, xs = x_prompt, x_sample
    ks_, vs_, Cs_, ns_, ms_ = [], [], [], [], []
    for l in range(DEPTH):
        lw = (w_in[l], b_gates[l], mlstm_norm_g[l], q_norm_g[l], k_norm_g[l], w_bm[l], w_ba[l], w_out[l],
              ln1_g[l], ln1_b[l], w_up[l], w_down[l], ln2_g[l], ln2_b[l])
        mod_ctx = jax.nn.silu(c_ctx)[None, :] @ w_mod[l] + b_mod[l]
        mod_lat = jax.nn.silu(c) @ w_mod[l] + b_mod[l]
        xp, (k_l, v_l), (C_l, n_l, m_l) = trunk_layer(xp, mod_ctx, *lw, zC, zn, zm, None, None)
        ks_.append(k_l); vs_.append(v_l); Cs_.append(C_l); ns_.append(n_l); ms_.append(m_l)
        xs, _, _ = trunk_layer(xs, mod_lat, *lw, state_C[:, l], state_n[:, l], state_m[:, l],
                               (cache_k[:, l], cache_v[:, l]), rope)
    return (xp, xs, jnp.stack(ks_, 1), jnp.stack(vs_, 1), jnp.stack(Cs_, 1), jnp.stack(ns_, 1), jnp.stack(ms_, 1))
```

```python
import math
import numpy as np
import concourse.bass as bass
import concourse.mybir as mybir
from concourse.bass_utils import run_bass_kernel_spmd

F32 = mybir.dt.float32
BF16 = mybir.dt.bfloat16
AF = mybir.ActivationFunctionType
ALU = mybir.AluOpType
AX = mybir.AxisListType
PE, ACT, DVE, POOL, SP = "pe", "act", "dve", "pool", "sp"
N_DMA_SLOTS = 8
EPS = 1e-6
ALPHA = 2.0 ** 0.25
LN16 = math.log(16.0)
NEG = -30000.0


class Buf:
    __slots__ = ("name", "w", "r", "excl")

    def __init__(self, name, excl=False):
        self.name = name
        self.w = None
        self.r = {}
        self.excl = excl


class Sched:
    def __init__(self, nc):
        self.nc = nc
        self.lists = {e: [] for e in (PE, ACT, DVE, POOL, SP)}
        self.sig = {e: 0 for e in (PE, ACT, DVE, POOL)}
        self.pending = {e: False for e in (PE, ACT, DVE, POOL)}
        self.waited = {}
        self.dma_n = {}
        self.dma_rr = {SP: 0, POOL: 0, ACT: 0}
        self.out_tokens = []

    def _deps(self, reads, writes, eng=None):
        deps = {}

        def add(tok):
            if tok is None:
                return
            k, v = tok
            if deps.get(k, 0) < v:
                deps[k] = v
        for b in reads:
            add(b.w)
            if b.excl:
                for k, v in b.r.items():
                    if k != eng:
                        add((k, v))
        for b in writes:
            add(b.w)
            for k, v in b.r.items():
                add((k, v))
        return deps

    def _waits(self, eng, deps):
        waits = []
        for k, v in deps.items():
            if k == PE and eng == PE:
                continue
            if k in (PE, ACT, DVE, POOL):
                assert self.sig[k] >= v, f"dependency on unsignalled {k} instruction"
            if self.waited.get((eng, k), 0) >= v:
                continue
            self.waited[(eng, k)] = v
            waits.append((k, v))
        return waits

    def op(self, eng, fn, reads=(), writes=(), signal=True):
        deps = self._deps(reads, writes, eng)
        waits = self._waits(eng, deps)
        if signal:
            self.sig[eng] += 1
            tok = (eng, self.sig[eng])
            self.pending[eng] = False
        else:
            tok = (eng, self.sig[eng] + 1)
            self.pending[eng] = True
        self.lists[eng].append((fn, waits, (eng, 1) if signal else None))
        for b in reads:
            if b.r.get(eng, 0) < tok[1]:
                b.r[eng] = tok[1]
        for b in writes:
            b.w = tok
            b.r = {}
        return tok

    def dma(self, q, fn, reads=(), writes=(), is_output=False):
        slot = self.dma_rr[q]
        self.dma_rr[q] = (slot + 1) % N_DMA_SLOTS
        key = f"dma_{q}_{slot}"
        n = self.dma_n.get(key, 0)
        deps = self._deps(reads, writes)
        if n > 0 and deps.get(key, 0) < 16 * n:
            deps[key] = 16 * n
        waits = self._waits(q, deps)
        self.dma_n[key] = n + 1
        tok = (key, 16 * (n + 1))
        self.lists[q].append((fn, waits, (key, 16)))
        for b in reads:
            if b.r.get(key, 0) < tok[1]:
                b.r[key] = tok[1]
        for b in writes:
            b.w = tok
            b.r = {}
        if is_output:
            self.out_tokens.append(tok)
        return tok

    def barrier(self, mode="all"):
        for e in (PE, ACT, DVE, POOL):
            assert not self.pending[e]
        deps = {e: self.sig[e] for e in (PE, ACT, DVE, POOL) if self.sig[e] > 0}
        if mode != "nodma":
            for k, n in self.dma_n.items():
                deps[k] = 16 * n
        engs = (PE, ACT, DVE, POOL, SP)
        if mode == "nopool":
            engs = (PE, ACT, DVE, SP)
        if mode == "nosp":
            engs = (PE, ACT, DVE, POOL)
        for e in engs:
            d = dict(deps)
            waits = []
            for k, v in d.items():
                if self.waited.get((e, k), 0) >= v:
                    continue
                self.waited[(e, k)] = v
                waits.append((k, v))
            if waits:
                self.lists[e].append((None, waits, None))

    def finish(self):
        deps = {}
        for k, v in self.out_tokens:
            if deps.get(k, 0) < v:
                deps[k] = v
        waits = self._waits(SP, deps)
        self.lists[SP].append((None, waits, None))

    def emit(self):
        nc = self.nc
        keys = set()
        for e, lst in self.lists.items():
            for fn, waits, inc in lst:
                for k, v in waits:
                    keys.add(k)
                if inc is not None:
                    keys.add(inc[0])
        for e in (PE, ACT, DVE, POOL):
            assert not self.pending[e], f"{e} ends with unsignalled instruction"
        sems = {k: nc.alloc_semaphore(f"s_{k}") for k in sorted(keys)}
        lists = self.lists

        def run(engobj, lst):
            for fn, waits, inc in lst:
                for k, v in waits:
                    engobj.wait_ge(sems[k], v)
                if fn is None:
                    continue
                ins = fn(engobj)
                if inc is not None:
                    ins.then_inc(sems[inc[0]], inc[1])

        with nc.Block() as block:
            @block.tensor
            def _(e):
                run(e, lists[PE])

            @block.scalar
            def _(e):
                run(e, lists[ACT])

            @block.vector
            def _(e):
                run(e, lists[DVE])

            @block.gpsimd
            def _(e):
                run(e, lists[POOL])

            @block.sync
            def _(e):
                run(e, lists[SP])


class Group:
    def __init__(self, name, nseq, n_own, n_bnd, n_ctx, n_cache, mset, rope):
        self.name = name
        self.nseq = nseq
        self.n_own = n_own
        self.n_bnd = n_bnd
        self.n_ctx = n_ctx
        self.n_cache = n_cache
        self.mset = mset
        self.rope = rope
        self.is_p = (name == "P")
        self.T_own = n_own * 128
        self.nck = n_own


class StopBuild(Exception):
    pass


class Builder:
    def __init__(self, dbg=None, stop=None):
        self.stop = stop
        self.phase_id = 0
        self.nc = nc = bass.Bass("TRN2", target_bir_lowering=False)
        self.S = Sched(nc)
        self.dbg = dbg
        self.bufs = {}
        self.uid = 0
        self.base = ((nc.sbuf_base + 63) // 64) * 64
        self.lim = nc.sbuf_top
        self.ps = [nc.alloc_psum_tensor(f"psb{i}", [128, 512], F32) for i in range(8)]
        self.cur = self.base
        self.rr = 0

    def B(self, *key):
        b = self.bufs.get(key)
        if b is None:
            b = self.bufs[key] = Buf(str(key), excl=(key[0] == "ps"))
        return b

    def Bs(self, name, *ranges):
        out = [()]
        for r in ranges:
            r = [r] if isinstance(r, int) else list(r)
            out = [o + (i,) for o in out for i in r]
        return [self.B(name, *o) for o in out]

    def sb(self, name, shape, dtype, off=None):
        self.uid += 1
        esz = 4 if dtype == F32 else 2
        n = esz
        for s in shape[1:]:
            n *= s
        n = (n + 63) // 64 * 64
        if off is None:
            o = self.cur
            self.cur += n
        else:
            o = self.base + off
        assert o + n <= self.lim, f"SBUF overflow {name} {o + n - self.lim}"
        t = self.nc.alloc_sbuf_tensor_at(f"{name}_{self.uid}", list(shape), dtype, offset=o)
        return t

    def dram_in(self, name, shape):
        return self.nc.dram_tensor(name, list(shape), F32, kind="ExternalInput").ap()

    def dram_out(self, name, shape):
        return self.nc.dram_tensor(name, list(shape), F32, kind="ExternalOutput").ap()

    def op(self, eng, fn, reads=(), writes=(), signal=True):
        return self.S.op(eng, fn, reads, writes, signal)

    def load(self, out_ap, in_ap, writes, q=SP, reads=()):
        return self.S.dma(q, lambda e: e.dma_start(out=out_ap, in_=in_ap, allow_slow_non_contiguous=True), reads=reads, writes=writes)

    def store(self, out_ap, in_ap, reads):
        return self.S.dma(SP, lambda e: e.dma_start(out=out_ap, in_=in_ap, allow_slow_non_contiguous=True), reads=reads, is_output=True)

    def mm(self, out, lhsT, rhs, start, stop, reads, writes, signal=None, skip=False):
        if signal is None:
            signal = stop
        if skip:
            return self.S.op(PE, lambda e: e.matmul(out, lhsT=lhsT, rhs=rhs, start=start, stop=stop,
                                                    skip_group_check=True), reads, writes, signal)
        return self.S.op(PE, lambda e: e.matmul(out, lhsT=lhsT, rhs=rhs, start=start, stop=stop),
                         reads, writes, signal)

    def tr(self, out, in_, ident, reads, writes, signal=True):
        return self.S.op(PE, lambda e: e.transpose(out=out, in_=in_, identity=ident), reads, writes, signal)

    def act(self, out, in_, func, reads, writes, bias=None, scale=None, accum=None, eng=ACT):
        kw = {}
        if bias is not None:
            kw["bias"] = bias
        if scale is not None:
            kw["scale"] = scale
        if accum is not None:
            kw["accum_out"] = accum
        return self.S.op(ACT, lambda e: e.activation(out=out, in_=in_, func=func, **kw), reads, writes)

    def tt(self, out, in0, in1, op, reads, writes, eng=DVE):
        return self.S.op(eng, lambda e: e.tensor_tensor(out=out, in0=in0, in1=in1, op=op), reads, writes)

    def ts(self, out, in0, s1, s2, op0, op1, reads, writes, eng=DVE):
        if s2 is None:
            return self.S.op(eng, lambda e: e.tensor_scalar(out=out, in0=in0, scalar1=s1, scalar2=None, op0=op0),
                             reads, writes)
        return self.S.op(eng, lambda e: e.tensor_scalar(out=out, in0=in0, scalar1=s1, scalar2=s2, op0=op0, op1=op1),
                         reads, writes)

    def stt(self, out, in0, scalar, in1, op0, op1, reads, writes, eng=DVE):
        return self.S.op(eng, lambda e: e.scalar_tensor_tensor(out=out, in0=in0, scalar=scalar, in1=in1,
                                                               op0=op0, op1=op1), reads, writes)

    def cp(self, out, in_, reads, writes, eng=DVE):
        if eng == ACT:
            return self.S.op(ACT, lambda e: e.activation(out=out, in_=in_, func=AF.Copy), reads, writes)
        return self.S.op(eng, lambda e: e.tensor_copy(out=out, in_=in_), reads, writes)

    def red(self, out, in_, op, reads, writes, eng=DVE):
        return self.S.op(eng, lambda e: e.tensor_reduce(out=out, in_=in_, axis=AX.X, op=op), reads, writes)

    def memset(self, ap, val, writes, eng=DVE):
        return self.S.op(eng, lambda e: e.memset(ap, val), (), writes)

    def alt(self):
        self.rr += 1
        return ACT if (self.rr & 1) else DVE

    def evac_cast(self, out, in_, reads, writes, eng=None):
        eng = eng or self.alt()
        return self.cp(out, in_, reads, writes, eng=eng)

    def build(self):
        nc = self.nc
        I = {}
        I["xp"] = self.dram_in("xp", [512, 1024])
        I["xs"] = self.dram_in("xs", [4096, 1024])
        I["rope"] = self.dram_in("rope", [4096, 128])
        I["ck"] = self.dram_in("ck", [256, 256])
        I["cv"] = self.dram_in("cv", [256, 256])
        I["sC"] = self.dram_in("sC", [8, 256, 256])
        I["sn"] = self.dram_in("sn", [8, 256])
        I["sm"] = self.dram_in("sm", [8])
        I["cT"] = self.dram_in("cT", [128, 16])
        I["blk"] = self.dram_in("blk", [128, 2, 24])
        I["cm"] = self.dram_in("cm", [128, 6, 128])
        I["wmod"] = self.dram_in("wmod", [12, 128, 8 * 512])
        I["bmod"] = self.dram_in("bmod", [6144])
        I["wg"] = self.dram_in("wg", [128, 8 * 16])
        I["bg"] = self.dram_in("bg", [16])
        I["wml"] = self.dram_in("wml", [4, 4, 128, 8 * 256])
        I["wqa"] = self.dram_in("wqa", [2, 128, 8 * 512])
        I["wkva"] = self.dram_in("wkva", [128, 8 * 512])
        I["w5"] = self.dram_in("w5", [8, 128, 4 * 8 * 128])
        I["wout"] = self.dram_in("wout", [2, 128, 8 * 512])
        I["wup"] = self.dram_in("wup", [16, 128, 8 * 256])
        I["wdn"] = self.dram_in("wdn", [4, 128, 8 * 1024])
        I["mng"] = self.dram_in("mng", [1024])
        I["qg"] = self.dram_in("qg", [128])
        I["kg"] = self.dram_in("kg", [128])
        for nme in ("ln1g", "ln1b", "ln2g", "ln2b"):
            I[nme] = self.dram_in(nme, [1024])
        O = {}
        O["yp"] = self.dram_out("yp", [512, 1024])
        O["ys"] = self.dram_out("ys", [1024, 1024])
        O["kc"] = self.dram_out("kc", [512, 256])
        O["vc"] = self.dram_out("vc", [512, 256])
        O["Cn"] = self.dram_out("Cn", [2, 8, 256, 256])
        O["nn"] = self.dram_out("nn", [2, 8, 256])
        O["mn"] = self.dram_out("mn", [2, 8])
        if self.dbg:
            O["dbg"] = self.dram_out("dbg", list(self.dbg))
        self.I, self.O = I, O

        self.cm = self.sb("cm", [128, 6, 128], F32)
        self.identb = self.sb("identb", [128, 128], BF16)
        self.onesb = self.sb("onesb", [128, 2], BF16)
        self.onesb128 = self.sb("onesb128", [128, 128], BF16)
        self.COL = self.sb("COL", [128, 2, 4, 8], F32)
        self.G1 = self.sb("G1", [128, 2, 1024], F32)
        self.G2 = self.sb("G2", [128, 2, 1024], F32)
        self.QG = self.sb("QG", [128, 128], F32)
        self.KG = self.sb("KG", [128, 128], F32)
        self.BG = self.sb("BG", [128, 16], F32)
        self.wg = self.sb("wg", [128, 8, 16], BF16)
        self.arena0 = self.cur - self.base
        bc = self.B("consts")
        self.load(self.cm[:], I["cm"], [bc])
        self.load(self.QG[:], I["qg"].partition_broadcast(128), [bc])
        self.load(self.KG[:], I["kg"].partition_broadcast(128), [bc])
        self.load(self.BG[:], I["bg"].partition_broadcast(128), [bc])
        self.load(self.wg[:], I["wg"].rearrange("p (c n) -> p c n", c=8), [bc], q=POOL)
        self.cp(self.identb[:], self.cm[:, 0, :], [bc], [self.B("identb")])
        self.memset(self.onesb[:], 1.0, [self.B("onesb")])
        self.memset(self.onesb128[:], 1.0, [self.B("onesb128")])
        self.ident = self.cm[:, 0, :]

        try:
            self.build_body()
        except StopBuild:
            pass
        self.S.finish()
        self.S.emit()
        return nc

    def chk(self, label):
        self.phase_id += 1
        if self.stop is not None and self.phase_id >= self.stop:
            print("STOP at", self.phase_id, label)
            raise StopBuild()

    def build_body(self):
        self.phase0()
        import os
        if os.environ.get("DBG_BAR"):
            self.S.barrier(os.environ.get("DBG_BAR"))
        self.chk("phase0")
        gP = Group("P", nseq=2, n_own=2, n_bnd=2, n_ctx=2, n_cache=0, mset=0, rope=False)
        gS = Group("S", nseq=1, n_own=8, n_bnd=24, n_ctx=32, n_cache=2, mset=1, rope=True)
        for g in (gP, gS):
            self.S.barrier()
            self.run_group(g)

    def phase0(self):
        I = self.I
        self.cur = self.base + self.arena0
        MOD = self.sb("MOD", [128, 2, 6144], F32)
        bmod = self.sb("bmodr", [128, 6144], F32)
        cT = self.sb("cT", [128, 8, 2], F32)
        sg = self.sb("sg", [128, 8, 2], F32)
        srep = self.sb("srep", [128, 8, 2, 128], BF16)
        wsl = [self.sb(f"wmods{i}", [128, 8, 512], BF16) for i in range(4)]
        wstage = [self.sb(f"wstage{i}", [128, 8, 512], F32) for i in range(2)]
        l1g = self.sb("l1g", [128, 1024], F32)
        l1b = self.sb("l1b", [128, 1024], F32)
        tmp = self.sb("tmp0", [128, 1024], F32)
        tmp2 = self.sb("tmp02", [128, 8, 128], F32)
        b0 = self.B("p0")
        self.load(bmod[:], I["bmod"].partition_broadcast(128), [self.B("bmod")])
        self.load(cT[:], I["cT"].rearrange("p (c s) -> p c s", s=2), [self.B("cT")])
        self.load(l1g[:], I["ln1g"].partition_broadcast(128), [self.B("l1g")])
        self.load(l1b[:], I["ln1b"].partition_broadcast(128), [self.B("l1b")])
        self.act(sg[:], cT[:], AF.Sigmoid, [self.B("cT")], [self.B("sg")])
        self.tt(sg[:], sg[:], cT[:], ALU.mult, [self.B("cT"), self.B("sg")], [self.B("sg")])
        self.cp(srep[:].rearrange("p c s n -> p (c s) n"),
                sg[:].rearrange("p c s -> p (c s)").unsqueeze(2).to_broadcast([128, 16, 128]),
                [self.B("sg")], [self.B("srep")])
        for j in range(12):
            w = wsl[j % 4]
            bw = self.B("wmods", j % 4)
            if j % 2 == 0:
                self.load(w[:], I["wmod"][j].rearrange("p (c n) -> p c n", c=8), [bw], q=POOL)
            else:
                stg = wstage[(j // 2) % 2]
                bst = self.B("wstage", (j // 2) % 2)
                self.load(stg[:], I["wmod"][j].rearrange("p (c n) -> p c n", c=8), [bst], q=SP)
                self.cp(w[:], stg[:], [bst], [bw])
            for s in range(2):
                pb = self.B("ps", (2 * j + s) % 4)
                pt = self.ps[(2 * j + s) % 4]
                for c in range(8):
                    self.mm(pt[:], srep[:, c, s, :], w[:, c, :], c == 0, c == 7, [bw, self.B("srep")], [pb])
                self.tt(MOD[:, s, j * 512:(j + 1) * 512], pt[:], bmod[:, j * 512:(j + 1) * 512], ALU.add,
                        [pb, self.B("bmod")], [self.B("MOD", s, j)])
        for s in range(2):
            def row(k):
                return MOD[:, s, k * 1024:(k + 1) * 1024]

            def rb(k):
                return [self.B("MOD", s, 2 * k), self.B("MOD", s, 2 * k + 1)]
            self.cp(self.G1[:, s, :], row(2), rb(2), [self.B("G1", s)], eng=ACT)
            self.cp(self.G2[:, s, :], row(5), rb(5), [self.B("G2", s)], eng=ACT)
            bt, bt2 = self.B("p0tmp"), self.B("p0tmp2")
            identB = self.cm[:, 0, :].unsqueeze(1).to_broadcast([128, 8, 128])

            def diag(dst, src_ap, reads):
                self.tt(tmp2[:], src_ap.rearrange("p (c n) -> p c n", c=8), identB, ALU.mult,
                        reads + [self.B("consts")], [bt2])
                self.red(dst, tmp2[:], ALU.add, [bt2], [self.B("COL", s)])
            self.ts(tmp[:], row(1), 1.0, None, ALU.add, None, rb(1), [bt])
            diag(self.COL[:, s, 0, :], tmp[:], [bt])
            diag(self.COL[:, s, 1, :], row(0), rb(0))
            self.ts(tmp[:], row(4), 1.0, None, ALU.add, None, rb(4), [bt])
            self.tt(row(4), tmp[:], l1g[:], ALU.mult, [bt, self.B("l1g")], rb(4))
            diag(self.COL[:, s, 2, :], row(4), rb(4))
            self.tt(tmp[:], tmp[:], l1b[:], ALU.mult, [bt, self.B("l1b")], [bt])
            self.tt(tmp[:], tmp[:], row(3), ALU.add, [bt] + rb(3), [bt])
            diag(self.COL[:, s, 3, :], tmp[:], [bt])

    def run_group(self, g):
        I, O = self.I, self.O
        KB = 1024
        a0 = self.arena0
        nseq, n_own, nck, T_own = g.nseq, g.n_own, g.nck, g.T_own
        TT = nseq * T_own
        T_ctx = g.n_ctx * 128
        T_keys = (g.n_ctx + g.n_cache) * 128
        nkt = g.n_ctx + g.n_cache
        s = g.mset
        hT = self.sb("hT", [128, 8, TT], BF16, off=a0)
        hmT = self.sb("hmT", [128, 8, TT], BF16, off=a0 + 16 * KB)
        haT = self.sb("haT", [128, 8, TT], BF16, off=a0 + 32 * KB)
        self.cur = self.base + a0 + 48 * KB
        KaT = self.sb("KaT", [128, nseq, 2, T_keys], BF16)
        Va = self.sb("Va", [128, nseq, nkt, 2, 129], BF16)
        G128 = self.sb("G128", [128, nseq, g.n_bnd, 16], F32)
        G64 = self.sb("G64", [128, nseq, nck, 16], F32)
        W8 = self.sb("W8", [128, nseq, g.n_bnd, 8], F32)
        U64 = self.sb("U64", [128, nseq, nck, 8], F32)
        E64 = self.sb("E64", [128, nseq, nck, 8], F32)
        Z64 = self.sb("Z64", [128, nseq, nck, 8], F32)
        DEC = self.sb("DEC", [128, nseq, nck, 8], F32)
        SCI = self.sb("SCI", [128, 8], F32)
        C0 = self.sb("C0", [128, nseq, 8, 2, 257], F32)
        mark_common = self.cur

        def own_cols(q, t):
            o = q * T_own + t * 128
            return slice(o, o + 128)

        hTo = self.sb("hTo", [128, 8, nseq * (g.n_ctx - n_own) * 128 if not g.is_p else 16], BF16) if not g.is_p else None
        mark2 = self.cur
        xin = [self.sb(f"xin{i}", [128, 1024], F32) for i in range(2)]
        xsrc = I["xp"] if g.is_p else I["xs"]
        sc1 = self.COL[:, s, 0, :]
        sh1 = self.COL[:, s, 1, :]
        bcol = self.B("COL", s)
        cnt = 0
        gcnt = 0
        PG = self.ps[7]
        hview = {}
        import os
        DBG_NT = int(os.environ.get("DBG_NT", "999"))
        DBG_NOG = int(os.environ.get("DBG_NOG", "0"))
        for q in range(nseq):
            for t in range(min(g.n_ctx, DBG_NT)):
                xi = xin[cnt % 2]
                bx = self.B("xin", cnt % 2)
                row0 = (q * g.n_ctx + t) * 128
                self.load(xi[:], xsrc[row0:row0 + 128, :], [bx])
                pbank = [self.ps[(cnt % 2) * 2], self.ps[(cnt % 2) * 2 + 1]]
                pb = [self.B("ps", (cnt % 2) * 2), self.B("ps", (cnt % 2) * 2 + 1)]
                if t < n_own:
                    dst, off = hT, q * T_own + t * 128
                else:
                    dst, off = hTo, (q * (g.n_ctx - n_own) + (t - n_own)) * 128
                hview[(q, t)] = (dst, off)
                for c in range(8):
                    self.tr(pbank[c // 4][:, (c % 4) * 128:(c % 4 + 1) * 128], xi[:, c * 128:(c + 1) * 128],
                            self.ident, [bx, self.B("consts")], [pb[c // 4]], signal=(c % 4 == 3))
                for c in range(8):
                    src = pbank[c // 4][:, (c % 4) * 128:(c % 4 + 1) * 128]
                    o_ap = dst[:, c, off:off + 128]
                    bh = self.B("hT", g.name, q, t, c)
                    if c // 4 == 0 or os.environ.get("DBG_ACTONLY"):
                        self.S.op(ACT, lambda e, o_ap=o_ap, src=src, c=c: e.activation(
                            out=o_ap, in_=src, func=AF.Identity, scale=sc1[:, c:c + 1], bias=sh1[:, c:c + 1]),
                            [pb[c // 4], bcol], [bh])
                    else:
                        self.stt(o_ap, src, sc1[:, c:c + 1], sh1[:, c:c + 1].to_broadcast([128, 128]), ALU.mult, ALU.add,
                                 [pb[c // 4], bcol], [bh])
                cnt += 1
                bidx = t if g.is_p else (t - n_own)
                if DBG_NOG:
                    continue
                if 0 <= bidx < g.n_bnd:
                    sl = 0
                    PG = self.ps[4 + gcnt % 4]
                    pgb = self.B("ps", 4 + gcnt % 4)
                    gcnt += 1
                    for c in range(8):
                        self.mm(PG[:, sl * 16:(sl + 1) * 16], dst[:, c, off:off + 128], self.wg[:, c, :],
                                c == 0, c == 7, [self.B("hT", g.name, q, t, c), self.B("consts")], [pgb])
                    self.tt(G128[:, q, bidx, :], PG[:, sl * 16:(sl + 1) * 16], self.BG[:], ALU.add,
                            [pgb, self.B("consts")], [self.B("G128", q)])
                if t < n_own:
                    sl = 0
                    PG = self.ps[4 + gcnt % 4]
                    pgb = self.B("ps", 4 + gcnt % 4)
                    gcnt += 1
                    for c in range(8):
                        self.mm(PG[:, 0:16], dst[:, c, off:off + 128], self.wg[:, c, :], c == 0, c == 7,
                                [self.B("hT", g.name, q, t, c), self.B("consts")], [pgb])
                    self.tt(G64[:, q, t, :], PG[:, 0:16], self.BG[:], ALU.add,
                            [pgb, self.B("consts")], [self.B("G64", q)])

        def hT_bufs(q, t):
            return [self.B("hT", g.name, q, t, c) for c in range(8)]

        self.chk(g.name + " phase1")
        reuse = not g.is_p
        if reuse:
            self.S.barrier()
            self.cur = mark2
        for q in range(nseq):
            self.gates(g, q, G128, G64, W8, U64, E64, Z64, DEC, SCI)
        if reuse:
            self.S.barrier()
            self.cur = mark2

        self.chk(g.name + " gates")
        self.kv_proj(g, hview, hT_bufs, KaT, Va)
        if reuse:
            self.S.barrier()
            self.cur = mark2

        self.chk(g.name + " kv")
        self.boundary(g, hview, hT_bufs, W8, SCI, C0)
        self.S.barrier()
        self.cur = mark_common

        self.chk(g.name + " boundary")
        self.mlstm_own(g, hT, hT_bufs, U64, E64, Z64, DEC, C0, hmT)
        self.S.barrier()
        self.cur = mark_common

        self.chk(g.name + " mlstm")
        self.attention(g, hT, hT_bufs, KaT, Va, haT)

        self.chk(g.name + " attention")
        self.S.barrier()
        self.cur = self.base + a0 + 48 * KB
        self.post(g, hT, hmT, haT)
        self.chk(g.name + " post")

    def gates(self, g, q, G128, G64, W8, U64, E64, Z64, DEC, SCI):
        I, O = self.I, self.O
        n = g.n_bnd
        nck = g.nck
        cmB = self.B("consts")
        Lst, Ust, ONES = self.cm[:, 3, :], self.cm[:, 4, :], self.cm[:, 5, :]
        Uf, Ub = self.cm[:, 1, :], self.cm[:, 2, :]
        NLF = self.sb("NLF", [128, n, 8], F32)
        TTs = self.sb("TTs", [128, n, 8], F32)
        OFF = self.sb("OFF", [128, n, 8], F32)
        X = self.sb("X", [128, n, 8], F32)
        bG = self.B("G128", q)
        bN, bT, bO, bX = self.B("NLF", q), self.B("TTs", q), self.B("OFF", q), self.B("X", q)
        self.act(NLF[:], G128[:, q, :, 8:16], AF.Exp, [bG], [bN], scale=-1.0)
        self.act(NLF[:], NLF[:], AF.Ln, [bN], [bN], bias=1.0)
        pt = self.ps[4]
        pb = self.B("ps", 4)
        ecf = pt[:, 0:n * 4].rearrange("p (t h) -> p t h", h=4)
        ecb = pt[:, n * 4:n * 8].rearrange("p (t h) -> p t h", h=4)
        ttv = pt[:, n * 8:n * 16].rearrange("p (t h) -> p t h", h=8)
        self.mm(ecf, Ust, NLF[:, :, 0:4], True, True, [bN, cmB], [pb])
        self.mm(ecb, Lst, NLF[:, :, 4:8], True, True, [bN, cmB], [pb])
        self.mm(ttv, ONES, NLF[:], True, True, [bN, cmB], [pb])
        self.cp(TTs[:], ttv, [pb], [bT])
        self.memset(OFF[:], 0.0, [bO])
        for t in range(n - 2, -1, -1):
            self.tt(OFF[:, t, 0:4], OFF[:, t + 1, 0:4], TTs[:, t + 1, 0:4], ALU.add, [bO, bT], [bO])
        for t in range(1, n):
            self.tt(OFF[:, t, 4:8], OFF[:, t - 1, 4:8], TTs[:, t - 1, 4:8], ALU.add, [bO, bT], [bO])
        self.tt(X[:, :, 0:4], G128[:, q, :, 0:4], ecf, ALU.subtract, [bG, pb], [bX])
        self.tt(X[:, :, 4:8], G128[:, q, :, 4:8], ecb, ALU.subtract, [bG, pb], [bX])
        self.tt(X[:], X[:], OFF[:], ALU.subtract, [bX, bO], [bX])
        if not g.is_p:
            blk = self.sb("blk", [128, 2, 24], F32)
            bB = self.B("blk")
            self.load(blk[:], I["blk"], [bB])
            bW = self.B("W8", q)
            self.act(W8[:, q, :, :], X[:], AF.Exp, [bX], [bW], bias=-LN16)
            self.tt(W8[:, q, :, 0:4], W8[:, q, :, 0:4], blk[:, 0, :].unsqueeze(2).to_broadcast([128, n, 4]), ALU.mult,
                    [bW, bB], [bW])
            self.tt(W8[:, q, :, 4:8], W8[:, q, :, 4:8], blk[:, 1, :].unsqueeze(2).to_broadcast([128, n, 4]), ALU.mult,
                    [bW, bB], [bW])
            tmpd = self.sb("tmpd", [128, 8, n], F32)
            sm = self.sb("smr", [128, 8], F32)
            bD = self.B("tmpd")
            self.load(sm[:], I["sm"].partition_broadcast(128), [self.B("smr")])
            self.tt(tmpd[:, 0:4, :], TTs[:, :, 0:4].rearrange("p t h -> p h t"),
                    blk[:, 0, :].unsqueeze(1).to_broadcast([128, 4, n]), ALU.mult, [bT, bB], [bD])
            self.tt(tmpd[:, 4:8, :], TTs[:, :, 4:8].rearrange("p t h -> p h t"),
                    blk[:, 1, :].unsqueeze(1).to_broadcast([128, 4, n]), ALU.mult, [bT, bB], [bD])
            self.red(SCI[:], tmpd[:], ALU.add, [bD], [self.B("SCI")])
            self.tt(SCI[:], sm[:], SCI[:], ALU.subtract, [self.B("SCI"), self.B("smr")], [self.B("SCI")])
            self.act(SCI[:], SCI[:], AF.Exp, [self.B("SCI")], [self.B("SCI")])
        else:
            assert n == 2
            pt2 = self.ps[5]
            pb2 = self.B("ps", 5)
            mx = self.sb("mx", [16, 1], F32)
            mxb = self.sb("mxb", [16, 128], F32)
            MF = self.sb("MF", [128, 8], F32)
            GT = self.sb("GT", [128, 8], F32)
            bM = self.B("mfin", q)
            self.tr(pt2[0:16, 0:128], X[:].rearrange("p t h -> p (t h)"), self.ident, [bX, cmB], [pb2])
            self.red(mx[:], pt2[0:16, 0:128], ALU.max, [pb2], [bM])
            self.cp(mxb[:], mx[:].to_broadcast([16, 128]), [bM], [bM])
            self.tr(pt2[:, 128:144], mxb[:], self.cm[0:16, 0, 0:16], [bM, cmB], [pb2])
            MX = self.sb("MX", [128, 16], F32)
            self.cp(MX[:], pt2[:, 128:144], [pb2], [bM])
            self.tt(MF[:], MX[:, 0:8], MX[:, 8:16], ALU.max, [bM], [bM])
            self.tt(GT[:], TTs[:, 0, :], TTs[:, 1, :], ALU.add, [bT], [bM])
            self.stt(MF[:], GT[:], -1.0, MF[:], ALU.mult, ALU.max, [bM], [bM])
            self.tt(X[:], X[:], MF[:].unsqueeze(1).to_broadcast([128, 2, 8]), ALU.subtract, [bX, bM], [bX])
            self.act(W8[:, q, :, :], X[:], AF.Exp, [bX], [self.B("W8", q)], bias=-LN16)
            self.store(O["mn"][q:q + 1, :], MF[0:1, :], [bM])
        N64 = self.sb("N64", [128, nck, 8], F32)
        A64 = self.sb("A64", [128, nck, 8], F32)
        bG6 = self.B("G64", q)
        bN6, bA6 = self.B("N64", q), self.B("A64", q)
        self.act(N64[:], G64[:, q, :, 8:16], AF.Exp, [bG6], [bN6], scale=-1.0)
        self.act(N64[:], N64[:], AF.Ln, [bN6], [bN6], bias=1.0)
        pt3 = self.ps[6]
        pb3 = self.B("ps", 6)
        bcf = pt3[:, 0:nck * 4].rearrange("p (t h) -> p t h", h=4)
        bcb = pt3[:, nck * 4:nck * 8].rearrange("p (t h) -> p t h", h=4)
        t64 = pt3[:, nck * 8:nck * 16].rearrange("p (t h) -> p t h", h=8)
        t128 = pt3[:, nck * 16:nck * 24].rearrange("p (t h) -> p t h", h=8)
        self.mm(bcf, Uf, N64[:, :, 0:4], True, True, [bN6, cmB], [pb3])
        self.mm(bcb, Ub, N64[:, :, 4:8], True, True, [bN6, cmB], [pb3])
        self.mm(t64, ONES, N64[:], True, True, [bN6, cmB], [pb3])
        self.mm(t128, ONES, N64[:], True, True, [bN6, cmB], [pb3])
        self.tt(A64[:, :, 0:4], G64[:, q, :, 0:4], bcf, ALU.add, [bG6, pb3], [bA6])
        self.tt(A64[:, :, 4:8], G64[:, q, :, 4:8], bcb, ALU.add, [bG6, pb3], [bA6])
        self.act(U64[:, q, :, :], A64[:], AF.Exp, [bA6], [self.B("U64", q)], bias=-LN16)
        self.tt(A64[:], A64[:], t64, ALU.subtract, [bA6, pb3], [bA6])
        self.act(Z64[:, q, :, :], A64[:], AF.Exp, [bA6], [self.B("Z64", q)], bias=-LN16)
        self.act(E64[:, q, :, 0:4], bcf, AF.Exp, [pb3], [self.B("E64", q)], scale=-1.0)
        self.act(E64[:, q, :, 4:8], bcb, AF.Exp, [pb3], [self.B("E64", q)], scale=-1.0)
        self.act(DEC[:, q, :, :], t128, AF.Exp, [pb3], [self.B("DEC", q)], scale=-1.0)

    def interleave(self, gens, width, stagger=0):
        active = []
        it = iter(gens)
        since = stagger
        done = False
        while True:
            if not done and len(active) < width and since >= stagger:
                try:
                    active.append(next(it))
                    since = 0
                except StopIteration:
                    done = True
            if not active:
                if done:
                    break
                since = stagger
                continue
            since += 1
            for gq in list(active):
                try:
                    next(gq)
                except StopIteration:
                    active.remove(gq)

    def rope_apply(self, xt, H, rt, bR, bx, tmp1, tmp2, btmp):
        cosv = rt[:, 0:64].rearrange("p (a f) -> p a f", a=2).unsqueeze(1).to_broadcast([128, H, 2, 32])
        sinv = rt[:, 64:128].rearrange("p (a f) -> p a f", a=2).unsqueeze(1).to_broadcast([128, H, 2, 32])
        xv = xt.rearrange("p h (a k f) -> p h a k f", a=2, k=2)
        t1 = tmp1.rearrange("p h (a k f) -> p h a k f", a=2, k=2)
        t2 = tmp2.rearrange("p h (a k f) -> p h a k f", a=2, k=2)
        for k in range(2):
            self.tt(t1[:, :, :, k, :], xv[:, :, :, k, :], cosv, ALU.mult, [bx, bR], [btmp])
            yield
            self.tt(t2[:, :, :, k, :], xv[:, :, :, 1 - k, :], sinv, ALU.mult, [bx, bR], [btmp])
            yield
        self.tt(xv[:, :, :, 0, :], t1[:, :, :, 0, :], t2[:, :, :, 0, :], ALU.subtract, [btmp], [bx])
        yield
        self.tt(xv[:, :, :, 1, :], t1[:, :, :, 1, :], t2[:, :, :, 1, :], ALU.add, [btmp], [bx])
        yield

    def rms_heads(self, src_ps, H, gain, dst, pbs, bdst, scr, ss, bss):
        for h in range(H):
            self.S.op(ACT, lambda e, h=h: e.activation(out=scr[:, h, :], in_=src_ps[:, h, :], func=AF.Square,
                                                       accum_out=ss[:, h:h + 1]), pbs, [bss])
        yield
        self.act(ss[:, 0:H], ss[:, 0:H], AF.Sqrt, [bss], [bss], bias=EPS, scale=1.0 / 128.0)
        yield
        self.S.op(DVE, lambda e: e.reciprocal(out=ss[:, 0:H], in_=ss[:, 0:H]), [bss], [bss])
        yield
        self.tt(dst, src_ps, ss[:, 0:H].unsqueeze(2).to_broadcast([128, H, 128]), ALU.mult, pbs + [bss], [bdst])
        yield
        self.tt(dst, dst, gain.unsqueeze(1).to_broadcast([128, H, 128]), ALU.mult, [bdst, self.B("consts")], [bdst])
        yield

    def kv_proj(self, g, hview, hT_bufs, KaT, Va):
        I, O = self.I, self.O
        w = self.sb("wkva", [128, 8, 512], BF16)
        bw = self.B("wkva")
        self.load(w[:], I["wkva"].rearrange("p (c n) -> p c n", c=8), [bw], q=POOL)
        NS = 3
        kn = [self.sb(f"kn{i}", [128, 2, 128], F32) for i in range(NS)]
        vf = [self.sb(f"vf{i}", [128, 256], F32) for i in range(NS)]
        kb = [self.sb(f"kb{i}", [128, 2, 128], BF16) for i in range(NS)]
        scr = [self.sb(f"kscr{i}", [128, 2, 128], F32) for i in range(NS)]
        ss = [self.sb(f"kss{i}", [128, 2], F32) for i in range(NS)]
        t1 = [self.sb(f"rt1{i}", [128, 2, 128], F32) for i in range(NS)]
        t2 = [self.sb(f"rt2{i}", [128, 2, 128], F32) for i in range(NS)]
        rts = [self.sb(f"rts{i}", [128, 128], F32) for i in range(NS)]
        self.memset(Va[:, :, :, :, 128:129], 1.0, [self.B("Vaones")])

        def tile_job(q, t, i2):
            dst, off = hview[(q, t)]
            pt = self.ps[i2]
            pb = self.B("ps", i2)
            if g.rope:
                self.load(rts[i2][:], I["rope"][t * 128:(t + 1) * 128, :], [self.B("rts", i2)])
            for c in range(8):
                self.mm(pt[:], dst[:, c, off:off + 128], w[:, c, :], c == 0, c == 7, hT_bufs(q, t) + [bw], [pb])
            yield
            kps = pt[:, 0:256].rearrange("p (h d) -> p h d", h=2)
            bk, bv, bs = self.B("kn", i2), self.B("vf", i2), self.B("kss", i2)
            yield from self.rms_heads(kps, 2, self.KG[:], kn[i2][:], [pb], bk, scr[i2], ss[i2], bs)
            if g.is_p:
                self.cp(vf[i2][:], pt[:, 256:512], [pb], [bv], eng=ACT)
                r0 = q * 256 + t * 128
                self.store(O["kc"][r0:r0 + 128, :], kn[i2][:].rearrange("p h d -> p (h d)"), [bk])
                self.store(O["vc"][r0:r0 + 128, :], vf[i2][:], [bv])
            self.cp(Va[:, q, t, :, 0:128], pt[:, 256:512].rearrange("p (h d) -> p h d", h=2),
                    [pb], [self.B("Va", q, t), self.B("Vaones")], eng=ACT)
            yield
            if g.rope:
                yield from self.rope_apply(kn[i2][:], 2, rts[i2], self.B("rts", i2), bk, t1[i2][:], t2[i2][:],
                                           self.B("rtmp", i2))
            bkb = self.B("kb", i2)
            self.cp(kb[i2][:], kn[i2][:], [bk], [bkb])
            yield
            ptr = self.ps[3 + i2]
            pbt = self.B("ps", 3 + i2)
            ptv = ptr[:].bitcast(BF16)
            for h in range(2):
                self.tr(ptv[:, h * 128:(h + 1) * 128], kb[i2][:, h, :], self.identb[:],
                        [bkb, self.B("identb")], [pbt], signal=(h == 1))
            yield
            self.cp(KaT[:, q, :, t * 128:(t + 1) * 128], ptv[:, 0:256].rearrange("p (h n) -> p h n", h=2),
                    [pbt], [self.B("KaT", q, t)])
            yield

        def jobs():
            cnt = 0
            for q in range(g.nseq):
                for t in range(g.n_ctx):
                    yield tile_job(q, t, cnt % NS)
                    cnt += 1
        self.interleave(jobs(), NS, stagger=5)
        for q in range(g.nseq):
            if g.n_cache:
                ckf = self.sb("ckf", [128, 2, 256], F32)
                cvf = self.sb("cvf", [128, 2, 256], F32)
                ckb = self.sb("ckb", [128, 2, 256], BF16)
                self.load(ckf[:], I["ck"].rearrange("(t p) n -> p t n", p=128), [self.B("ckf")])
                self.load(cvf[:], I["cv"].rearrange("(t p) n -> p t n", p=128), [self.B("cvf")])
                self.cp(ckb[:], ckf[:], [self.B("ckf")], [self.B("ckb")])
                for tc in range(g.n_cache):
                    t = g.n_ctx + tc
                    self.cp(Va[:, q, t, :, 0:128], cvf[:, tc, :].rearrange("p (h d) -> p h d", h=2),
                            [self.B("cvf")], [self.B("Va", q, t), self.B("Vaones")])
                    ptr = self.ps[6 + tc % 2]
                    pbt = self.B("ps", 6 + tc % 2)
                    ptv = ptr[:].bitcast(BF16)
                    for h in range(2):
                        self.tr(ptv[:, h * 128:(h + 1) * 128], ckb[:, tc, h * 128:(h + 1) * 128], self.identb[:],
                                [self.B("ckb"), self.B("identb")], [pbt], signal=(h == 1))
                    self.cp(KaT[:, q, :, t * 128:(t + 1) * 128], ptv[:, 0:256].rearrange("p (h n) -> p h n", h=2),
                            [pbt], [self.B("KaT", q, t)])

    def boundary(self, g, hview, hT_bufs, W8, SCI, C0):
        I, O = self.I, self.O
        n = g.n_bnd
        n_own = g.n_own
        wk = [self.sb(f"bwk{i}", [128, 8, 256], BF16) for i in range(2)]
        wv = [self.sb(f"bwv{i}", [128, 8, 256], BF16) for i in range(2)]
        vb = [self.sb(f"bvb{i}", [128, 256], BF16) for i in range(3)]
        kp = [[self.sb(f"bkp{d}{i}", [128, 256], BF16) for i in range(3)] for d in range(2)]
        nsb = self.sb("bnsb", [128, 2, 2], F32)
        cinit = [self.sb(f"cinit{i}", [128, 2, 256], F32) for i in range(2)]
        ninit = [self.sb(f"ninit{i}", [128, 2], F32) for i in range(2)]
        cout = [self.sb(f"cout{i}", [128, 2, 256], F32) for i in range(2)]
        NS = 3
        ptb = [0, 1, 5]
        tiles = []
        for h in range(4):
            for q in range(g.nseq):
                for bi in range(n):
                    tiles.append((h, q, bi))
        state = {"ci": 0}

        def front(k):
            h, q, bi = tiles[k]
            i2 = h % 2
            bwk, bwv = self.B("bwk", i2), self.B("bwv", i2)
            if q == 0 and bi == 0:
                self.load(wk[i2][:], I["wml"][h, 1].rearrange("p (c n) -> p c n", c=8), [bwk], q=POOL)
                self.load(wv[i2][:], I["wml"][h, 2].rearrange("p (c n) -> p c n", c=8), [bwv], q=POOL)
            t = bi if g.is_p else n_own + bi
            dst, off = hview[(q, t)]
            j2 = k % NS
            pt = self.ps[ptb[j2]]
            pb = self.B("ps", ptb[j2])
            for c in range(8):
                self.mm(pt[:, 0:256], dst[:, c, off:off + 128], wk[i2][:, c, :], c == 0, c == 7,
                        hT_bufs(q, t) + [bwk], [pb], signal=False)
            for c in range(8):
                self.mm(pt[:, 256:512], dst[:, c, off:off + 128], wv[i2][:, c, :], c == 0, c == 7,
                        hT_bufs(q, t) + [bwv], [pb])
            self.cp(vb[j2][:], pt[:, 256:512], [pb], [self.B("bvb", j2)], eng=ACT)
            for d in range(2):
                self.ts(kp[d][j2][:], pt[:, 0:256], W8[:, q, bi, d * 4 + h:d * 4 + h + 1], None, ALU.mult, None,
                        [pb, self.B("W8", q)], [self.B("bkp", d, j2)])

        def back(k):
            h, q, bi = tiles[k]
            j2 = k % NS
            par = (h * g.nseq + q) % 2
            cacc = [self.ps[2 + par * 4], self.ps[3 + par * 4]]
            pbc = [self.B("ps", 2 + par * 4), self.B("ps", 3 + par * 4)]
            nacc = self.ps[4]
            pbn = self.B("ps", 4)
            no = par * 4
            bvb = self.B("bvb", j2)
            for d in range(2):
                bkp = self.B("bkp", d, j2)
                for dc in range(2):
                    self.mm(cacc[d][:, dc * 256:(dc + 1) * 256], kp[d][j2][:, dc * 128:(dc + 1) * 128],
                            vb[j2][:], bi == 0 and dc == 0, bi == n - 1, [bkp, bvb], [pbc[d]], signal=False,
                            skip=True)
                    self.mm(nacc[:, no + d * 2 + dc:no + d * 2 + dc + 1], kp[d][j2][:, dc * 128:(dc + 1) * 128],
                            self.onesb[:, 0:1], bi == 0 and dc == 0 and d == 0, bi == n - 1,
                            [bkp, self.B("onesb")], [pbn], signal=(d == 1 and dc == 1), skip=True)
            if bi < n - 1:
                return
            for d in range(2):
                hd = d * 4 + h
                bC0 = self.B("C0", q, hd)
                cv = cacc[d][:].rearrange("p (c e) -> p c e", c=2)
                ci = state["ci"]
                state["ci"] += 1
                if g.is_p:
                    co = cout[ci % 2]
                    bco = self.B("cout", ci % 2)
                    self.cp(co[:], cv, [pbc[d]], [bco], eng=ACT)
                    self.store(O["Cn"][q, hd].rearrange("(c p) e -> p c e", p=128), co[:], [bco])
                    bns = self.B("bnsb", ci % 2)
                    self.cp(nsb[:, ci % 2, 0:2], nacc[:, no + d * 2:no + d * 2 + 2], [pbn], [bns])
                    self.store(O["nn"][q, hd].rearrange("(c p) -> p c", p=128), nsb[:, ci % 2, 0:2], [bns])
                    self.memset(C0[:, q, hd, :, :], 0.0, [bC0])
                else:
                    cn = cinit[ci % 2]
                    nn_ = ninit[ci % 2]
                    bci = self.B("cinit", ci % 2)
                    self.load(cn[:], I["sC"][hd].rearrange("(c p) e -> p c e", p=128), [bci])
                    self.load(nn_[:], I["sn"][hd].rearrange("(c p) -> p c", p=128), [bci])
                    self.stt(C0[:, q, hd, :, 0:256], cn[:], SCI[:, hd:hd + 1], cv, ALU.mult, ALU.add,
                             [bci, self.B("SCI"), pbc[d]], [bC0])
                    self.stt(C0[:, q, hd, :, 256], nn_[:], SCI[:, hd:hd + 1], nacc[:, no + d * 2:no + d * 2 + 2],
                             ALU.mult, ALU.add, [bci, self.B("SCI"), pbn], [bC0])

        LA = 2
        for k in range(min(LA, len(tiles))):
            front(k)
        for k in range(len(tiles)):
            if k + LA < len(tiles):
                front(k + LA)
            back(k)

    def mlstm_own(self, g, hT, hT_bufs, U64, E64, Z64, DEC, C0, hmT):
        I, O = self.I, self.O
        nseq, n_own, nck, T_own = g.nseq, g.n_own, g.nck, g.T_own
        TT = nseq * T_own
        NC = nseq * nck
        NW = 8 if g.is_p else 7
        wpr = [self.sb(f"mw{i}", [128, 8, 256], BF16) for i in range(NW)]
        qT = self.sb("qT", [128, 2, TT], BF16)
        kT = self.sb("kT", [128, 2, TT], BF16)
        ktok = self.sb("ktok", [128, NC, 256], BF16)
        vext = self.sb("vext", [128, NC, 257], BF16)
        sigo = self.sb("sigo", [128, NC, 256], BF16)
        hraw = [self.sb(f"hraw{d}", [128, NC, 257], F32) for d in range(2)]
        hm = hraw[0]
        Ecp = self.sb("Ecp", [128, 2, NC], F32)
        Rd = self.sb("Rd", [128, 2, NC], F32)
        Rd2 = self.sb("Rd2", [128, 2, NC], F32)
        Cst = [[self.sb(f"Cst{q}{d}", [128, 2, 257], F32) for d in range(2)] for q in range(nseq)]
        Cbf = [[[self.sb(f"Cbf{q}{d}{i}", [128, 2, 257], BF16) for i in range(2)] for d in range(2)] for q in range(nseq)]
        PTall = self.sb("PTall", [128, 2, NC, 128], BF16)
        kpr = [[[self.sb(f"kpr{q}{d}{i}", [128, 256], BF16) for i in range(2)] for d in range(2)] for q in range(nseq)]
        sm = [[self.sb(f"sm{d}{i}", [128, 4], F32) for i in range(2)] for d in range(2)]
        mng = self.sb("mng", [128, 256], F32)
        hsq = self.sb("hsq", [128, 256], F32)
        ssq = self.sb("ssq", [128, NC], F32)
        hn = [self.sb(f"hn{i}", [128, 256], BF16) for i in range(2)]
        self.memset(vext[:, :, 256:257], 1.0, [self.B("vones")])
        Uf, Ub = self.cm[:, 1, :], self.cm[:, 2, :]
        bvo = self.B("vones")
        loaded = set()
        for h in range(4):
            slots = [(h * 4 + pc) % NW for pc in range(4)]
            bw = [self.B("mw", sl_) for sl_ in slots]
            wp = [wpr[sl_] for sl_ in slots]
            self.load(mng[:], I["mng"][h * 256:(h + 1) * 256].partition_broadcast(128), [self.B("mng")])
            for pc in range(4):
                if (h, pc) not in loaded:
                    self.load(wp[pc][:], I["wml"][h, pc].rearrange("p (c n) -> p c n", c=8), [bw[pc]], q=POOL)
                    loaded.add((h, pc))
            if h + 1 < 4:
                for pc in range(NW - 4):
                    sl_ = ((h + 1) * 4 + pc) % NW
                    self.load(wpr[sl_][:], I["wml"][h + 1, pc].rearrange("p (c n) -> p c n", c=8),
                              [self.B("mw", sl_)], q=POOL)
                    loaded.add((h + 1, pc))
            wq_, wk_, wv_, wo_ = wp
            blk = 512 if TT >= 512 else TT
            cnt = 0
            for (wsrc, bws, dstT, nm) in ((wq_, bw[0], qT, "qT"), (wk_, bw[1], kT, "kT")):
                for dc in range(2):
                    for b0 in range(0, TT, blk):
                        pt = self.ps[cnt % 2]
                        pb = self.B("ps", cnt % 2)
                        cnt += 1
                        tl = range(b0 // 128, (b0 + blk) // 128)
                        rd = [self.B("hT", g.name, tt_ // n_own, tt_ % n_own, c) for tt_ in tl for c in range(8)]
                        for c in range(8):
                            self.mm(pt[:, 0:blk], wsrc[:, c, dc * 128:(dc + 1) * 128], hT[:, c, b0:b0 + blk],
                                    c == 0, c == 7, rd + [bws], [pb])
                        self.evac_cast(dstT[:, dc, b0:b0 + blk], pt[:, 0:blk], [pb],
                                       [self.B(nm, dc, tt_) for tt_ in tl])
            for ck in range(NC):
                q, cl = ck // nck, ck % nck
                t = cl
                cols = slice(q * T_own + cl * 128, q * T_own + cl * 128 + 128)
                rd = hT_bufs(q, t)
                pt = self.ps[2 + ck % 2]
                pb = self.B("ps", 2 + ck % 2)
                for c in range(8):
                    self.mm(pt[:, 0:256], hT[:, c, cols], wk_[:, c, :], c == 0, c == 7, rd + [bw[1]], [pb],
                            signal=False)
                for c in range(8):
                    self.mm(pt[:, 256:512], hT[:, c, cols], wv_[:, c, :], c == 0, c == 7, rd + [bw[2]], [pb])
                self.cp(ktok[:, ck, :], pt[:, 0:256], [pb], [self.B("ktok", ck)], eng=ACT)
                self.cp(vext[:, ck, 0:256], pt[:, 256:512], [pb], [self.B("vext", ck), bvo])
                pt2 = self.ps[4 + ck % 2]
                pb2 = self.B("ps", 4 + ck % 2)
                for c in range(8):
                    self.mm(pt2[:, 0:256], hT[:, c, cols], wo_[:, c, :], c == 0, c == 7, rd + [bw[3]], [pb2])
                self.act(sigo[:, ck, :], pt2[:, 0:256], AF.Sigmoid, [pb2], [self.B("sigo", ck)])
            cntA = 0
            for q in range(nseq):
                for cl in range(nck):
                    for d in range(2):
                        hd = d * 4 + h
                        ck = q * nck + cl
                        t = cl
                        cols = slice(q * T_own + cl * 128, q * T_own + cl * 128 + 128)
                        bank = 2 + cntA % 4
                        cntA += 1
                        pS = self.ps[bank][:, 0:128]
                        bS = self.B("ps", bank)
                        for dc in range(2):
                            self.mm(pS, kT[:, dc, cols], qT[:, dc, cols], dc == 0, dc == 1,
                                    [self.B("kT", dc, (q * T_own) // 128 + t), self.B("qT", dc, (q * T_own) // 128 + t)],
                                    [bS])
                        self.stt(PTall[:, d, ck, :], pS, U64[:, q, cl, hd:hd + 1], (Uf if d == 0 else Ub),
                                 ALU.mult, ALU.mult, [bS, self.B("U64", q), self.B("consts")], [self.B("PT", d, ck)])
            def idx(q, d, step):
                cl = step if d == 0 else nck - 1 - step
                return cl, q * nck + cl

            def emit_kpr(q, d, step):
                hd = d * 4 + h
                cl, ck = idx(q, d, step)
                kdst = kpr[q][d][step % 2]
                self.S.op(ACT, lambda e, kdst=kdst, ck=ck, q=q, cl=cl, hd=hd: e.activation(
                    out=kdst[:], in_=ktok[:, ck, :], func=AF.Copy, scale=Z64[:, q, cl, hd:hd + 1]),
                    [self.B("ktok", ck), self.B("Z64", q)], [self.B("kpr", q, d, step % 2)])

            def emit_U(q, d, step):
                cl, ck = idx(q, d, step)
                par = step % 2
                slot = par if nseq == 1 else q
                bkp = self.B("kpr", q, d, par)
                pUC = self.ps[2 + d * 2 + slot]
                bUC = self.B("ps", 2 + d * 2 + slot)
                for dc in range(2):
                    self.mm(pUC[:, dc * 256:(dc + 1) * 256], kpr[q][d][par][:, dc * 128:(dc + 1) * 128],
                            vext[:, ck, 0:256], True, True, [bkp, self.B("vext", ck)], [bUC])
                n0 = (d * 2 + slot) * 2
                for dc in range(2):
                    self.mm(self.ps[6][:, n0 + dc:n0 + dc + 1], kpr[q][d][par][:, dc * 128:(dc + 1) * 128],
                            vext[:, ck, 256:257], True, True, [bkp, bvo], [self.B("ps", 6)])

            for q in range(nseq):
                for d in range(2):
                    hd = d * 4 + h
                    self.cp(Cst[q][d][:], C0[:, q, hd, :, :], [self.B("C0", q, hd)], [self.B("Cst", q, d)])
                    self.cp(Cbf[q][d][0][:], C0[:, q, hd, :, :], [self.B("C0", q, hd)], [self.B("Cbf", q, d, 0)], eng=ACT)
                    if nck > 1:
                        emit_kpr(q, d, 0)
                        emit_U(q, d, 0)
                    if nck > 2:
                        emit_kpr(q, d, 1)
            for step in range(nck):
                cur, nxt = step % 2, (step + 1) % 2
                for q in range(nseq):
                    for d in range(2):
                        hd = d * 4 + h
                        cl, ck = idx(q, d, step)
                        t = cl
                        cols = slice(q * T_own + cl * 128, q * T_own + cl * 128 + 128)
                        bCs = self.B("Cst", q, d)
                        if step < nck - 1:
                            slot = cur if nseq == 1 else q
                            pUC = self.ps[2 + d * 2 + slot]
                            bUC = self.B("ps", 2 + d * 2 + slot)
                            n0 = (d * 2 + slot) * 2
                            self.stt(Cst[q][d][:, :, 0:256], Cst[q][d][:, :, 0:256], DEC[:, q, cl, hd:hd + 1],
                                     pUC[:].rearrange("p (c e) -> p c e", c=2), ALU.mult, ALU.add,
                                     [bCs, self.B("DEC", q), bUC], [bCs])
                            self.stt(Cst[q][d][:, :, 256], Cst[q][d][:, :, 256], DEC[:, q, cl, hd:hd + 1],
                                     self.ps[6][:, n0:n0 + 2], ALU.mult, ALU.add,
                                     [bCs, self.B("DEC", q), self.B("ps", 6)], [bCs])
                            self.cp(Cbf[q][d][nxt][:], Cst[q][d][:], [bCs], [self.B("Cbf", q, d, nxt)], eng=ACT)
                            if step + 1 < nck - 1:
                                emit_U(q, d, step + 1)
                            if step + 2 < nck - 1:
                                emit_kpr(q, d, step + 2)
                        pO = self.ps[d][:, 0:257]
                        bO = self.B("ps", d)
                        self.mm(pO, PTall[:, d, ck, :], vext[:, ck, :], True, False,
                                [self.B("PT", d, ck), self.B("vext", ck), bvo], [bO], signal=False)
                        for dc in range(2):
                            self.mm(pO, qT[:, dc, cols], Cbf[q][d][cur][:, dc, :], False, dc == 1,
                                    [self.B("qT", dc, (q * T_own) // 128 + t), self.B("Cbf", q, d, cur)], [bO])
                        self.cp(hraw[d][:, ck, :], pO[:, 0:257], [bO], [self.B("hraw", d, ck)], eng=ACT)
            bR = self.B("Rden")
            rdall = [self.B("hraw", d, ck) for d in range(2) for ck in range(NC)]
            for d in range(2):
                self.cp(Ecp[:, d, :].rearrange("p (q c) -> p q c", q=nseq), E64[:, :, :, d * 4 + h],
                        [self.B("E64", q) for q in range(nseq)], [bR])
                self.tt(Rd[:, d, :], hraw[d][:, :, 256], Ecp[:, d, :], ALU.mult, rdall + [bR], [bR])
            self.stt(Rd2[:], Rd[:], -1.0, Rd[:], ALU.mult, ALU.max, [bR], [bR])
            self.ts(Rd2[:], Rd2[:], 1.0, None, ALU.max, None, [bR], [bR])
            self.S.op(DVE, lambda e: e.reciprocal(out=Rd2[:], in_=Rd2[:]), [bR], [bR])
            self.tt(Rd[:], Ecp[:], Rd2[:], ALU.mult, [bR], [bR])
            for d in range(2):
                self.tt(hraw[d][:, :, 0:256], hraw[d][:, :, 0:256],
                        Rd[:, d, :].unsqueeze(2).to_broadcast([128, NC, 256]), ALU.mult,
                        [self.B("hraw", d, ck) for ck in range(NC)] + [bR], [self.B("hraw", d, ck) for ck in range(NC)])
            for ck in range(NC):
                self.tt(hm[:, ck, 0:256], hraw[0][:, ck, 0:256], hraw[1][:, ck, 0:256], ALU.add,
                        [self.B("hraw", 0, ck), self.B("hraw", 1, ck)], [self.B("hm", ck)])
            bss = self.B("ssq")
            for ck in range(NC):
                self.S.op(ACT, lambda e, ck=ck: e.activation(out=hsq[:], in_=hm[:, ck, 0:256], func=AF.Square,
                                                             accum_out=ssq[:, ck:ck + 1]),
                          [self.B("hm", ck)], [bss, self.B("hsq")])
            self.act(ssq[:], ssq[:], AF.Sqrt, [bss], [bss], bias=EPS, scale=1.0 / 256.0)
            self.S.op(DVE, lambda e: e.reciprocal(out=ssq[:], in_=ssq[:]), [bss], [bss])
            for ck in range(NC):
                q, cl = ck // nck, ck % nck
                i2 = ck % 2
                bhm = self.B("hm", ck)
                bhn = self.B("hn", i2)
                self.stt(hm[:, ck, 0:256], hm[:, ck, 0:256], ssq[:, ck:ck + 1], mng[:], ALU.mult, ALU.mult,
                         [bhm, bss, self.B("mng")], [bhm])
                self.tt(hn[i2][:], hm[:, ck, 0:256], sigo[:, ck, :], ALU.mult, [bhm, self.B("sigo", ck)], [bhn])
                ptr = self.ps[7]
                pbt = self.B("ps", 7)
                ptv = ptr[:].bitcast(BF16)
                for dc in range(2):
                    self.tr(ptv[:, i2 * 256 + dc * 128:i2 * 256 + dc * 128 + 128], hn[i2][:, dc * 128:(dc + 1) * 128],
                            self.identb[:], [bhn, self.B("identb")], [pbt], signal=(dc == 1))
                c0 = q * T_own + cl * 128
                self.cp(hmT[:, 2 * h:2 * h + 2, c0:c0 + 128],
                        ptv[:, i2 * 256:i2 * 256 + 256].rearrange("p (c n) -> p c n", c=2),
                        [pbt], [self.B("hmT", h, ck)], eng=ACT)

    def attention(self, g, hT, hT_bufs, KaT, Va, haT):
        I, O = self.I, self.O
        nseq, n_own, T_own = g.nseq, g.n_own, g.T_own
        TT = nseq * T_own
        nkt = g.n_ctx + g.n_cache
        QT = self.sb("QT", [128, 8, TT], BF16)
        wq = [self.sb(f"wqa{i}", [128, 8, 512], BF16) for i in range(2)]
        for i in range(2):
            self.load(wq[i][:], I["wqa"][i].rearrange("p (c n) -> p c n", c=8), [self.B("wqa", i)], q=POOL)
        qn = [self.sb(f"qn{i}", [128, 8, 128], F32) for i in range(2)]
        qb = [self.sb(f"qb{i}", [128, 8, 128], BF16) for i in range(2)]
        scr1 = self.sb("qscr", [128, 8, 128], F32)
        scr = [scr1, scr1]
        ss = [self.sb(f"qss{i}", [128, 8], F32) for i in range(2)]
        t1 = [self.sb(f"qt1{i}", [128, 8, 128], F32) for i in range(2)]
        t2 = [self.sb(f"qt2{i}", [128, 8, 128], F32) for i in range(2)]
        if g.rope:
            ropeT = self.sb("ropeT", [128, n_own, 128], F32)
            self.load(ropeT[:], I["rope"][0:n_own * 128, :].rearrange("(t p) n -> p t n", p=128), [self.B("ropeT")])
        def q_job(q, t):
            tg = q * n_own + t
            i2 = tg % 2
            cols = slice(tg * 128, tg * 128 + 128)
            pts = [self.ps[2 * i2], self.ps[2 * i2 + 1]]
            pbs = [self.B("ps", 2 * i2), self.B("ps", 2 * i2 + 1)]
            for hf in range(2):
                for c in range(8):
                    self.mm(pts[hf][:], hT[:, c, cols], wq[hf][:, c, :], c == 0, c == 7,
                            hT_bufs(q, t) + [self.B("wqa", hf)], [pbs[hf]])
            yield
            bqn, bss = self.B("qn", i2), self.B("qss", i2)
            for hf in range(2):
                yield from self.rms_heads(pts[hf][:].rearrange("p (h d) -> p h d", h=4), 4, self.QG[:],
                                          qn[i2][:, hf * 4:(hf + 1) * 4, :], [pbs[hf]], bqn,
                                          scr[i2][:, hf * 4:(hf + 1) * 4, :], ss[i2][:, hf * 4:(hf + 1) * 4], bss)
            if g.rope:
                yield from self.rope_apply(qn[i2][:], 8, ropeT[:, t, :], self.B("ropeT"), bqn, t1[i2][:], t2[i2][:],
                                           self.B("qrtmp", i2))
            bqb = self.B("qb", i2)
            self.cp(qb[i2][:], qn[i2][:], [bqn], [bqb], eng=ACT)
            yield
            for hf in range(2):
                ptr = self.ps[4 + 2 * i2 + hf]
                pbt = self.B("ps", 4 + 2 * i2 + hf)
                ptv = ptr[:].bitcast(BF16)
                for hh in range(4):
                    self.tr(ptv[:, hh * 128:(hh + 1) * 128], qb[i2][:, hf * 4 + hh, :], self.identb[:],
                            [bqb, self.B("identb")], [pbt], signal=(hh == 3))
                yield
                self.cp(QT[:, hf * 4:(hf + 1) * 4, cols], ptv[:, 0:512].rearrange("p (h n) -> p h n", h=4),
                        [pbt], [self.B("QT", hf, tg)], eng=(ACT if hf else DVE))
                yield

        self.interleave((q_job(q, t) for q in range(nseq) for t in range(n_own)), 2, stagger=10)
        PTs = [self.sb(f"aPT{i}", [128, 512], BF16) for i in range(5)]
        DACC = [self.sb(f"dacc{i}", [128, 512], F32) for i in range(2)]
        rden = [self.sb(f"rden{i}", [128, 512], F32) for i in range(2)]
        qblk = 512 if T_own >= 512 else T_own
        scale = 128.0 ** -0.5
        ONES = self.cm[:, 5, :]
        its = []
        ob = 0
        for q in range(nseq):
            for hq in range(8):
                for b0 in range(0, T_own, qblk):
                    for kt in range(nkt):
                        its.append((q, hq, b0, kt, ob))
                    ob += 1
        use_pool = nkt >= 8

        def issue_st(i):
            q, hq, b0, kt, ob_ = its[i]
            kvh = hq // 4
            c0 = q * T_own + b0
            tgl = range(c0 // 128, (c0 + qblk) // 128)
            pS = self.ps[i % 5]
            bS = self.B("ps", i % 5)
            self.mm(pS[:, 0:qblk], KaT[:, q, kvh, kt * 128:(kt + 1) * 128], QT[:, hq, c0:c0 + qblk],
                    True, True, [self.B("KaT", q, kt)] + [self.B("QT", hq // 4, tg) for tg in tgl], [bS])

        LA = 4
        for i in range(min(LA, len(its))):
            issue_st(i)
        for i in range(len(its)):
            q, hq, b0, kt, ob_ = its[i]
            kvh = hq // 4
            c0 = q * T_own + b0
            tgl = range(c0 // 128, (c0 + qblk) // 128)
            par = ob_ % 2
            pO = self.ps[5 + par]
            bO = self.B("ps", 5 + par)
            pS = self.ps[i % 5]
            bS = self.B("ps", i % 5)
            pt_ = PTs[i % 5]
            bP = self.B("aPT", i % 5)
            self.act(pt_[:, 0:qblk], pS[:, 0:qblk], AF.Exp, [bS], [bP], scale=scale)
            if i + LA < len(its):
                issue_st(i + LA)
            self.mm(pO[:, 0:qblk], Va[:, q, kt, kvh, 0:128], pt_[:, 0:qblk], kt == 0, kt == nkt - 1,
                    [bP, self.B("Va", q, kt)], [bO])
            pD = self.ps[7]
            bD = self.B("ps", 7)
            if kt % 2 == 1:
                self.mm(pD[:, 0:qblk], self.onesb128[:], pt_[:, 0:qblk], kt == 1, False,
                        [bP, self.B("onesb128")], [bD], signal=False)
            else:
                acc = DACC[par]
                bacc = self.B("dacc", par)
                if kt == 0:
                    self.cp(acc[:, 0:qblk], pt_[:, 0:qblk], [bP], [bacc])
                else:
                    self.tt(acc[:, 0:qblk], acc[:, 0:qblk], pt_[:, 0:qblk], ALU.add, [bacc, bP], [bacc])
            if kt == nkt - 1:
                self.mm(pD[:, 0:qblk], ONES, DACC[par][:, 0:qblk], nkt == 1, True,
                        [self.B("dacc", par), self.B("consts")], [bD])
                brd = self.B("rden", par)
                self.S.op(DVE, lambda e, par=par, pD=pD: e.reciprocal(out=rden[par][:, 0:qblk], in_=pD[:, 0:qblk]),
                          [bD], [brd])
                self.tt(haT[:, hq, c0:c0 + qblk], pO[:, 0:qblk], rden[par][:, 0:qblk], ALU.mult, [bO, brd],
                        [self.B("haT", hq, tg) for tg in tgl])

    def post(self, g, hT, hmT, haT):
        I, O = self.I, self.O
        nseq, n_own, T_own = g.nseq, g.n_own, g.T_own
        TT = nseq * T_own
        ntile = TT // 128
        s = g.mset
        xsrc = I["xp"] if g.is_p else I["xs"]
        ydst = O["yp"] if g.is_p else O["ys"]

        def xrow(tg):
            q, t = tg // n_own, tg % n_own
            return (q * g.n_ctx + t) * 128
        x1a = self.sb("x1a", [128, ntile, 1024], F32)
        h2T = self.sb("h2T", [128, 8, TT], BF16)
        mark = self.cur
        w5 = [self.sb(f"w5{i}", [128, 4, 8, 128], BF16) for i in range(3)]
        wo = [self.sb(f"wo{i}", [128, 8, 512], BF16) for i in range(2)]
        mT = self.sb("mT", [128, 8, 512], BF16)
        xin = [self.sb(f"xin5{i}", [128, 1024], F32) for i in range(2)]
        ytmp = [self.sb(f"ytmp{i}", [128, 1024], F32) for i in range(2)]
        rows = self.sb("rows5", [128, 2, 1024], F32)
        sgm = [self.sb(f"sgm{i}", [128, 512], F32) for i in range(2)]
        sga = [self.sb(f"sga{i}", [128, 512], F32) for i in range(2)]
        m1 = [self.sb(f"m1{i}", [128, 512], F32) for i in range(2)]
        st = self.sb("st5", [128, 2, 12], F32)
        mv = self.sb("mv5", [128, 2, 4], F32)
        brow = self.B("rows5")
        self.load(rows[:, 0, :], I["ln1g"].partition_broadcast(128), [brow])
        self.load(rows[:, 1, :], I["ln1b"].partition_broadcast(128), [brow])
        self.ts(rows[:], rows[:], ALPHA, None, ALU.mult, None, [brow], [brow], eng=POOL)
        for i in range(2):
            self.load(wo[i][:], I["wout"][i].rearrange("p (c n) -> p c n", c=8), [self.B("wo", i)], q=POOL)
        S2 = self.COL[:, s, 2, :]
        B2 = self.COL[:, s, 3, :]
        bcol = self.B("COL", s)
        nblk = TT // 512
        xc = 0
        for blk in range(nblk):
            b0 = blk * 512
            tgl = list(range(b0 // 128, b0 // 128 + 4))
            rd_h = [self.B("hT", g.name, tg // n_own, tg % n_own, c) for tg in tgl for c in range(8)]
            rd_hm = [self.B("hmT", h, ck) for h in range(4) for ck in range(b0 // 128, b0 // 128 + 4)]
            rd_ha = [self.B("haT", hq, tg) for hq in range(8) for tg in tgl]
            for fc in range(8):
                wi = (blk * 8 + fc) % 3
                w = w5[wi]
                bw = self.B("w5", wi)
                self.load(w[:], I["w5"][fc].rearrange("p (k c n) -> p k c n", k=4, c=8), [bw], q=POOL)
                srcs = (hmT, haT, hT, hT)
                rds = (rd_hm, rd_ha, rd_h, rd_h)
                pof = 4 * (fc % 2)
                pss = [self.ps[pof + k] for k in range(4)]
                pbs = [self.B("ps", pof + k) for k in range(4)]
                for k in (2, 3, 0, 1):
                    for c in range(8):
                        self.mm(pss[k][:], w[:, k, c, :], srcs[k][:, c, b0:b0 + 512], c == 0, c == 7,
                                rds[k] + [bw], [pbs[k]])
                f2 = fc % 2
                bsg, bsa, bm1 = self.B("sgm", f2), self.B("sga", f2), self.B("m1", f2)
                self.act(sgm[f2][:], pss[2][:], AF.Sigmoid, [pbs[2]], [bsg])
                self.act(sga[f2][:], pss[3][:], AF.Sigmoid, [pbs[3]], [bsa])
                self.tt(m1[f2][:], pss[0][:], sgm[f2][:], ALU.mult, [pbs[0], bsg], [bm1])
                self.tt(sga[f2][:], pss[1][:], sga[f2][:], ALU.mult, [pbs[1], bsa], [bsa])
                self.tt(mT[:, fc, :], m1[f2][:], sga[f2][:], ALU.add, [bm1, bsa], [self.B("mT", fc)])
            rd_m = [self.B("mT", fc) for fc in range(8)]

            def tile_front(ti, tg):
                nonlocal xc
                i2 = xc % 2
                xc += 1
                bx = self.B("xin5", i2)
                self.load(xin[i2][:], xsrc[xrow(tg):xrow(tg) + 128, :], [bx])
                pm = [self.ps[4 + (ti % 2) * 2], self.ps[5 + (ti % 2) * 2]]
                bpm = [self.B("ps", 4 + (ti % 2) * 2), self.B("ps", 5 + (ti % 2) * 2)]
                for hf in range(2):
                    for c in range(8):
                        self.mm(pm[hf][:], mT[:, c, ti * 128:(ti + 1) * 128], wo[hf][:, c, :], c == 0, c == 7,
                                rd_m + [self.B("wo", hf)], [bpm[hf]])
                y = ytmp[i2]
                by = self.B("ytmp", i2)
                for hf in range(2):
                    self.tt(y[:, hf * 512:(hf + 1) * 512], pm[hf][:], self.G1[:, s, hf * 512:(hf + 1) * 512], ALU.mult,
                            [bpm[hf], self.B("G1", s)], [by])
                self.stt(y[:], xin[i2][:], ALPHA, y[:], ALU.mult, ALU.add, [bx, by], [by])
                self.layernorm(y, by, st, mv, i2)
                bxa = self.B("x1a", tg)
                self.tt(x1a[:, tg, :], y[:], rows[:, 0, :], ALU.mult, [by, brow], [bxa])
                self.tt(x1a[:, tg, :], x1a[:, tg, :], rows[:, 1, :], ALU.add, [bxa, brow], [bxa])
                return i2

            def tile_back(ti, tg, i2):
                y = ytmp[i2]
                by = self.B("ytmp", i2)
                pbank = [self.ps[0 + (ti % 2) * 2], self.ps[1 + (ti % 2) * 2]]
                pbb = [self.B("ps", 0 + (ti % 2) * 2), self.B("ps", 1 + (ti % 2) * 2)]
                for c in range(8):
                    self.tr(pbank[c // 4][:, (c % 4) * 128:(c % 4 + 1) * 128], y[:, c * 128:(c + 1) * 128],
                            self.ident, [by, self.B("consts")], [pbb[c // 4]], signal=(c % 4 == 3))
                for c in range(8):
                    src = pbank[c // 4][:, (c % 4) * 128:(c % 4 + 1) * 128]
                    o_ap = h2T[:, c, tg * 128:(tg + 1) * 128]
                    bh = self.B("h2T", tg, c)
                    if c // 4 == 0:
                        self.S.op(ACT, lambda e, o_ap=o_ap, src=src, c=c: e.activation(
                            out=o_ap, in_=src, func=AF.Identity, scale=S2[:, c:c + 1], bias=B2[:, c:c + 1]),
                            [pbb[c // 4], bcol], [bh])
                    else:
                        self.ts(o_ap, src, S2[:, c:c + 1], B2[:, c:c + 1], ALU.mult, ALU.add, [pbb[c // 4], bcol], [bh])

            prev = None
            for ti, tg in enumerate(tgl):
                i2 = tile_front(ti, tg)
                if prev is not None:
                    tile_back(*prev)
                prev = (ti, tg, i2)
            tile_back(*prev)
        self.S.barrier()
        self.cur = self.base + self.arena0
        uT = self.sb("uT", [128, 32, 512], BF16)
        rows6 = self.sb("rows6", [128, 2, 1024], F32)
        y2 = [self.sb(f"y2{i}", [128, 1024], F32) for i in range(2)]
        assert self.cur <= self.base + self.arena0 + 48 * 1024
        self.cur = mark
        wd = self.sb("wd", [128, 32, 1024], BF16)
        wu = [self.sb(f"wu{i}", [128, 8, 256], BF16) for i in range(4)]
        ur = [self.sb(f"ur{i}", [128, 512], F32) for i in range(2)]
        st = self.sb("st6", [128, 2, 12], F32)
        mv = self.sb("mv6", [128, 2, 4], F32)
        br6 = self.B("rows6")
        self.load(rows6[:, 0, :], I["ln2g"].partition_broadcast(128), [br6])
        self.load(rows6[:, 1, :], I["ln2b"].partition_broadcast(128), [br6])
        wc = 0
        two_path = g.is_p
        pcs = 0
        if two_path:
            wstg = [self.sb(f"wdstg{i}", [128, 2, 1024], F32) for i in range(2)]
        for blk in range(nblk):
            b0 = blk * 512
            tgl = list(range(b0 // 128, b0 // 128 + 4))
            rd_h2 = [self.B("h2T", tg, c) for tg in tgl for c in range(8)]
            for sl in range(16):
                wi = wc % 4
                wc += 1
                bw = self.B("wu", wi)
                self.load(wu[wi][:], I["wup"][sl].rearrange("p (c n) -> p c n", c=8), [bw], q=POOL)
                if two_path and sl % 4 == 0:
                    qd = sl // 4
                    for piece in range(4):
                        stg = wstg[pcs % 2]
                        bst = self.B("wdstg", pcs % 2)
                        pcs += 1
                        self.load(stg[:], I["wdn"][qd][:, piece * 2048:(piece + 1) * 2048].rearrange(
                            "p (c n) -> p c n", c=2), [bst], q=SP)
                        fc0 = qd * 8 + piece * 2
                        self.cp(wd[:, fc0:fc0 + 2, :], stg[:], [bst], [self.B("wd", qd)])
                if (not two_path) and sl % 4 == 3:
                    qd = sl // 4
                    self.load(wd[:, qd * 8:(qd + 1) * 8, :], I["wdn"][qd].rearrange("p (c n) -> p c n", c=8),
                              [self.B("wd", qd)], q=POOL)
                for f4 in range(2):
                    fc = sl * 2 + f4
                    pt = self.ps[fc % 4]
                    pb = self.B("ps", fc % 4)
                    for c in range(8):
                        self.mm(pt[:], wu[wi][:, c, f4 * 128:(f4 + 1) * 128], h2T[:, c, b0:b0 + 512], c == 0, c == 7,
                                rd_h2 + [bw], [pb])
                    bur = self.B("ur", fc % 2)
                    self.act(ur[fc % 2][:], pt[:], AF.Relu, [pb], [bur])
                    self.tt(uT[:, fc, :], ur[fc % 2][:], ur[fc % 2][:], ALU.mult, [bur], [self.B("uT", fc)])
            rd_u = [self.B("uT", fc) for fc in range(32)]
            for ti, tg in enumerate(tgl):
                i2 = ti % 2
                pm = [self.ps[4 + i2 * 2], self.ps[5 + i2 * 2]]
                bpm = [self.B("ps", 4 + i2 * 2), self.B("ps", 5 + i2 * 2)]
                for hf in range(2):
                    for fc in range(32):
                        self.mm(pm[hf][:], uT[:, fc, ti * 128:(ti + 1) * 128], wd[:, fc, hf * 512:(hf + 1) * 512],
                                fc == 0, fc == 31, rd_u + [self.B("wd", fc // 8)], [bpm[hf]])
                y = y2[i2]
                by = self.B("y2", i2)
                for hf in range(2):
                    self.tt(y[:, hf * 512:(hf + 1) * 512], pm[hf][:], self.G2[:, s, hf * 512:(hf + 1) * 512], ALU.mult,
                            [bpm[hf], self.B("G2", s)], [by])
                self.tt(y[:], y[:], x1a[:, tg, :], ALU.add, [by, self.B("x1a", tg)], [by])
                self.layernorm(y, by, st, mv, i2)
                self.tt(y[:], y[:], rows6[:, 0, :], ALU.mult, [by, br6], [by])
                self.tt(y[:], y[:], rows6[:, 1, :], ALU.add, [by, br6], [by])
                self.store(ydst[tg * 128:(tg + 1) * 128, :], y[:], [by])

    def layernorm(self, y, by, st, mv, i2):
        bst = self.B("lnst", i2)
        self.S.op(DVE, lambda e: e.bn_stats(out=st[:, i2, 0:6], in_=y[:, 0:512]), [by], [bst])
        self.S.op(DVE, lambda e: e.bn_stats(out=st[:, i2, 6:12], in_=y[:, 512:1024]), [by], [bst])
        self.S.op(DVE, lambda e: e.bn_aggr(out=mv[:, i2, 0:2], in_=st[:, i2, :]), [bst], [bst])
        self.act(mv[:, i2, 2:3], mv[:, i2, 1:2], AF.Sqrt, [bst], [bst], bias=EPS, scale=1.0)
        self.S.op(DVE, lambda e: e.reciprocal(out=mv[:, i2, 2:3], in_=mv[:, i2, 2:3]), [bst], [bst])
        self.ts(y[:], y[:], mv[:, i2, 0:1], mv[:, i2, 2:3], ALU.subtract, ALU.mult, [by, bst], [by])


def _lay(w):
    n = w.shape[1]
    return np.ascontiguousarray(w.reshape(8, 128, n).transpose(1, 0, 2).reshape(128, 8 * n))


def _rope_tables():
    T, GW, NF = 4096, 64, 32
    rows = T // GW
    row = np.repeat(np.arange(rows), GW)
    col = np.tile(np.arange(GW), rows)
    inv = (np.float32(10000.0) ** (-np.arange(NF, dtype=np.float32) / np.float32(NF))).astype(np.float32)
    ang = np.stack([row, col], -1).astype(np.float32)[..., None] * inv
    return np.concatenate([np.cos(ang).reshape(T, 64), np.sin(ang).reshape(T, 64)], axis=1).astype(np.float32)


_NC_CACHE = {}


def kernel(x_prompt, x_sample, cache_k, cache_v, state_C, state_n, state_m, c, c_ctx,
           w_mod, b_mod, w_in, b_gates, mlstm_norm_g, q_norm_g, k_norm_g, w_bm, w_ba, w_out,
           ln1_g, ln1_b, w_up, w_down, ln2_g, ln2_b, _dbg=None, _stop=None):
    f = lambda a: np.ascontiguousarray(np.asarray(a, dtype=np.float32))
    x_prompt, x_sample, w_in0 = f(x_prompt), f(x_sample), f(w_in)[0]
    w_mod0, w_bm0, w_ba0, w_out0, w_up0, w_down0 = f(w_mod)[0], f(w_bm)[0], f(w_ba)[0], f(w_out)[0], f(w_up)[0], f(w_down)[0]
    perm = np.array([0, 1, 2, 3, 8, 9, 10, 11, 4, 5, 6, 7, 12, 13, 14, 15])
    shared = {}
    shared["wmod"] = np.stack([_lay(w_mod0[:, j * 512:(j + 1) * 512]) for j in range(12)])
    shared["bmod"] = f(b_mod)[0]
    shared["wg"] = _lay(w_in0[:, 4096 + perm])
    shared["bg"] = f(b_gates)[0][perm].copy()
    shared["wml"] = np.stack([np.stack([_lay(w_in0[:, p * 1024 + h * 256:p * 1024 + (h + 1) * 256]) for p in range(4)])
                              for h in range(4)])
    shared["wqa"] = np.stack([_lay(w_in0[:, 4112 + i * 512:4112 + (i + 1) * 512]) for i in range(2)])
    shared["wkva"] = _lay(w_in0[:, 5136:5648])
    w5 = []
    for fc in range(8):
        sl = slice(fc * 128, (fc + 1) * 128)
        parts = [w_bm0[:, sl], w_ba0[:, sl], w_in0[:, 5648 + fc * 128:5648 + (fc + 1) * 128],
                 w_in0[:, 6672 + fc * 128:6672 + (fc + 1) * 128]]
        w5.append(np.concatenate([_lay(p) for p in parts], axis=1))
    shared["w5"] = np.stack(w5)
    shared["wout"] = np.stack([_lay(w_out0[:, i * 512:(i + 1) * 512]) for i in range(2)])
    shared["wup"] = np.stack([_lay(w_up0[:, i * 256:(i + 1) * 256]) for i in range(16)])
    shared["wdn"] = np.ascontiguousarray(w_down0.reshape(4, 8, 128, 1024).transpose(0, 2, 1, 3).reshape(4, 128, 8192))
    shared["mng"] = f(mlstm_norm_g)[0]
    shared["qg"] = f(q_norm_g)[0]
    shared["kg"] = f(k_norm_g)[0]
    shared["ln1g"], shared["ln1b"] = f(ln1_g)[0], f(ln1_b)[0]
    shared["ln2g"], shared["ln2b"] = f(ln2_g)[0], f(ln2_b)[0]
    p = np.arange(128)
    cm = np.zeros((128, 6, 128), np.float32)
    cm[:, 0, :] = (p[:, None] == p[None, :])
    cm[:, 1, :] = (p[:, None] <= p[None, :])
    cm[:, 2, :] = (p[:, None] >= p[None, :])
    cm[:, 3, :] = (p[:, None] < p[None, :])
    cm[:, 4, :] = (p[:, None] > p[None, :])
    cm[:, 5, :] = 1.0
    shared["cm"] = cm
    rope = _rope_tables()
    in_maps = []
    for r in range(8):
        b, j = r // 4, r % 4
        order = [(j + i) % 4 for i in range(4)]
        m = dict(shared)
        m["xp"] = x_prompt[2 * r:2 * r + 2].reshape(512, 1024)
        m["xs"] = np.ascontiguousarray(x_sample[b].reshape(4, 1024, 1024)[order].reshape(4096, 1024))
        m["rope"] = np.ascontiguousarray(rope.reshape(4, 1024, 128)[order].reshape(4096, 128))
        m["ck"] = f(cache_k)[b, 0].reshape(256, 256)
        m["cv"] = f(cache_v)[b, 0].reshape(256, 256)
        m["sC"] = f(state_C)[b, 0].reshape(8, 256, 256)
        m["sn"] = f(state_n)[b, 0].reshape(8, 256)
        m["sm"] = f(state_m)[b, 0].reshape(8)
        cvec = np.stack([f(c_ctx), f(c)[b]])
        m["cT"] = np.ascontiguousarray(cvec.reshape(2, 8, 128).transpose(2, 1, 0).reshape(128, 16))
        blk = np.zeros((128, 2, 24), np.float32)
        for tau in range(24):
            i = tau // 8 + 1
            vb = 1.0 if i <= 3 - j else 0.0
            blk[:, 0, tau] = 1.0 - vb
            blk[:, 1, tau] = vb
        m["blk"] = blk
        in_maps.append({k: np.ascontiguousarray(v, dtype=np.float32) for k, v in m.items()})
    key = (tuple(_dbg) if _dbg else None, _stop)
    if key not in _NC_CACHE:
        _NC_CACHE[key] = Builder(dbg=_dbg, stop=_stop).build()
    nc = _NC_CACHE[key]
    res = run_bass_kernel_spmd(nc, in_maps, core_ids=list(range(8)))
    R = res.results
    y_prompt = np.zeros((16, 256, 1024), np.float32)
    y_sample = np.zeros((2, 4096, 1024), np.float32)
    nk = np.zeros((16, 1, 256, 2, 128), np.float32)
    nv = np.zeros((16, 1, 256, 2, 128), np.float32)
    nC = np.zeros((16, 1, 2, 4, 256, 256), np.float32)
    nn = np.zeros((16, 1, 2, 4, 256), np.float32)
    nm = np.zeros((16, 1, 2, 4), np.float32)
    for r in range(8):
        b, j = r // 4, r % 4
        o = R[r]
        y_prompt[2 * r:2 * r + 2] = o["yp"].reshape(2, 256, 1024)
        y_sample[b, j * 1024:(j + 1) * 1024] = o["ys"]
        nk[2 * r:2 * r + 2, 0] = o["kc"].reshape(2, 256, 2, 128)
        nv[2 * r:2 * r + 2, 0] = o["vc"].reshape(2, 256, 2, 128)
        nC[2 * r:2 * r + 2, 0] = o["Cn"].reshape(2, 2, 4, 256, 256)
        nn[2 * r:2 * r + 2, 0] = o["nn"].reshape(2, 2, 4, 256)
        nm[2 * r:2 * r + 2, 0] = o["mn"].reshape(2, 2, 4)
    if _dbg:
        kernel._dbg_out = [R[r]["dbg"] for r in range(8)]
    return (y_prompt, y_sample, nk, nv, nC, nn, nm)
```

```python
import math
import numpy as np
import concourse.bass as bass
import concourse.mybir as mybir
from concourse.bass_utils import run_bass_kernel_spmd

F32 = mybir.dt.float32
BF16 = mybir.dt.bfloat16
AF = mybir.ActivationFunctionType
ALU = mybir.AluOpType
AX = mybir.AxisListType
PE, ACT, DVE, POOL, SP = "pe", "act", "dve", "pool", "sp"
N_DMA_SLOTS = 8
EPS = 1e-6
ALPHA = 2.0 ** 0.25
LN16 = math.log(16.0)
NEG = -30000.0


class Buf:
    __slots__ = ("name", "w", "r", "excl")

    def __init__(self, name, excl=False):
        self.name = name
        self.w = None
        self.r = {}
        self.excl = excl


class Sched:
    def __init__(self, nc):
        self.nc = nc
        self.lists = {e: [] for e in (PE, ACT, DVE, POOL, SP)}
        self.sig = {e: 0 for e in (PE, ACT, DVE, POOL)}
        self.pending = {e: False for e in (PE, ACT, DVE, POOL)}
        self.waited = {}
        self.dma_n = {}
        self.dma_rr = {SP: 0, POOL: 0, ACT: 0}
        self.out_tokens = []

    def _deps(self, reads, writes, eng=None):
        deps = {}

        def add(tok):
            if tok is None:
                return
            k, v = tok
            if deps.get(k, 0) < v:
                deps[k] = v
        for b in reads:
            add(b.w)
            if b.excl:
                for k, v in b.r.items():
                    if k != eng:
                        add((k, v))
        for b in writes:
            add(b.w)
            for k, v in b.r.items():
                add((k, v))
        return deps

    def _waits(self, eng, deps):
        waits = []
        for k, v in deps.items():
            if k == PE and eng == PE:
                continue
            if k in (PE, ACT, DVE, POOL):
                assert self.sig[k] >= v, f"dependency on unsignalled {k} instruction"
            if self.waited.get((eng, k), 0) >= v:
                continue
            self.waited[(eng, k)] = v
            waits.append((k, v))
        return waits

    def op(self, eng, fn, reads=(), writes=(), signal=True):
        deps = self._deps(reads, writes, eng)
        waits = self._waits(eng, deps)
        if signal:
            self.sig[eng] += 1
            tok = (eng, self.sig[eng])
            self.pending[eng] = False
        else:
            tok = (eng, self.sig[eng] + 1)
            self.pending[eng] = True
        self.lists[eng].append((fn, waits, (eng, 1) if signal else None))
        for b in reads:
            if b.r.get(eng, 0) < tok[1]:
                b.r[eng] = tok[1]
        for b in writes:
            b.w = tok
            b.r = {}
        return tok

    def dma(self, q, fn, reads=(), writes=(), is_output=False):
        slot = self.dma_rr[q]
        self.dma_rr[q] = (slot + 1) % N_DMA_SLOTS
        key = f"dma_{q}_{slot}"
        n = self.dma_n.get(key, 0)
        deps = self._deps(reads, writes)
        if n > 0 and deps.get(key, 0) < 16 * n:
            deps[key] = 16 * n
        waits = self._waits(q, deps)
        self.dma_n[key] = n + 1
        tok = (key, 16 * (n + 1))
        self.lists[q].append((fn, waits, (key, 16)))
        for b in reads:
            if b.r.get(key, 0) < tok[1]:
                b.r[key] = tok[1]
        for b in writes:
            b.w = tok
            b.r = {}
        if is_output:
            self.out_tokens.append(tok)
        return tok

    def barrier(self, mode="all"):
        for e in (PE, ACT, DVE, POOL):
            assert not self.pending[e]
        deps = {e: self.sig[e] for e in (PE, ACT, DVE, POOL) if self.sig[e] > 0}
        if mode != "nodma":
            for k, n in self.dma_n.items():
                deps[k] = 16 * n
        engs = (PE, ACT, DVE, POOL, SP)
        if mode == "nopool":
            engs = (PE, ACT, DVE, SP)
        if mode == "nosp":
            engs = (PE, ACT, DVE, POOL)
        for e in engs:
            d = dict(deps)
            waits = []
            for k, v in d.items():
                if self.waited.get((e, k), 0) >= v:
                    continue
                self.waited[(e, k)] = v
                waits.append((k, v))
            if waits:
                self.lists[e].append((None, waits, None))

    def finish(self):
        deps = {}
        for k, v in self.out_tokens:
            if deps.get(k, 0) < v:
                deps[k] = v
        waits = self._waits(SP, deps)
        self.lists[SP].append((None, waits, None))

    def emit(self):
        nc = self.nc
        keys = set()
        for e, lst in self.lists.items():
            for fn, waits, inc in lst:
                for k, v in waits:
                    keys.add(k)
                if inc is not None:
                    keys.add(inc[0])
        for e in (PE, ACT, DVE, POOL):
            assert not self.pending[e], f"{e} ends with unsignalled instruction"
        sems = {k: nc.alloc_semaphore(f"s_{k}") for k in sorted(keys)}
        lists = self.lists

        def run(engobj, lst):
            for fn, waits, inc in lst:
                for k, v in waits:
                    engobj.wait_ge(sems[k], v)
                if fn is None:
                    continue
                ins = fn(engobj)
                if inc is not None:
                    ins.then_inc(sems[inc[0]], inc[1])

        with nc.Block() as block:
            @block.tensor
            def _(e):
                run(e, lists[PE])

            @block.scalar
            def _(e):
                run(e, lists[ACT])

            @block.vector
            def _(e):
                run(e, lists[DVE])

            @block.gpsimd
            def _(e):
                run(e, lists[POOL])

            @block.sync
            def _(e):
                run(e, lists[SP])


class Group:
    def __init__(self, name, nseq, n_own, n_bnd, n_ctx, n_cache, mset, rope):
        self.name = name
        self.nseq = nseq
        self.n_own = n_own
        self.n_bnd = n_bnd
        self.n_ctx = n_ctx
        self.n_cache = n_cache
        self.mset = mset
        self.rope = rope
        self.is_p = (name == "P")
        self.T_own = n_own * 128
        self.nck = n_own


class StopBuild(Exception):
    pass


class Builder:
    def __init__(self, dbg=None, stop=None):
        self.stop = stop
        self.phase_id = 0
        self.nc = nc = bass.Bass("TRN2", target_bir_lowering=False)
        self.S = Sched(nc)
        self.dbg = dbg
        self.bufs = {}
        self.uid = 0
        self.base = ((nc.sbuf_base + 63) // 64) * 64
        self.lim = nc.sbuf_top
        self.ps = [nc.alloc_psum_tensor(f"psb{i}", [128, 512], F32) for i in range(8)]
        self.cur = self.base
        self.rr = 0

    def B(self, *key):
        b = self.bufs.get(key)
        if b is None:
            b = self.bufs[key] = Buf(str(key), excl=(key[0] == "ps"))
        return b

    def Bs(self, name, *ranges):
        out = [()]
        for r in ranges:
            r = [r] if isinstance(r, int) else list(r)
            out = [o + (i,) for o in out for i in r]
        return [self.B(name, *o) for o in out]

    def sb(self, name, shape, dtype, off=None):
        self.uid += 1
        esz = 4 if dtype == F32 else 2
        n = esz
        for s in shape[1:]:
            n *= s
        n = (n + 63) // 64 * 64
        if off is None:
            o = self.cur
            self.cur += n
        else:
            o = self.base + off
        assert o + n <= self.lim, f"SBUF overflow {name} {o + n - self.lim}"
        t = self.nc.alloc_sbuf_tensor_at(f"{name}_{self.uid}", list(shape), dtype, offset=o)
        return t

    def dram_in(self, name, shape):
        return self.nc.dram_tensor(name, list(shape), F32, kind="ExternalInput").ap()

    def dram_out(self, name, shape):
        return self.nc.dram_tensor(name, list(shape), F32, kind="ExternalOutput").ap()

    def op(self, eng, fn, reads=(), writes=(), signal=True):
        return self.S.op(eng, fn, reads, writes, signal)

    def load(self, out_ap, in_ap, writes, q=SP, reads=()):
        return self.S.dma(q, lambda e: e.dma_start(out=out_ap, in_=in_ap, allow_slow_non_contiguous=True), reads=reads, writes=writes)

    def store(self, out_ap, in_ap, reads):
        return self.S.dma(SP, lambda e: e.dma_start(out=out_ap, in_=in_ap, allow_slow_non_contiguous=True), reads=reads, is_output=True)

    def mm(self, out, lhsT, rhs, start, stop, reads, writes, signal=None, skip=False):
        if signal is None:
            signal = stop
        if skip:
            return self.S.op(PE, lambda e: e.matmul(out, lhsT=lhsT, rhs=rhs, start=start, stop=stop,
                                                    skip_group_check=True), reads, writes, signal)
        return self.S.op(PE, lambda e: e.matmul(out, lhsT=lhsT, rhs=rhs, start=start, stop=stop),
                         reads, writes, signal)

    def tr(self, out, in_, ident, reads, writes, signal=True):
        return self.S.op(PE, lambda e: e.transpose(out=out, in_=in_, identity=ident), reads, writes, signal)

    def act(self, out, in_, func, reads, writes, bias=None, scale=None, accum=None, eng=ACT):
        kw = {}
        if bias is not None:
            kw["bias"] = bias
        if scale is not None:
            kw["scale"] = scale
        if accum is not None:
            kw["accum_out"] = accum
        return self.S.op(ACT, lambda e: e.activation(out=out, in_=in_, func=func, **kw), reads, writes)

    def tt(self, out, in0, in1, op, reads, writes, eng=DVE):
        return self.S.op(eng, lambda e: e.tensor_tensor(out=out, in0=in0, in1=in1, op=op), reads, writes)

    def ts(self, out, in0, s1, s2, op0, op1, reads, writes, eng=DVE):
        if s2 is None:
            return self.S.op(eng, lambda e: e.tensor_scalar(out=out, in0=in0, scalar1=s1, scalar2=None, op0=op0),
                             reads, writes)
        return self.S.op(eng, lambda e: e.tensor_scalar(out=out, in0=in0, scalar1=s1, scalar2=s2, op0=op0, op1=op1),
                         reads, writes)

    def stt(self, out, in0, scalar, in1, op0, op1, reads, writes, eng=DVE):
        return self.S.op(eng, lambda e: e.scalar_tensor_tensor(out=out, in0=in0, scalar=scalar, in1=in1,
                                                               op0=op0, op1=op1), reads, writes)

    def cp(self, out, in_, reads, writes, eng=DVE):
        if eng == ACT:
            return self.S.op(ACT, lambda e: e.activation(out=out, in_=in_, func=AF.Copy), reads, writes)
        return self.S.op(eng, lambda e: e.tensor_copy(out=out, in_=in_), reads, writes)

    def red(self, out, in_, op, reads, writes, eng=DVE):
        return self.S.op(eng, lambda e: e.tensor_reduce(out=out, in_=in_, axis=AX.X, op=op), reads, writes)

    def memset(self, ap, val, writes, eng=DVE):
        return self.S.op(eng, lambda e: e.memset(ap, val), (), writes)

    def alt(self):
        self.rr += 1
        return ACT if (self.rr & 1) else DVE

    def evac_cast(self, out, in_, reads, writes, eng=None):
        eng = eng or self.alt()
        return self.cp(out, in_, reads, writes, eng=eng)

    def build(self):
        nc = self.nc
        I = {}
        I["xp"] = self.dram_in("xp", [512, 1024])
        I["xs"] = self.dram_in("xs", [4096, 1024])
        I["rope"] = self.dram_in("rope", [4096, 128])
        I["ck"] = self.dram_in("ck", [256, 256])
        I["cv"] = self.dram_in("cv", [256, 256])
        I["sC"] = self.dram_in("sC", [8, 256, 256])
        I["sn"] = self.dram_in("sn", [8, 256])
        I["sm"] = self.dram_in("sm", [8])
        I["cT"] = self.dram_in("cT", [128, 16])
        I["blk"] = self.dram_in("blk", [128, 2, 24])
        I["cm"] = self.dram_in("cm", [128, 6, 128])
        I["wmod"] = self.dram_in("wmod", [12, 128, 8 * 512])
        I["bmod"] = self.dram_in("bmod", [6144])
        I["wg"] = self.dram_in("wg", [128, 8 * 16])
        I["bg"] = self.dram_in("bg", [16])
        I["wml"] = self.dram_in("wml", [4, 4, 128, 8 * 256])
        I["wqa"] = self.dram_in("wqa", [2, 128, 8 * 512])
        I["wkva"] = self.dram_in("wkva", [128, 8 * 512])
        I["w5"] = self.dram_in("w5", [8, 128, 4 * 8 * 128])
        I["wout"] = self.dram_in("wout", [2, 128, 8 * 512])
        I["wup"] = self.dram_in("wup", [16, 128, 8 * 256])
        I["wdn"] = self.dram_in("wdn", [4, 128, 8 * 1024])
        I["mng"] = self.dram_in("mng", [1024])
        I["qg"] = self.dram_in("qg", [128])
        I["kg"] = self.dram_in("kg", [128])
        for nme in ("ln1g", "ln1b", "ln2g", "ln2b"):
            I[nme] = self.dram_in(nme, [1024])
        O = {}
        O["yp"] = self.dram_out("yp", [512, 1024])
        O["ys"] = self.dram_out("ys", [1024, 1024])
        O["kc"] = self.dram_out("kc", [512, 256])
        O["vc"] = self.dram_out("vc", [512, 256])
        O["Cn"] = self.dram_out("Cn", [2, 8, 256, 256])
        O["nn"] = self.dram_out("nn", [2, 8, 256])
        O["mn"] = self.dram_out("mn", [2, 8])
        if self.dbg:
            O["dbg"] = self.dram_out("dbg", list(self.dbg))
        self.I, self.O = I, O

        self.cm = self.sb("cm", [128, 6, 128], F32)
        self.identb = self.sb("identb", [128, 128], BF16)
        self.onesb = self.sb("onesb", [128, 2], BF16)
        self.onesb128 = self.sb("onesb128", [128, 128], BF16)
        self.COL = self.sb("COL", [128, 2, 4, 8], F32)
        self.G1 = self.sb("G1", [128, 2, 1024], F32)
        self.G2 = self.sb("G2", [128, 2, 1024], F32)
        self.QG = self.sb("QG", [128, 128], F32)
        self.KG = self.sb("KG", [128, 128], F32)
        self.BG = self.sb("BG", [128, 16], F32)
        self.wg = self.sb("wg", [128, 8, 16], BF16)
        self.arena0 = self.cur - self.base
        bc = self.B("consts")
        self.load(self.cm[:], I["cm"], [bc])
        self.load(self.QG[:], I["qg"].partition_broadcast(128), [bc])
        self.load(self.KG[:], I["kg"].partition_broadcast(128), [bc])
        self.load(self.BG[:], I["bg"].partition_broadcast(128), [bc])
        self.load(self.wg[:], I["wg"].rearrange("p (c n) -> p c n", c=8), [bc], q=POOL)
        self.cp(self.identb[:], self.cm[:, 0, :], [bc], [self.B("identb")])
        self.memset(self.onesb[:], 1.0, [self.B("onesb")])
        self.memset(self.onesb128[:], 1.0, [self.B("onesb128")])
        self.ident = self.cm[:, 0, :]

        try:
            self.build_body()
        except StopBuild:
            pass
        self.S.finish()
        self.S.emit()
        return nc

    def chk(self, label):
        self.phase_id += 1
        if self.stop is not None and self.phase_id >= self.stop:
            print("STOP at", self.phase_id, label)
            raise StopBuild()

    def build_body(self):
        self.phase0()
        import os
        if os.environ.get("DBG_BAR"):
            self.S.barrier(os.environ.get("DBG_BAR"))
        self.chk("phase0")
        gP = Group("P", nseq=2, n_own=2, n_bnd=2, n_ctx=2, n_cache=0, mset=0, rope=False)
        gS = Group("S", nseq=1, n_own=8, n_bnd=24, n_ctx=32, n_cache=2, mset=1, rope=True)
        for g in (gP, gS):
            self.S.barrier()
            self.run_group(g)

    def phase0(self):
        I = self.I
        self.cur = self.base + self.arena0
        MOD = self.sb("MOD", [128, 2, 6144], F32)
        bmod = self.sb("bmodr", [128, 6144], F32)
        cT = self.sb("cT", [128, 8, 2], F32)
        sg = self.sb("sg", [128, 8, 2], F32)
        srep = self.sb("srep", [128, 8, 2, 128], BF16)
        wsl = [self.sb(f"wmods{i}", [128, 8, 512], BF16) for i in range(4)]
        wstage = [self.sb(f"wstage{i}", [128, 8, 512], F32) for i in range(2)]
        l1g = self.sb("l1g", [128, 1024], F32)
        l1b = self.sb("l1b", [128, 1024], F32)
        tmp = self.sb("tmp0", [128, 1024], F32)
        tmp2 = self.sb("tmp02", [128, 8, 128], F32)
        b0 = self.B("p0")
        self.load(bmod[:], I["bmod"].partition_broadcast(128), [self.B("bmod")])
        self.load(cT[:], I["cT"].rearrange("p (c s) -> p c s", s=2), [self.B("cT")])
        self.load(l1g[:], I["ln1g"].partition_broadcast(128), [self.B("l1g")])
        self.load(l1b[:], I["ln1b"].partition_broadcast(128), [self.B("l1b")])
        self.act(sg[:], cT[:], AF.Sigmoid, [self.B("cT")], [self.B("sg")])
        self.tt(sg[:], sg[:], cT[:], ALU.mult, [self.B("cT"), self.B("sg")], [self.B("sg")])
        self.cp(srep[:].rearrange("p c s n -> p (c s) n"),
                sg[:].rearrange("p c s -> p (c s)").unsqueeze(2).to_broadcast([128, 16, 128]),
                [self.B("sg")], [self.B("srep")])
        for j in range(12):
            w = wsl[j % 4]
            bw = self.B("wmods", j % 4)
            if j % 2 == 0:
                self.load(w[:], I["wmod"][j].rearrange("p (c n) -> p c n", c=8), [bw], q=POOL)
            else:
                stg = wstage[(j // 2) % 2]
                bst = self.B("wstage", (j // 2) % 2)
                self.load(stg[:], I["wmod"][j].rearrange("p (c n) -> p c n", c=8), [bst], q=SP)
                self.cp(w[:], stg[:], [bst], [bw])
            for s in range(2):
                pb = self.B("ps", (2 * j + s) % 4)
                pt = self.ps[(2 * j + s) % 4]
                for c in range(8):
                    self.mm(pt[:], srep[:, c, s, :], w[:, c, :], c == 0, c == 7, [bw, self.B("srep")], [pb])
                self.tt(MOD[:, s, j * 512:(j + 1) * 512], pt[:], bmod[:, j * 512:(j + 1) * 512], ALU.add,
                        [pb, self.B("bmod")], [self.B("MOD", s, j)])
        for s in range(2):
            def row(k):
                return MOD[:, s, k * 1024:(k + 1) * 1024]

            def rb(k):
                return [self.B("MOD", s, 2 * k), self.B("MOD", s, 2 * k + 1)]
            self.cp(self.G1[:, s, :], row(2), rb(2), [self.B("G1", s)], eng=ACT)
            self.cp(self.G2[:, s, :], row(5), rb(5), [self.B("G2", s)], eng=ACT)
            bt, bt2 = self.B("p0tmp"), self.B("p0tmp2")
            identB = self.cm[:, 0, :].unsqueeze(1).to_broadcast([128, 8, 128])

            def diag(dst, src_ap, reads):
                self.tt(tmp2[:], src_ap.rearrange("p (c n) -> p c n", c=8), identB, ALU.mult,
                        reads + [self.B("consts")], [bt2])
                self.red(dst, tmp2[:], ALU.add, [bt2], [self.B("COL", s)])
            self.ts(tmp[:], row(1), 1.0, None, ALU.add, None, rb(1), [bt])
            diag(self.COL[:, s, 0, :], tmp[:], [bt])
            diag(self.COL[:, s, 1, :], row(0), rb(0))
            self.ts(tmp[:], row(4), 1.0, None, ALU.add, None, rb(4), [bt])
            self.tt(row(4), tmp[:], l1g[:], ALU.mult, [bt, self.B("l1g")], rb(4))
            diag(self.COL[:, s, 2, :], row(4), rb(4))
            self.tt(tmp[:], tmp[:], l1b[:], ALU.mult, [bt, self.B("l1b")], [bt])
            self.tt(tmp[:], tmp[:], row(3), ALU.add, [bt] + rb(3), [bt])
            diag(self.COL[:, s, 3, :], tmp[:], [bt])

    def run_group(self, g):
        I, O = self.I, self.O
        KB = 1024
        a0 = self.arena0
        nseq, n_own, nck, T_own = g.nseq, g.n_own, g.nck, g.T_own
        TT = nseq * T_own
        T_ctx = g.n_ctx * 128
        T_keys = (g.n_ctx + g.n_cache) * 128
        nkt = g.n_ctx + g.n_cache
        s = g.mset
        hT = self.sb("hT", [128, 8, TT], BF16, off=a0)
        hmT = self.sb("hmT", [128, 8, TT], BF16, off=a0 + 16 * KB)
        haT = self.sb("haT", [128, 8, TT], BF16, off=a0 + 32 * KB)
        self.cur = self.base + a0 + 48 * KB
        KaT = self.sb("KaT", [128, nseq, 2, T_keys], BF16)
        Va = self.sb("Va", [128, nseq, nkt, 2, 129], BF16)
        G128 = self.sb("G128", [128, nseq, g.n_bnd, 16], F32)
        G64 = self.sb("G64", [128, nseq, nck, 16], F32)
        W8 = self.sb("W8", [128, nseq, g.n_bnd, 8], F32)
        U64 = self.sb("U64", [128, nseq, nck, 8], F32)
        E64 = self.sb("E64", [128, nseq, nck, 8], F32)
        Z64 = self.sb("Z64", [128, nseq, nck, 8], F32)
        DEC = self.sb("DEC", [128, nseq, nck, 8], F32)
        SCI = self.sb("SCI", [128, 8], F32)
        C0 = self.sb("C0", [128, nseq, 8, 2, 257], F32)
        mark_common = self.cur

        def own_cols(q, t):
            o = q * T_own + t * 128
            return slice(o, o + 128)

        hTo = self.sb("hTo", [128, 8, nseq * (g.n_ctx - n_own) * 128 if not g.is_p else 16], BF16) if not g.is_p else None
        mark2 = self.cur
        xin = [self.sb(f"xin{i}", [128, 1024], F32) for i in range(2)]
        xsrc = I["xp"] if g.is_p else I["xs"]
        sc1 = self.COL[:, s, 0, :]
        sh1 = self.COL[:, s, 1, :]
        bcol = self.B("COL", s)
        cnt = 0
        gcnt = 0
        PG = self.ps[7]
        hview = {}
        import os
        DBG_NT = int(os.environ.get("DBG_NT", "999"))
        DBG_NOG = int(os.environ.get("DBG_NOG", "0"))
        for q in range(nseq):
            for t in range(min(g.n_ctx, DBG_NT)):
                xi = xin[cnt % 2]
                bx = self.B("xin", cnt % 2)
                row0 = (q * g.n_ctx + t) * 128
                self.load(xi[:], xsrc[row0:row0 + 128, :], [bx])
                pbank = [self.ps[(cnt % 2) * 2], self.ps[(cnt % 2) * 2 + 1]]
                pb = [self.B("ps", (cnt % 2) * 2), self.B("ps", (cnt % 2) * 2 + 1)]
                if t < n_own:
                    dst, off = hT, q * T_own + t * 128
                else:
                    dst, off = hTo, (q * (g.n_ctx - n_own) + (t - n_own)) * 128
                hview[(q, t)] = (dst, off)
                for c in range(8):
                    self.tr(pbank[c // 4][:, (c % 4) * 128:(c % 4 + 1) * 128], xi[:, c * 128:(c + 1) * 128],
                            self.ident, [bx, self.B("consts")], [pb[c // 4]], signal=(c % 4 == 3))
                for c in range(8):
                    src = pbank[c // 4][:, (c % 4) * 128:(c % 4 + 1) * 128]
                    o_ap = dst[:, c, off:off + 128]
                    bh = self.B("hT", g.name, q, t, c)
                    if c // 4 == 0 or os.environ.get("DBG_ACTONLY"):
                        self.S.op(ACT, lambda e, o_ap=o_ap, src=src, c=c: e.activation(
                            out=o_ap, in_=src, func=AF.Identity, scale=sc1[:, c:c + 1], bias=sh1[:, c:c + 1]),
                            [pb[c // 4], bcol], [bh])
                    else:
                        self.stt(o_ap, src, sc1[:, c:c + 1], sh1[:, c:c + 1].to_broadcast([128, 128]), ALU.mult, ALU.add,
                                 [pb[c // 4], bcol], [bh])
                cnt += 1
                bidx = t if g.is_p else (t - n_own)
                if DBG_NOG:
                    continue
                if 0 <= bidx < g.n_bnd:
                    sl = 0
                    PG = self.ps[4 + gcnt % 4]
                    pgb = self.B("ps", 4 + gcnt % 4)
                    gcnt += 1
                    for c in range(8):
                        self.mm(PG[:, sl * 16:(sl + 1) * 16], dst[:, c, off:off + 128], self.wg[:, c, :],
                                c == 0, c == 7, [self.B("hT", g.name, q, t, c), self.B("consts")], [pgb])
                    self.tt(G128[:, q, bidx, :], PG[:, sl * 16:(sl + 1) * 16], self.BG[:], ALU.add,
                            [pgb, self.B("consts")], [self.B("G128", q)])
                if t < n_own:
                    sl = 0
                    PG = self.ps[4 + gcnt % 4]
                    pgb = self.B("ps", 4 + gcnt % 4)
                    gcnt += 1
                    for c in range(8):
                        self.mm(PG[:, 0:16], dst[:, c, off:off + 128], self.wg[:, c, :], c == 0, c == 7,
                                [self.B("hT", g.name, q, t, c), self.B("consts")], [pgb])
                    self.tt(G64[:, q, t, :], PG[:, 0:16], self.BG[:], ALU.add,
                            [pgb, self.B("consts")], [self.B("G64", q)])

        def hT_bufs(q, t):
            return [self.B("hT", g.name, q, t, c) for c in range(8)]

        self.chk(g.name + " phase1")
        reuse = not g.is_p
        if reuse:
            self.S.barrier()
            self.cur = mark2
        for q in range(nseq):
            self.gates(g, q, G128, G64, W8, U64, E64, Z64, DEC, SCI)
        if reuse:
            self.S.barrier()
            self.cur = mark2

        self.chk(g.name + " gates")
        self.kv_proj(g, hview, hT_bufs, KaT, Va)
        if reuse:
            self.S.barrier()
            self.cur = mark2

        self.chk(g.name + " kv")
        self.boundary(g, hview, hT_bufs, W8, SCI, C0)
        self.S.barrier()
        self.cur = mark_common

        self.chk(g.name + " boundary")
        self.mlstm_own(g, hT, hT_bufs, U64, E64, Z64, DEC, C0, hmT)
        self.S.barrier()
        self.cur = mark_common

        self.chk(g.name + " mlstm")
        self.attention(g, hT, hT_bufs, KaT, Va, haT)

        self.chk(g.name + " attention")
        self.S.barrier()
        self.cur = self.base + a0 + 48 * KB
        self.post(g, hT, hmT, haT)
        self.chk(g.name + " post")

    def gates(self, g, q, G128, G64, W8, U64, E64, Z64, DEC, SCI):
        I, O = self.I, self.O
        n = g.n_bnd
        nck = g.nck
        cmB = self.B("consts")
        Lst, Ust, ONES = self.cm[:, 3, :], self.cm[:, 4, :], self.cm[:, 5, :]
        Uf, Ub = self.cm[:, 1, :], self.cm[:, 2, :]
        NLF = self.sb("NLF", [128, n, 8], F32)
        TTs = self.sb("TTs", [128, n, 8], F32)
        OFF = self.sb("OFF", [128, n, 8], F32)
        X = self.sb("X", [128, n, 8], F32)
        bG = self.B("G128", q)
        bN, bT, bO, bX = self.B("NLF", q), self.B("TTs", q), self.B("OFF", q), self.B("X", q)
        self.act(NLF[:], G128[:, q, :, 8:16], AF.Exp, [bG], [bN], scale=-1.0)
        self.act(NLF[:], NLF[:], AF.Ln, [bN], [bN], bias=1.0)
        pt = self.ps[4]
        pb = self.B("ps", 4)
        ecf = pt[:, 0:n * 4].rearrange("p (t h) -> p t h", h=4)
        ecb = pt[:, n * 4:n * 8].rearrange("p (t h) -> p t h", h=4)
        ttv = pt[:, n * 8:n * 16].rearrange("p (t h) -> p t h", h=8)
        self.mm(ecf, Ust, NLF[:, :, 0:4], True, True, [bN, cmB], [pb])
        self.mm(ecb, Lst, NLF[:, :, 4:8], True, True, [bN, cmB], [pb])
        self.mm(ttv, ONES, NLF[:], True, True, [bN, cmB], [pb])
        self.cp(TTs[:], ttv, [pb], [bT])
        self.memset(OFF[:], 0.0, [bO])
        for t in range(n - 2, -1, -1):
            self.tt(OFF[:, t, 0:4], OFF[:, t + 1, 0:4], TTs[:, t + 1, 0:4], ALU.add, [bO, bT], [bO])
        for t in range(1, n):
            self.tt(OFF[:, t, 4:8], OFF[:, t - 1, 4:8], TTs[:, t - 1, 4:8], ALU.add, [bO, bT], [bO])
        self.tt(X[:, :, 0:4], G128[:, q, :, 0:4], ecf, ALU.subtract, [bG, pb], [bX])
        self.tt(X[:, :, 4:8], G128[:, q, :, 4:8], ecb, ALU.subtract, [bG, pb], [bX])
        self.tt(X[:], X[:], OFF[:], ALU.subtract, [bX, bO], [bX])
        if not g.is_p:
            blk = self.sb("blk", [128, 2, 24], F32)
            bB = self.B("blk")
            self.load(blk[:], I["blk"], [bB])
            bW = self.B("W8", q)
            self.act(W8[:, q, :, :], X[:], AF.Exp, [bX], [bW], bias=-LN16)
            self.tt(W8[:, q, :, 0:4], W8[:, q, :, 0:4], blk[:, 0, :].unsqueeze(2).to_broadcast([128, n, 4]), ALU.mult,
                    [bW, bB], [bW])
            self.tt(W8[:, q, :, 4:8], W8[:, q, :, 4:8], blk[:, 1, :].unsqueeze(2).to_broadcast([128, n, 4]), ALU.mult,
                    [bW, bB], [bW])
            tmpd = self.sb("tmpd", [128, 8, n], F32)
            sm = self.sb("smr", [128, 8], F32)
            bD = self.B("tmpd")
            self.load(sm[:], I["sm"].partition_broadcast(128), [self.B("smr")])
            self.tt(tmpd[:, 0:4, :], TTs[:, :, 0:4].rearrange("p t h -> p h t"),
                    blk[:, 0, :].unsqueeze(1).to_broadcast([128, 4, n]), ALU.mult, [bT, bB], [bD])
            self.tt(tmpd[:, 4:8, :], TTs[:, :, 4:8].rearrange("p t h -> p h t"),
                    blk[:, 1, :].unsqueeze(1).to_broadcast([128, 4, n]), ALU.mult, [bT, bB], [bD])
            self.red(SCI[:], tmpd[:], ALU.add, [bD], [self.B("SCI")])
            self.tt(SCI[:], sm[:], SCI[:], ALU.subtract, [self.B("SCI"), self.B("smr")], [self.B("SCI")])
            self.act(SCI[:], SCI[:], AF.Exp, [self.B("SCI")], [self.B("SCI")])
        else:
            assert n == 2
            pt2 = self.ps[5]
            pb2 = self.B("ps", 5)
            mx = self.sb("mx", [16, 1], F32)
            mxb = self.sb("mxb", [16, 128], F32)
            MF = self.sb("MF", [128, 8], F32)
            GT = self.sb("GT", [128, 8], F32)
            bM = self.B("mfin", q)
            self.tr(pt2[0:16, 0:128], X[:].rearrange("p t h -> p (t h)"), self.ident, [bX, cmB], [pb2])
            self.red(mx[:], pt2[0:16, 0:128], ALU.max, [pb2], [bM])
            self.cp(mxb[:], mx[:].to_broadcast([16, 128]), [bM], [bM])
            self.tr(pt2[:, 128:144], mxb[:], self.cm[0:16, 0, 0:16], [bM, cmB], [pb2])
            MX = self.sb("MX", [128, 16], F32)
            self.cp(MX[:], pt2[:, 128:144], [pb2], [bM])
            self.tt(MF[:], MX[:, 0:8], MX[:, 8:16], ALU.max, [bM], [bM])
            self.tt(GT[:], TTs[:, 0, :], TTs[:, 1, :], ALU.add, [bT], [bM])
            self.stt(MF[:], GT[:], -1.0, MF[:], ALU.mult, ALU.max, [bM], [bM])
            self.tt(X[:], X[:], MF[:].unsqueeze(1).to_broadcast([128, 2, 8]), ALU.subtract, [bX, bM], [bX])
            self.act(W8[:, q, :, :], X[:], AF.Exp, [bX], [self.B("W8", q)], bias=-LN16)
            self.store(O["mn"][q:q + 1, :], MF[0:1, :], [bM])
        N64 = self.sb("N64", [128, nck, 8], F32)
        A64 = self.sb("A64", [128, nck, 8], F32)
        bG6 = self.B("G64", q)
        bN6, bA6 = self.B("N64", q), self.B("A64", q)
        self.act(N64[:], G64[:, q, :, 8:16], AF.Exp, [bG6], [bN6], scale=-1.0)
        self.act(N64[:], N64[:], AF.Ln, [bN6], [bN6], bias=1.0)
        pt3 = self.ps[6]
        pb3 = self.B("ps", 6)
        bcf = pt3[:, 0:nck * 4].rearrange("p (t h) -> p t h", h=4)
        bcb = pt3[:, nck * 4:nck * 8].rearrange("p (t h) -> p t h", h=4)
        t64 = pt3[:, nck * 8:nck * 16].rearrange("p (t h) -> p t h", h=8)
        t128 = pt3[:, nck * 16:nck * 24].rearrange("p (t h) -> p t h", h=8)
        self.mm(bcf, Uf, N64[:, :, 0:4], True, True, [bN6, cmB], [pb3])
        self.mm(bcb, Ub, N64[:, :, 4:8], True, True, [bN6, cmB], [pb3])
        self.mm(t64, ONES, N64[:], True, True, [bN6, cmB], [pb3])
        self.mm(t128, ONES, N64[:], True, True, [bN6, cmB], [pb3])
        self.tt(A64[:, :, 0:4], G64[:, q, :, 0:4], bcf, ALU.add, [bG6, pb3], [bA6])
        self.tt(A64[:, :, 4:8], G64[:, q, :, 4:8], bcb, ALU.add, [bG6, pb3], [bA6])
        self.act(U64[:, q, :, :], A64[:], AF.Exp, [bA6], [self.B("U64", q)], bias=-LN16)
        self.tt(A64[:], A64[:], t64, ALU.subtract, [bA6, pb3], [bA6])
        self.act(Z64[:, q, :, :], A64[:], AF.Exp, [bA6], [self.B("Z64", q)], bias=-LN16)
        self.act(E64[:, q, :, 0:4], bcf, AF.Exp, [pb3], [self.B("E64", q)], scale=-1.0)
        self.act(E64[:, q, :, 4:8], bcb, AF.Exp, [pb3], [self.B("E64", q)], scale=-1.0)
        self.act(DEC[:, q, :, :], t128, AF.Exp, [pb3], [self.B("DEC", q)], scale=-1.0)

    def interleave(self, gens, width, stagger=0):
        active = []
        it = iter(gens)
        since = stagger
        done = False
        while True:
            if not done and len(active) < width and since >= stagger:
                try:
                    active.append(next(it))
                    since = 0
                except StopIteration:
                    done = True
            if not active:
                if done:
                    break
                since = stagger
                continue
            since += 1
            for gq in list(active):
                try:
                    next(gq)
                except StopIteration:
                    active.remove(gq)

    def rope_apply(self, xt, H, rt, bR, bx, tmp1, tmp2, btmp):
        cosv = rt[:, 0:64].rearrange("p (a f) -> p a f", a=2).unsqueeze(1).to_broadcast([128, H, 2, 32])
        sinv = rt[:, 64:128].rearrange("p (a f) -> p a f", a=2).unsqueeze(1).to_broadcast([128, H, 2, 32])
        xv = xt.rearrange("p h (a k f) -> p h a k f", a=2, k=2)
        t1 = tmp1.rearrange("p h (a k f) -> p h a k f", a=2, k=2)
        t2 = tmp2.rearrange("p h (a k f) -> p h a k f", a=2, k=2)
        for k in range(2):
            self.tt(t1[:, :, :, k, :], xv[:, :, :, k, :], cosv, ALU.mult, [bx, bR], [btmp])
            yield
            self.tt(t2[:, :, :, k, :], xv[:, :, :, 1 - k, :], sinv, ALU.mult, [bx, bR], [btmp])
            yield
        self.tt(xv[:, :, :, 0, :], t1[:, :, :, 0, :], t2[:, :, :, 0, :], ALU.subtract, [btmp], [bx])
        yield
        self.tt(xv[:, :, :, 1, :], t1[:, :, :, 1, :], t2[:, :, :, 1, :], ALU.add, [btmp], [bx])
        yield

    def rms_heads(self, src_ps, H, gain, dst, pbs, bdst, scr, ss, bss):
        for h in range(H):
            self.S.op(ACT, lambda e, h=h: e.activation(out=scr[:, h, :], in_=src_ps[:, h, :], func=AF.Square,
                                                       accum_out=ss[:, h:h + 1]), pbs, [bss])
        yield
        self.act(ss[:, 0:H], ss[:, 0:H], AF.Sqrt, [bss], [bss], bias=EPS, scale=1.0 / 128.0)
        yield
        self.S.op(DVE, lambda e: e.reciprocal(out=ss[:, 0:H], in_=ss[:, 0:H]), [bss], [bss])
        yield
        self.tt(dst, src_ps, ss[:, 0:H].unsqueeze(2).to_broadcast([128, H, 128]), ALU.mult, pbs + [bss], [bdst])
        yield
        self.tt(dst, dst, gain.unsqueeze(1).to_broadcast([128, H, 128]), ALU.mult, [bdst, self.B("consts")], [bdst])
        yield

    def kv_proj(self, g, hview, hT_bufs, KaT, Va):
        I, O = self.I, self.O
        w = self.sb("wkva", [128, 8, 512], BF16)
        bw = self.B("wkva")
        self.load(w[:], I["wkva"].rearrange("p (c n) -> p c n", c=8), [bw], q=POOL)
        NS = 3
        kn = [self.sb(f"kn{i}", [128, 2, 128], F32) for i in range(NS)]
        vf = [self.sb(f"vf{i}", [128, 256], F32) for i in range(NS)]
        kb = [self.sb(f"kb{i}", [128, 2, 128], BF16) for i in range(NS)]
        scr = [self.sb(f"kscr{i}", [128, 2, 128], F32) for i in range(NS)]
        ss = [self.sb(f"kss{i}", [128, 2], F32) for i in range(NS)]
        t1 = [self.sb(f"rt1{i}", [128, 2, 128], F32) for i in range(NS)]
        t2 = [self.sb(f"rt2{i}", [128, 2, 128], F32) for i in range(NS)]
        rts = [self.sb(f"rts{i}", [128, 128], F32) for i in range(NS)]
        self.memset(Va[:, :, :, :, 128:129], 1.0, [self.B("Vaones")])

        def tile_job(q, t, i2):
            dst, off = hview[(q, t)]
            pt = self.ps[i2]
            pb = self.B("ps", i2)
            if g.rope:
                self.load(rts[i2][:], I["rope"][t * 128:(t + 1) * 128, :], [self.B("rts", i2)])
            for c in range(8):
                self.mm(pt[:], dst[:, c, off:off + 128], w[:, c, :], c == 0, c == 7, hT_bufs(q, t) + [bw], [pb])
            yield
            kps = pt[:, 0:256].rearrange("p (h d) -> p h d", h=2)
            bk, bv, bs = self.B("kn", i2), self.B("vf", i2), self.B("kss", i2)
            yield from self.rms_heads(kps, 2, self.KG[:], kn[i2][:], [pb], bk, scr[i2], ss[i2], bs)
            if g.is_p:
                self.cp(vf[i2][:], pt[:, 256:512], [pb], [bv], eng=ACT)
                r0 = q * 256 + t * 128
                self.store(O["kc"][r0:r0 + 128, :], kn[i2][:].rearrange("p h d -> p (h d)"), [bk])
                self.store(O["vc"][r0:r0 + 128, :], vf[i2][:], [bv])
            self.cp(Va[:, q, t, :, 0:128], pt[:, 256:512].rearrange("p (h d) -> p h d", h=2),
                    [pb], [self.B("Va", q, t), self.B("Vaones")], eng=ACT)
            yield
            if g.rope:
                yield from self.rope_apply(kn[i2][:], 2, rts[i2], self.B("rts", i2), bk, t1[i2][:], t2[i2][:],
                                           self.B("rtmp", i2))
            bkb = self.B("kb", i2)
            self.cp(kb[i2][:], kn[i2][:], [bk], [bkb])
            yield
            ptr = self.ps[3 + i2]
            pbt = self.B("ps", 3 + i2)
            ptv = ptr[:].bitcast(BF16)
            for h in range(2):
                self.tr(ptv[:, h * 128:(h + 1) * 128], kb[i2][:, h, :], self.identb[:],
                        [bkb, self.B("identb")], [pbt], signal=(h == 1))
            yield
            self.cp(KaT[:, q, :, t * 128:(t + 1) * 128], ptv[:, 0:256].rearrange("p (h n) -> p h n", h=2),
                    [pbt], [self.B("KaT", q, t)])
            yield

        def jobs():
            cnt = 0
            for q in range(g.nseq):
                for t in range(g.n_ctx):
                    yield tile_job(q, t, cnt % NS)
                    cnt += 1
        self.interleave(jobs(), NS, stagger=5)
        for q in range(g.nseq):
            if g.n_cache:
                ckf = self.sb("ckf", [128, 2, 256], F32)
                cvf = self.sb("cvf", [128, 2, 256], F32)
                ckb = self.sb("ckb", [128, 2, 256], BF16)
                self.load(ckf[:], I["ck"].rearrange("(t p) n -> p t n", p=128), [self.B("ckf")])
                self.load(cvf[:], I["cv"].rearrange("(t p) n -> p t n", p=128), [self.B("cvf")])
                self.cp(ckb[:], ckf[:], [self.B("ckf")], [self.B("ckb")])
                for tc in range(g.n_cache):
                    t = g.n_ctx + tc
                    self.cp(Va[:, q, t, :, 0:128], cvf[:, tc, :].rearrange("p (h d) -> p h d", h=2),
                            [self.B("cvf")], [self.B("Va", q, t), self.B("Vaones")])
                    ptr = self.ps[6 + tc % 2]
                    pbt = self.B("ps", 6 + tc % 2)
                    ptv = ptr[:].bitcast(BF16)
                    for h in range(2):
                        self.tr(ptv[:, h * 128:(h + 1) * 128], ckb[:, tc, h * 128:(h + 1) * 128], self.identb[:],
                                [self.B("ckb"), self.B("identb")], [pbt], signal=(h == 1))
                    self.cp(KaT[:, q, :, t * 128:(t + 1) * 128], ptv[:, 0:256].rearrange("p (h n) -> p h n", h=2),
                            [pbt], [self.B("KaT", q, t)])

    def boundary(self, g, hview, hT_bufs, W8, SCI, C0):
        I, O = self.I, self.O
        n = g.n_bnd
        n_own = g.n_own
        wk = [self.sb(f"bwk{i}", [128, 8, 256], BF16) for i in range(2)]
        wv = [self.sb(f"bwv{i}", [128, 8, 256], BF16) for i in range(2)]
        vb = [self.sb(f"bvb{i}", [128, 256], BF16) for i in range(3)]
        kp = [[self.sb(f"bkp{d}{i}", [128, 256], BF16) for i in range(3)] for d in range(2)]
        nsb = self.sb("bnsb", [128, 2, 2], F32)
        cinit = [self.sb(f"cinit{i}", [128, 2, 256], F32) for i in range(2)]
        ninit = [self.sb(f"ninit{i}", [128, 2], F32) for i in range(2)]
        cout = [self.sb(f"cout{i}", [128, 2, 256], F32) for i in range(2)]
        NS = 3
        ptb = [0, 1, 5]
        tiles = []
        for h in range(4):
            for q in range(g.nseq):
                for bi in range(n):
                    tiles.append((h, q, bi))
        state = {"ci": 0}

        def front(k):
            h, q, bi = tiles[k]
            i2 = h % 2
            bwk, bwv = self.B("bwk", i2), self.B("bwv", i2)
            if q == 0 and bi == 0:
                self.load(wk[i2][:], I["wml"][h, 1].rearrange("p (c n) -> p c n", c=8), [bwk], q=POOL)
                self.load(wv[i2][:], I["wml"][h, 2].rearrange("p (c n) -> p c n", c=8), [bwv], q=POOL)
            t = bi if g.is_p else n_own + bi
            dst, off = hview[(q, t)]
            j2 = k % NS
            pt = self.ps[ptb[j2]]
            pb = self.B("ps", ptb[j2])
            for c in range(8):
                self.mm(pt[:, 0:256], dst[:, c, off:off + 128], wk[i2][:, c, :], c == 0, c == 7,
                        hT_bufs(q, t) + [bwk], [pb], signal=False)
            for c in range(8):
                self.mm(pt[:, 256:512], dst[:, c, off:off + 128], wv[i2][:, c, :], c == 0, c == 7,
                        hT_bufs(q, t) + [bwv], [pb])
            self.cp(vb[j2][:], pt[:, 256:512], [pb], [self.B("bvb", j2)], eng=ACT)
            for d in range(2):
                self.ts(kp[d][j2][:], pt[:, 0:256], W8[:, q, bi, d * 4 + h:d * 4 + h + 1], None, ALU.mult, None,
                        [pb, self.B("W8", q)], [self.B("bkp", d, j2)])

        def back(k):
            h, q, bi = tiles[k]
            j2 = k % NS
            par = (h * g.nseq + q) % 2
            cacc = [self.ps[2 + par * 4], self.ps[3 + par * 4]]
            pbc = [self.B("ps", 2 + par * 4), self.B("ps", 3 + par * 4)]
            nacc = self.ps[4]
            pbn = self.B("ps", 4)
            no = par * 4
            bvb = self.B("bvb", j2)
            for d in range(2):
                bkp = self.B("bkp", d, j2)
                for dc in range(2):
                    self.mm(cacc[d][:, dc * 256:(dc + 1) * 256], kp[d][j2][:, dc * 128:(dc + 1) * 128],
                            vb[j2][:], bi == 0 and dc == 0, bi == n - 1, [bkp, bvb], [pbc[d]], signal=False,
                            skip=True)
                    self.mm(nacc[:, no + d * 2 + dc:no + d * 2 + dc + 1], kp[d][j2][:, dc * 128:(dc + 1) * 128],
                            self.onesb[:, 0:1], bi == 0 and dc == 0 and d == 0, bi == n - 1,
                            [bkp, self.B("onesb")], [pbn], signal=(d == 1 and dc == 1), skip=True)
            if bi < n - 1:
                return
            for d in range(2):
                hd = d * 4 + h
                bC0 = self.B("C0", q, hd)
                cv = cacc[d][:].rearrange("p (c e) -> p c e", c=2)
                ci = state["ci"]
                state["ci"] += 1
                if g.is_p:
                    co = cout[ci % 2]
                    bco = self.B("cout", ci % 2)
                    self.cp(co[:], cv, [pbc[d]], [bco], eng=ACT)
                    self.store(O["Cn"][q, hd].rearrange("(c p) e -> p c e", p=128), co[:], [bco])
                    bns = self.B("bnsb", ci % 2)
                    self.cp(nsb[:, ci % 2, 0:2], nacc[:, no + d * 2:no + d * 2 + 2], [pbn], [bns])
                    self.store(O["nn"][q, hd].rearrange("(c p) -> p c", p=128), nsb[:, ci % 2, 0:2], [bns])
                    self.memset(C0[:, q, hd, :, :], 0.0, [bC0])
                else:
                    cn = cinit[ci % 2]
                    nn_ = ninit[ci % 2]
                    bci = self.B("cinit", ci % 2)
                    self.load(cn[:], I["sC"][hd].rearrange("(c p) e -> p c e", p=128), [bci])
                    self.load(nn_[:], I["sn"][hd].rearrange("(c p) -> p c", p=128), [bci])
                    self.stt(C0[:, q, hd, :, 0:256], cn[:], SCI[:, hd:hd + 1], cv, ALU.mult, ALU.add,
                             [bci, self.B("SCI"), pbc[d]], [bC0])
                    self.stt(C0[:, q, hd, :, 256], nn_[:], SCI[:, hd:hd + 1], nacc[:, no + d * 2:no + d * 2 + 2],
                             ALU.mult, ALU.add, [bci, self.B("SCI"), pbn], [bC0])

        LA = 2
        for k in range(min(LA, len(tiles))):
            front(k)
        for k in range(len(tiles)):
            if k + LA < len(tiles):
                front(k + LA)
            back(k)

    def mlstm_own(self, g, hT, hT_bufs, U64, E64, Z64, DEC, C0, hmT):
        I, O = self.I, self.O
        nseq, n_own, nck, T_own = g.nseq, g.n_own, g.nck, g.T_own
        TT = nseq * T_own
        NC = nseq * nck
        NW = 8 if g.is_p else 7
        wpr = [self.sb(f"mw{i}", [128, 8, 256], BF16) for i in range(NW)]
        qT = self.sb("qT", [128, 2, TT], BF16)
        kT = self.sb("kT", [128, 2, TT], BF16)
        ktok = self.sb("ktok", [128, NC, 256], BF16)
        vext = self.sb("vext", [128, NC, 257], BF16)
        sigo = self.sb("sigo", [128, NC, 256], BF16)
        hraw = [self.sb(f"hraw{d}", [128, NC, 257], F32) for d in range(2)]
        hm = hraw[0]
        Ecp = self.sb("Ecp", [128, 2, NC], F32)
        Rd = self.sb("Rd", [128, 2, NC], F32)
        Rd2 = self.sb("Rd2", [128, 2, NC], F32)
        Cst = [[self.sb(f"Cst{q}{d}", [128, 2, 257], F32) for d in range(2)] for q in range(nseq)]
        Cbf = [[[self.sb(f"Cbf{q}{d}{i}", [128, 2, 257], BF16) for i in range(2)] for d in range(2)] for q in range(nseq)]
        PTall = self.sb("PTall", [128, 2, NC, 128], BF16)
        kpr = [[[self.sb(f"kpr{q}{d}{i}", [128, 256], BF16) for i in range(2)] for d in range(2)] for q in range(nseq)]
        sm = [[self.sb(f"sm{d}{i}", [128, 4], F32) for i in range(2)] for d in range(2)]
        mng = self.sb("mng", [128, 256], F32)
        hsq = self.sb("hsq", [128, 256], F32)
        ssq = self.sb("ssq", [128, NC], F32)
        hn = [self.sb(f"hn{i}", [128, 256], BF16) for i in range(2)]
        self.memset(vext[:, :, 256:257], 1.0, [self.B("vones")])
        Uf, Ub = self.cm[:, 1, :], self.cm[:, 2, :]
        bvo = self.B("vones")
        loaded = set()
        for h in range(4):
            slots = [(h * 4 + pc) % NW for pc in range(4)]
            bw = [self.B("mw", sl_) for sl_ in slots]
            wp = [wpr[sl_] for sl_ in slots]
            self.load(mng[:], I["mng"][h * 256:(h + 1) * 256].partition_broadcast(128), [self.B("mng")])
            for pc in range(4):
                if (h, pc) not in loaded:
                    self.load(wp[pc][:], I["wml"][h, pc].rearrange("p (c n) -> p c n", c=8), [bw[pc]], q=POOL)
                    loaded.add((h, pc))
            if h + 1 < 4:
                for pc in range(NW - 4):
                    sl_ = ((h + 1) * 4 + pc) % NW
                    self.load(wpr[sl_][:], I["wml"][h + 1, pc].rearrange("p (c n) -> p c n", c=8),
                              [self.B("mw", sl_)], q=POOL)
                    loaded.add((h + 1, pc))
            wq_, wk_, wv_, wo_ = wp
            blk = 512 if TT >= 512 else TT
            cnt = 0
            for (wsrc, bws, dstT, nm) in ((wq_, bw[0], qT, "qT"), (wk_, bw[1], kT, "kT")):
                for dc in range(2):
                    for b0 in range(0, TT, blk):
                        pt = self.ps[cnt % 2]
                        pb = self.B("ps", cnt % 2)
                        cnt += 1
                        tl = range(b0 // 128, (b0 + blk) // 128)
                        rd = [self.B("hT", g.name, tt_ // n_own, tt_ % n_own, c) for tt_ in tl for c in range(8)]
                        for c in range(8):
                            self.mm(pt[:, 0:blk], wsrc[:, c, dc * 128:(dc + 1) * 128], hT[:, c, b0:b0 + blk],
                                    c == 0, c == 7, rd + [bws], [pb])
                        self.evac_cast(dstT[:, dc, b0:b0 + blk], pt[:, 0:blk], [pb],
                                       [self.B(nm, dc, tt_) for tt_ in tl])
            for ck in range(NC):
                q, cl = ck // nck, ck % nck
                t = cl
                cols = slice(q * T_own + cl * 128, q * T_own + cl * 128 + 128)
                rd = hT_bufs(q, t)
                pt = self.ps[2 + ck % 2]
                pb = self.B("ps", 2 + ck % 2)
                for c in range(8):
                    self.mm(pt[:, 0:256], hT[:, c, cols], wk_[:, c, :], c == 0, c == 7, rd + [bw[1]], [pb],
                            signal=False)
                for c in range(8):
                    self.mm(pt[:, 256:512], hT[:, c, cols], wv_[:, c, :], c == 0, c == 7, rd + [bw[2]], [pb])
                self.cp(ktok[:, ck, :], pt[:, 0:256], [pb], [self.B("ktok", ck)], eng=ACT)
                self.cp(vext[:, ck, 0:256], pt[:, 256:512], [pb], [self.B("vext", ck), bvo])
                pt2 = self.ps[4 + ck % 2]
                pb2 = self.B("ps", 4 + ck % 2)
                for c in range(8):
                    self.mm(pt2[:, 0:256], hT[:, c, cols], wo_[:, c, :], c == 0, c == 7, rd + [bw[3]], [pb2])
                self.act(sigo[:, ck, :], pt2[:, 0:256], AF.Sigmoid, [pb2], [self.B("sigo", ck)])
            cntA = 0
            for q in range(nseq):
                for cl in range(nck):
                    for d in range(2):
                        hd = d * 4 + h
                        ck = q * nck + cl
                        t = cl
                        cols = slice(q * T_own + cl * 128, q * T_own + cl * 128 + 128)
                        bank = 2 + cntA % 4
                        cntA += 1
                        pS = self.ps[bank][:, 0:128]
                        bS = self.B("ps", bank)
                        for dc in range(2):
                            self.mm(pS, kT[:, dc, cols], qT[:, dc, cols], dc == 0, dc == 1,
                                    [self.B("kT", dc, (q * T_own) // 128 + t), self.B("qT", dc, (q * T_own) // 128 + t)],
                                    [bS])
                        self.stt(PTall[:, d, ck, :], pS, U64[:, q, cl, hd:hd + 1], (Uf if d == 0 else Ub),
                                 ALU.mult, ALU.mult, [bS, self.B("U64", q), self.B("consts")], [self.B("PT", d, ck)])
            def idx(q, d, step):
                cl = step if d == 0 else nck - 1 - step
                return cl, q * nck + cl

            def emit_kpr(q, d, step):
                hd = d * 4 + h
                cl, ck = idx(q, d, step)
                kdst = kpr[q][d][step % 2]
                self.S.op(ACT, lambda e, kdst=kdst, ck=ck, q=q, cl=cl, hd=hd: e.activation(
                    out=kdst[:], in_=ktok[:, ck, :], func=AF.Copy, scale=Z64[:, q, cl, hd:hd + 1]),
                    [self.B("ktok", ck), self.B("Z64", q)], [self.B("kpr", q, d, step % 2)])

            def emit_U(q, d, step):
                cl, ck = idx(q, d, step)
                par = step % 2
                slot = par if nseq == 1 else q
                bkp = self.B("kpr", q, d, par)
                pUC = self.ps[2 + d * 2 + slot]
                bUC = self.B("ps", 2 + d * 2 + slot)
                for dc in range(2):
                    self.mm(pUC[:, dc * 256:(dc + 1) * 256], kpr[q][d][par][:, dc * 128:(dc + 1) * 128],
                            vext[:, ck, 0:256], True, True, [bkp, self.B("vext", ck)], [bUC])
                n0 = (d * 2 + slot) * 2
                for dc in range(2):
                    self.mm(self.ps[6][:, n0 + dc:n0 + dc + 1], kpr[q][d][par][:, dc * 128:(dc + 1) * 128],
                            vext[:, ck, 256:257], True, True, [bkp, bvo], [self.B("ps", 6)])

            for q in range(nseq):
                for d in range(2):
                    hd = d * 4 + h
                    self.cp(Cst[q][d][:], C0[:, q, hd, :, :], [self.B("C0", q, hd)], [self.B("Cst", q, d)])
                    self.cp(Cbf[q][d][0][:], C0[:, q, hd, :, :], [self.B("C0", q, hd)], [self.B("Cbf", q, d, 0)], eng=ACT)
                    if nck > 1:
                        emit_kpr(q, d, 0)
                        emit_U(q, d, 0)
                    if nck > 2:
                        emit_kpr(q, d, 1)
            for step in range(nck):
                cur, nxt = step % 2, (step + 1) % 2
                for q in range(nseq):
                    for d in range(2):
                        hd = d * 4 + h
                        cl, ck = idx(q, d, step)
                        t = cl
                        cols = slice(q * T_own + cl * 128, q * T_own + cl * 128 + 128)
                        bCs = self.B("Cst", q, d)
                        if step < nck - 1:
                            slot = cur if nseq == 1 else q
                            pUC = self.ps[2 + d * 2 + slot]
                            bUC = self.B("ps", 2 + d * 2 + slot)
                            n0 = (d * 2 + slot) * 2
                            self.stt(Cst[q][d][:, :, 0:256], Cst[q][d][:, :, 0:256], DEC[:, q, cl, hd:hd + 1],
                                     pUC[:].rearrange("p (c e) -> p c e", c=2), ALU.mult, ALU.add,
                                     [bCs, self.B("DEC", q), bUC], [bCs])
                            self.stt(Cst[q][d][:, :, 256], Cst[q][d][:, :, 256], DEC[:, q, cl, hd:hd + 1],
                                     self.ps[6][:, n0:n0 + 2], ALU.mult, ALU.add,
                                     [bCs, self.B("DEC", q), self.B("ps", 6)], [bCs])
                            self.cp(Cbf[q][d][nxt][:], Cst[q][d][:], [bCs], [self.B("Cbf", q, d, nxt)], eng=ACT)
                            if step + 1 < nck - 1:
                                emit_U(q, d, step + 1)
                            if step + 2 < nck - 1:
                                emit_kpr(q, d, step + 2)
                        pO = self.ps[d][:, 0:257]
                        bO = self.B("ps", d)
                        self.mm(pO, PTall[:, d, ck, :], vext[:, ck, :], True, False,
                                [self.B("PT", d, ck), self.B("vext", ck), bvo], [bO], signal=False)
                        for dc in range(2):
                            self.mm(pO, qT[:, dc, cols], Cbf[q][d][cur][:, dc, :], False, dc == 1,
                                    [self.B("qT", dc, (q * T_own) // 128 + t), self.B("Cbf", q, d, cur)], [bO])
                        self.cp(hraw[d][:, ck, :], pO[:, 0:257], [bO], [self.B("hraw", d, ck)], eng=ACT)
            bR = self.B("Rden")
            rdall = [self.B("hraw", d, ck) for d in range(2) for ck in range(NC)]
            for d in range(2):
                self.cp(Ecp[:, d, :].rearrange("p (q c) -> p q c", q=nseq), E64[:, :, :, d * 4 + h],
                        [self.B("E64", q) for q in range(nseq)], [bR])
                self.tt(Rd[:, d, :], hraw[d][:, :, 256], Ecp[:, d, :], ALU.mult, rdall + [bR], [bR])
            self.stt(Rd2[:], Rd[:], -1.0, Rd[:], ALU.mult, ALU.max, [bR], [bR])
            self.ts(Rd2[:], Rd2[:], 1.0, None, ALU.max, None, [bR], [bR])
            self.S.op(DVE, lambda e: e.reciprocal(out=Rd2[:], in_=Rd2[:]), [bR], [bR])
            self.tt(Rd[:], Ecp[:], Rd2[:], ALU.mult, [bR], [bR])
            for d in range(2):
                self.tt(hraw[d][:, :, 0:256], hraw[d][:, :, 0:256],
                        Rd[:, d, :].unsqueeze(2).to_broadcast([128, NC, 256]), ALU.mult,
                        [self.B("hraw", d, ck) for ck in range(NC)] + [bR], [self.B("hraw", d, ck) for ck in range(NC)])
            for ck in range(NC):
                self.tt(hm[:, ck, 0:256], hraw[0][:, ck, 0:256], hraw[1][:, ck, 0:256], ALU.add,
                        [self.B("hraw", 0, ck), self.B("hraw", 1, ck)], [self.B("hm", ck)])
            bss = self.B("ssq")
            for ck in range(NC):
                self.S.op(ACT, lambda e, ck=ck: e.activation(out=hsq[:], in_=hm[:, ck, 0:256], func=AF.Square,
                                                             accum_out=ssq[:, ck:ck + 1]),
                          [self.B("hm", ck)], [bss, self.B("hsq")])
            self.act(ssq[:], ssq[:], AF.Sqrt, [bss], [bss], bias=EPS, scale=1.0 / 256.0)
            self.S.op(DVE, lambda e: e.reciprocal(out=ssq[:], in_=ssq[:]), [bss], [bss])
            for ck in range(NC):
                q, cl = ck // nck, ck % nck
                i2 = ck % 2
                bhm = self.B("hm", ck)
                bhn = self.B("hn", i2)
                self.stt(hm[:, ck, 0:256], hm[:, ck, 0:256], ssq[:, ck:ck + 1], mng[:], ALU.mult, ALU.mult,
                         [bhm, bss, self.B("mng")], [bhm])
                self.tt(hn[i2][:], hm[:, ck, 0:256], sigo[:, ck, :], ALU.mult, [bhm, self.B("sigo", ck)], [bhn])
                ptr = self.ps[7]
                pbt = self.B("ps", 7)
                ptv = ptr[:].bitcast(BF16)
                for dc in range(2):
                    self.tr(ptv[:, i2 * 256 + dc * 128:i2 * 256 + dc * 128 + 128], hn[i2][:, dc * 128:(dc + 1) * 128],
                            self.identb[:], [bhn, self.B("identb")], [pbt], signal=(dc == 1))
                c0 = q * T_own + cl * 128
                self.cp(hmT[:, 2 * h:2 * h + 2, c0:c0 + 128],
                        ptv[:, i2 * 256:i2 * 256 + 256].rearrange("p (c n) -> p c n", c=2),
                        [pbt], [self.B("hmT", h, ck)], eng=ACT)

    def attention(self, g, hT, hT_bufs, KaT, Va, haT):
        I, O = self.I, self.O
        nseq, n_own, T_own = g.nseq, g.n_own, g.T_own
        TT = nseq * T_own
        nkt = g.n_ctx + g.n_cache
        QT = self.sb("QT", [128, 8, TT], BF16)
        wq = [self.sb(f"wqa{i}", [128, 8, 512], BF16) for i in range(2)]
        for i in range(2):
            self.load(wq[i][:], I["wqa"][i].rearrange("p (c n) -> p c n", c=8), [self.B("wqa", i)], q=POOL)
        qn = [self.sb(f"qn{i}", [128, 8, 128], F32) for i in range(2)]
        qb = [self.sb(f"qb{i}", [128, 8, 128], BF16) for i in range(2)]
        scr1 = self.sb("qscr", [128, 8, 128], F32)
        scr = [scr1, scr1]
        ss = [self.sb(f"qss{i}", [128, 8], F32) for i in range(2)]
        t1 = [self.sb(f"qt1{i}", [128, 8, 128], F32) for i in range(2)]
        t2 = [self.sb(f"qt2{i}", [128, 8, 128], F32) for i in range(2)]
        if g.rope:
            ropeT = self.sb("ropeT", [128, n_own, 128], F32)
            self.load(ropeT[:], I["rope"][0:n_own * 128, :].rearrange("(t p) n -> p t n", p=128), [self.B("ropeT")])
        def q_job(q, t):
            tg = q * n_own + t
            i2 = tg % 2
            cols = slice(tg * 128, tg * 128 + 128)
            pts = [self.ps[2 * i2], self.ps[2 * i2 + 1]]
            pbs = [self.B("ps", 2 * i2), self.B("ps", 2 * i2 + 1)]
            for hf in range(2):
                for c in range(8):
                    self.mm(pts[hf][:], hT[:, c, cols], wq[hf][:, c, :], c == 0, c == 7,
                            hT_bufs(q, t) + [self.B("wqa", hf)], [pbs[hf]])
            yield
            bqn, bss = self.B("qn", i2), self.B("qss", i2)
            for hf in range(2):
                yield from self.rms_heads(pts[hf][:].rearrange("p (h d) -> p h d", h=4), 4, self.QG[:],
                                          qn[i2][:, hf * 4:(hf + 1) * 4, :], [pbs[hf]], bqn,
                                          scr[i2][:, hf * 4:(hf + 1) * 4, :], ss[i2][:, hf * 4:(hf + 1) * 4], bss)
            if g.rope:
                yield from self.rope_apply(qn[i2][:], 8, ropeT[:, t, :], self.B("ropeT"), bqn, t1[i2][:], t2[i2][:],
                                           self.B("qrtmp", i2))
            bqb = self.B("qb", i2)
            self.cp(qb[i2][:], qn[i2][:], [bqn], [bqb], eng=ACT)
            yield
            for hf in range(2):
                ptr = self.ps[4 + 2 * i2 + hf]
                pbt = self.B("ps", 4 + 2 * i2 + hf)
                ptv = ptr[:].bitcast(BF16)
                for hh in range(4):
                    self.tr(ptv[:, hh * 128:(hh + 1) * 128], qb[i2][:, hf * 4 + hh, :], self.identb[:],
                            [bqb, self.B("identb")], [pbt], signal=(hh == 3))
                yield
                self.cp(QT[:, hf * 4:(hf + 1) * 4, cols], ptv[:, 0:512].rearrange("p (h n) -> p h n", h=4),
                        [pbt], [self.B("QT", hf, tg)], eng=(ACT if hf else DVE))
                yield

        self.interleave((q_job(q, t) for q in range(nseq) for t in range(n_own)), 2, stagger=10)
        PTs = [self.sb(f"aPT{i}", [128, 512], BF16) for i in range(5)]
        DACC = [self.sb(f"dacc{i}", [128, 512], F32) for i in range(2)]
        rden = [self.sb(f"rden{i}", [128, 512], F32) for i in range(2)]
        qblk = 512 if T_own >= 512 else T_own
        scale = 128.0 ** -0.5
        ONES = self.cm[:, 5, :]
        its = []
        ob = 0
        for q in range(nseq):
            for hq in range(8):
                for b0 in range(0, T_own, qblk):
                    for kt in range(nkt):
                        its.append((q, hq, b0, kt, ob))
                    ob += 1
        use_pool = nkt >= 8

        def issue_st(i):
            q, hq, b0, kt, ob_ = its[i]
            kvh = hq // 4
            c0 = q * T_own + b0
            tgl = range(c0 // 128, (c0 + qblk) // 128)
            pS = self.ps[i % 5]
            bS = self.B("ps", i % 5)
            self.mm(pS[:, 0:qblk], KaT[:, q, kvh, kt * 128:(kt + 1) * 128], QT[:, hq, c0:c0 + qblk],
                    True, True, [self.B("KaT", q, kt)] + [self.B("QT", hq // 4, tg) for tg in tgl], [bS])

        LA = 4
        for i in range(min(LA, len(its))):
            issue_st(i)
        for i in range(len(its)):
            q, hq, b0, kt, ob_ = its[i]
            kvh = hq // 4
            c0 = q * T_own + b0
            tgl = range(c0 // 128, (c0 + qblk) // 128)
            par = ob_ % 2
            pO = self.ps[5 + par]
            bO = self.B("ps", 5 + par)
            pS = self.ps[i % 5]
            bS = self.B("ps", i % 5)
            pt_ = PTs[i % 5]
            bP = self.B("aPT", i % 5)
            self.act(pt_[:, 0:qblk], pS[:, 0:qblk], AF.Exp, [bS], [bP], scale=scale)
            if i + LA < len(its):
                issue_st(i + LA)
            self.mm(pO[:, 0:qblk], Va[:, q, kt, kvh, 0:128], pt_[:, 0:qblk], kt == 0, kt == nkt - 1,
                    [bP, self.B("Va", q, kt)], [bO])
            pD = self.ps[7]
            bD = self.B("ps", 7)
            if kt % 2 == 1:
                self.mm(pD[:, 0:qblk], self.onesb128[:], pt_[:, 0:qblk], kt == 1, False,
                        [bP, self.B("onesb128")], [bD], signal=False)
            else:
                acc = DACC[par]
                bacc = self.B("dacc", par)
                if kt == 0:
                    self.cp(acc[:, 0:qblk], pt_[:, 0:qblk], [bP], [bacc])
                else:
                    self.tt(acc[:, 0:qblk], acc[:, 0:qblk], pt_[:, 0:qblk], ALU.add, [bacc, bP], [bacc])
            if kt == nkt - 1:
                self.mm(pD[:, 0:qblk], ONES, DACC[par][:, 0:qblk], nkt == 1, True,
                        [self.B("dacc", par), self.B("consts")], [bD])
                brd = self.B("rden", par)
                self.S.op(DVE, lambda e, par=par, pD=pD: e.reciprocal(out=rden[par][:, 0:qblk], in_=pD[:, 0:qblk]),
                          [bD], [brd])
                self.tt(haT[:, hq, c0:c0 + qblk], pO[:, 0:qblk], rden[par][:, 0:qblk], ALU.mult, [bO, brd],
                        [self.B("haT", hq, tg) for tg in tgl])

    def post(self, g, hT, hmT, haT):
        I, O = self.I, self.O
        nseq, n_own, T_own = g.nseq, g.n_own, g.T_own
        TT = nseq * T_own
        ntile = TT // 128
        s = g.mset
        xsrc = I["xp"] if g.is_p else I["xs"]
        ydst = O["yp"] if g.is_p else O["ys"]

        def xrow(tg):
            q, t = tg // n_own, tg % n_own
            return (q * g.n_ctx + t) * 128
        x1a = self.sb("x1a", [128, ntile, 1024], F32)
        h2T = self.sb("h2T", [128, 8, TT], BF16)
        mark = self.cur
        w5 = [self.sb(f"w5{i}", [128, 4, 8, 128], BF16) for i in range(3)]
        wo = [self.sb(f"wo{i}", [128, 8, 512], BF16) for i in range(2)]
        mT = self.sb("mT", [128, 8, 512], BF16)
        xin = [self.sb(f"xin5{i}", [128, 1024], F32) for i in range(2)]
        ytmp = [self.sb(f"ytmp{i}", [128, 1024], F32) for i in range(2)]
        rows = self.sb("rows5", [128, 2, 1024], F32)
        sgm = [self.sb(f"sgm{i}", [128, 512], F32) for i in range(2)]
        sga = [self.sb(f"sga{i}", [128, 512], F32) for i in range(2)]
        m1 = [self.sb(f"m1{i}", [128, 512], F32) for i in range(2)]
        st = self.sb("st5", [128, 2, 12], F32)
        mv = self.sb("mv5", [128, 2, 4], F32)
        brow = self.B("rows5")
        self.load(rows[:, 0, :], I["ln1g"].partition_broadcast(128), [brow])
        self.load(rows[:, 1, :], I["ln1b"].partition_broadcast(128), [brow])
        self.ts(rows[:], rows[:], ALPHA, None, ALU.mult, None, [brow], [brow])
        for i in range(2):
            self.load(wo[i][:], I["wout"][i].rearrange("p (c n) -> p c n", c=8), [self.B("wo", i)], q=POOL)
        S2 = self.COL[:, s, 2, :]
        B2 = self.COL[:, s, 3, :]
        bcol = self.B("COL", s)
        nblk = TT // 512
        xc = 0
        for blk in range(nblk):
            b0 = blk * 512
            tgl = list(range(b0 // 128, b0 // 128 + 4))
            rd_h = [self.B("hT", g.name, tg // n_own, tg % n_own, c) for tg in tgl for c in range(8)]
            rd_hm = [self.B("hmT", h, ck) for h in range(4) for ck in range(b0 // 128, b0 // 128 + 4)]
            rd_ha = [self.B("haT", hq, tg) for hq in range(8) for tg in tgl]
            for fc in range(8):
                wi = (blk * 8 + fc) % 3
                w = w5[wi]
                bw = self.B("w5", wi)
                self.load(w[:], I["w5"][fc].rearrange("p (k c n) -> p k c n", k=4, c=8), [bw], q=POOL)
                srcs = (hmT, haT, hT, hT)
                rds = (rd_hm, rd_ha, rd_h, rd_h)
                pof = 4 * (fc % 2)
                pss = [self.ps[pof + k] for k in range(4)]
                pbs = [self.B("ps", pof + k) for k in range(4)]
                for k in (2, 3, 0, 1):
                    for c in range(8):
                        self.mm(pss[k][:], w[:, k, c, :], srcs[k][:, c, b0:b0 + 512], c == 0, c == 7,
                                rds[k] + [bw], [pbs[k]])
                f2 = fc % 2
                bsg, bsa, bm1 = self.B("sgm", f2), self.B("sga", f2), self.B("m1", f2)
                self.act(sgm[f2][:], pss[2][:], AF.Sigmoid, [pbs[2]], [bsg])
                self.act(sga[f2][:], pss[3][:], AF.Sigmoid, [pbs[3]], [bsa])
                self.tt(m1[f2][:], pss[0][:], sgm[f2][:], ALU.mult, [pbs[0], bsg], [bm1])
                self.tt(sga[f2][:], pss[1][:], sga[f2][:], ALU.mult, [pbs[1], bsa], [bsa])
                self.tt(mT[:, fc, :], m1[f2][:], sga[f2][:], ALU.add, [bm1, bsa], [self.B("mT", fc)])
            rd_m = [self.B("mT", fc) for fc in range(8)]

            def tile_front(ti, tg):
                nonlocal xc
                i2 = xc % 2
                xc += 1
                bx = self.B("xin5", i2)
                self.load(xin[i2][:], xsrc[xrow(tg):xrow(tg) + 128, :], [bx])
                pm = [self.ps[4 + (ti % 2) * 2], self.ps[5 + (ti % 2) * 2]]
                bpm = [self.B("ps", 4 + (ti % 2) * 2), self.B("ps", 5 + (ti % 2) * 2)]
                for hf in range(2):
                    for c in range(8):
                        self.mm(pm[hf][:], mT[:, c, ti * 128:(ti + 1) * 128], wo[hf][:, c, :], c == 0, c == 7,
                                rd_m + [self.B("wo", hf)], [bpm[hf]])
                y = ytmp[i2]
                by = self.B("ytmp", i2)
                for hf in range(2):
                    self.tt(y[:, hf * 512:(hf + 1) * 512], pm[hf][:], self.G1[:, s, hf * 512:(hf + 1) * 512], ALU.mult,
                            [bpm[hf], self.B("G1", s)], [by])
                self.stt(y[:], xin[i2][:], ALPHA, y[:], ALU.mult, ALU.add, [bx, by], [by])
                self.layernorm(y, by, st, mv, i2)
                bxa = self.B("x1a", tg)
                self.tt(x1a[:, tg, :], y[:], rows[:, 0, :], ALU.mult, [by, brow], [bxa])
                self.tt(x1a[:, tg, :], x1a[:, tg, :], rows[:, 1, :], ALU.add, [bxa, brow], [bxa])
                return i2

            def tile_back(ti, tg, i2):
                y = ytmp[i2]
                by = self.B("ytmp", i2)
                pbank = [self.ps[0 + (ti % 2) * 2], self.ps[1 + (ti % 2) * 2]]
                pbb = [self.B("ps", 0 + (ti % 2) * 2), self.B("ps", 1 + (ti % 2) * 2)]
                for c in range(8):
                    self.tr(pbank[c // 4][:, (c % 4) * 128:(c % 4 + 1) * 128], y[:, c * 128:(c + 1) * 128],
                            self.ident, [by, self.B("consts")], [pbb[c // 4]], signal=(c % 4 == 3))
                for c in range(8):
                    src = pbank[c // 4][:, (c % 4) * 128:(c % 4 + 1) * 128]
                    o_ap = h2T[:, c, tg * 128:(tg + 1) * 128]
                    bh = self.B("h2T", tg, c)
                    if c // 4 == 0:
                        self.S.op(ACT, lambda e, o_ap=o_ap, src=src, c=c: e.activation(
                            out=o_ap, in_=src, func=AF.Identity, scale=S2[:, c:c + 1], bias=B2[:, c:c + 1]),
                            [pbb[c // 4], bcol], [bh])
                    else:
                        self.ts(o_ap, src, S2[:, c:c + 1], B2[:, c:c + 1], ALU.mult, ALU.add, [pbb[c // 4], bcol], [bh])

            prev = None
            for ti, tg in enumerate(tgl):
                i2 = tile_front(ti, tg)
                if prev is not None:
                    tile_back(*prev)
                prev = (ti, tg, i2)
            tile_back(*prev)
        self.S.barrier()
        self.cur = self.base + self.arena0
        uT = self.sb("uT", [128, 32, 512], BF16)
        rows6 = self.sb("rows6", [128, 2, 1024], F32)
        y2 = [self.sb(f"y2{i}", [128, 1024], F32) for i in range(2)]
        assert self.cur <= self.base + self.arena0 + 48 * 1024
        self.cur = mark
        wd = self.sb("wd", [128, 32, 1024], BF16)
        wu = [self.sb(f"wu{i}", [128, 8, 256], BF16) for i in range(4)]
        ur = [self.sb(f"ur{i}", [128, 512], F32) for i in range(2)]
        st = self.sb("st6", [128, 2, 12], F32)
        mv = self.sb("mv6", [128, 2, 4], F32)
        br6 = self.B("rows6")
        self.load(rows6[:, 0, :], I["ln2g"].partition_broadcast(128), [br6])
        self.load(rows6[:, 1, :], I["ln2b"].partition_broadcast(128), [br6])
        wc = 0
        two_path = g.is_p
        pcs = 0
        if two_path:
            wstg = [self.sb(f"wdstg{i}", [128, 2, 1024], F32) for i in range(2)]
        for blk in range(nblk):
            b0 = blk * 512
            tgl = list(range(b0 // 128, b0 // 128 + 4))
            rd_h2 = [self.B("h2T", tg, c) for tg in tgl for c in range(8)]
            for sl in range(16):
                wi = wc % 4
                wc += 1
                bw = self.B("wu", wi)
                self.load(wu[wi][:], I["wup"][sl].rearrange("p (c n) -> p c n", c=8), [bw], q=POOL)
                if two_path and sl % 4 == 0:
                    qd = sl // 4
                    for piece in range(4):
                        stg = wstg[pcs % 2]
                        bst = self.B("wdstg", pcs % 2)
                        pcs += 1
                        self.load(stg[:], I["wdn"][qd][:, piece * 2048:(piece + 1) * 2048].rearrange(
                            "p (c n) -> p c n", c=2), [bst], q=SP)
                        fc0 = qd * 8 + piece * 2
                        self.cp(wd[:, fc0:fc0 + 2, :], stg[:], [bst], [self.B("wd", qd)])
                if (not two_path) and sl % 4 == 3:
                    qd = sl // 4
                    self.load(wd[:, qd * 8:(qd + 1) * 8, :], I["wdn"][qd].rearrange("p (c n) -> p c n", c=8),
                              [self.B("wd", qd)], q=POOL)
                for f4 in range(2):
                    fc = sl * 2 + f4
                    pt = self.ps[fc % 4]
                    pb = self.B("ps", fc % 4)
                    for c in range(8):
                        self.mm(pt[:], wu[wi][:, c, f4 * 128:(f4 + 1) * 128], h2T[:, c, b0:b0 + 512], c == 0, c == 7,
                                rd_h2 + [bw], [pb])
                    bur = self.B("ur", fc % 2)
                    self.act(ur[fc % 2][:], pt[:], AF.Relu, [pb], [bur])
                    self.tt(uT[:, fc, :], ur[fc % 2][:], ur[fc % 2][:], ALU.mult, [bur], [self.B("uT", fc)])
            rd_u = [self.B("uT", fc) for fc in range(32)]
            for ti, tg in enumerate(tgl):
                i2 = ti % 2
                pm = [self.ps[4 + i2 * 2], self.ps[5 + i2 * 2]]
                bpm = [self.B("ps", 4 + i2 * 2), self.B("ps", 5 + i2 * 2)]
                for hf in range(2):
                    for fc in range(32):
                        self.mm(pm[hf][:], uT[:, fc, ti * 128:(ti + 1) * 128], wd[:, fc, hf * 512:(hf + 1) * 512],
                                fc == 0, fc == 31, rd_u + [self.B("wd", fc // 8)], [bpm[hf]])
                y = y2[i2]
                by = self.B("y2", i2)
                for hf in range(2):
                    self.tt(y[:, hf * 512:(hf + 1) * 512], pm[hf][:], self.G2[:, s, hf * 512:(hf + 1) * 512], ALU.mult,
                            [bpm[hf], self.B("G2", s)], [by])
                self.tt(y[:], y[:], x1a[:, tg, :], ALU.add, [by, self.B("x1a", tg)], [by])
                self.layernorm(y, by, st, mv, i2)
                self.tt(y[:], y[:], rows6[:, 0, :], ALU.mult, [by, br6], [by])
                self.tt(y[:], y[:], rows6[:, 1, :], ALU.add, [by, br6], [by])
                self.store(ydst[tg * 128:(tg + 1) * 128, :], y[:], [by])

    def layernorm(self, y, by, st, mv, i2):
        bst = self.B("lnst", i2)
        self.S.op(DVE, lambda e: e.bn_stats(out=st[:, i2, 0:6], in_=y[:, 0:512]), [by], [bst])
        self.S.op(DVE, lambda e: e.bn_stats(out=st[:, i2, 6:12], in_=y[:, 512:1024]), [by], [bst])
        self.S.op(DVE, lambda e: e.bn_aggr(out=mv[:, i2, 0:2], in_=st[:, i2, :]), [bst], [bst])
        self.act(mv[:, i2, 2:3], mv[:, i2, 1:2], AF.Sqrt, [bst], [bst], bias=EPS, scale=1.0)
        self.S.op(DVE, lambda e: e.reciprocal(out=mv[:, i2, 2:3], in_=mv[:, i2, 2:3]), [bst], [bst])
        self.ts(y[:], y[:], mv[:, i2, 0:1], mv[:, i2, 2:3], ALU.subtract, ALU.mult, [by, bst], [by])


def _lay(w):
    n = w.shape[1]
    return np.ascontiguousarray(w.reshape(8, 128, n).transpose(1, 0, 2).reshape(128, 8 * n))


def _rope_tables():
    T, GW, NF = 4096, 64, 32
    rows = T // GW
    row = np.repeat(np.arange(rows), GW)
    col = np.tile(np.arange(GW), rows)
    inv = (np.float32(10000.0) ** (-np.arange(NF, dtype=np.float32) / np.float32(NF))).astype(np.float32)
    ang = np.stack([row, col], -1).astype(np.float32)[..., None] * inv
    return np.concatenate([np.cos(ang).reshape(T, 64), np.sin(ang).reshape(T, 64)], axis=1).astype(np.float32)


_NC_CACHE = {}


def kernel(x_prompt, x_sample, cache_k, cache_v, state_C, state_n, state_m, c, c_ctx,
           w_mod, b_mod, w_in, b_gates, mlstm_norm_g, q_norm_g, k_norm_g, w_bm, w_ba, w_out,
           ln1_g, ln1_b, w_up, w_down, ln2_g, ln2_b, _dbg=None, _stop=None):
    f = lambda a: np.ascontiguousarray(np.asarray(a, dtype=np.float32))
    x_prompt, x_sample, w_in0 = f(x_prompt), f(x_sample), f(w_in)[0]
    w_mod0, w_bm0, w_ba0, w_out0, w_up0, w_down0 = f(w_mod)[0], f(w_bm)[0], f(w_ba)[0], f(w_out)[0], f(w_up)[0], f(w_down)[0]
    perm = np.array([0, 1, 2, 3, 8, 9, 10, 11, 4, 5, 6, 7, 12, 13, 14, 15])
    shared = {}
    shared["wmod"] = np.stack([_lay(w_mod0[:, j * 512:(j + 1) * 512]) for j in range(12)])
    shared["bmod"] = f(b_mod)[0]
    shared["wg"] = _lay(w_in0[:, 4096 + perm])
    shared["bg"] = f(b_gates)[0][perm].copy()
    shared["wml"] = np.stack([np.stack([_lay(w_in0[:, p * 1024 + h * 256:p * 1024 + (h + 1) * 256]) for p in range(4)])
                              for h in range(4)])
    shared["wqa"] = np.stack([_lay(w_in0[:, 4112 + i * 512:4112 + (i + 1) * 512]) for i in range(2)])
    shared["wkva"] = _lay(w_in0[:, 5136:5648])
    w5 = []
    for fc in range(8):
        sl = slice(fc * 128, (fc + 1) * 128)
        parts = [w_bm0[:, sl], w_ba0[:, sl], w_in0[:, 5648 + fc * 128:5648 + (fc + 1) * 128],
                 w_in0[:, 6672 + fc * 128:6672 + (fc + 1) * 128]]
        w5.append(np.concatenate([_lay(p) for p in parts], axis=1))
    shared["w5"] = np.stack(w5)
    shared["wout"] = np.stack([_lay(w_out0[:, i * 512:(i + 1) * 512]) for i in range(2)])
    shared["wup"] = np.stack([_lay(w_up0[:, i * 256:(i + 1) * 256]) for i in range(16)])
    shared["wdn"] = np.ascontiguousarray(w_down0.reshape(4, 8, 128, 1024).transpose(0, 2, 1, 3).reshape(4, 128, 8192))
    shared["mng"] = f(mlstm_norm_g)[0]
    shared["qg"] = f(q_norm_g)[0]
    shared["kg"] = f(k_norm_g)[0]
    shared["ln1g"], shared["ln1b"] = f(ln1_g)[0], f(ln1_b)[0]
    shared["ln2g"], shared["ln2b"] = f(ln2_g)[0], f(ln2_b)[0]
    p = np.arange(128)
    cm = np.zeros((128, 6, 128), np.float32)
    cm[:, 0, :] = (p[:, None] == p[None, :])
    cm[:, 1, :] = (p[:, None] <= p[None, :])
    cm[:, 2, :] = (p[:, None] >= p[None, :])
    cm[:, 3, :] = (p[:, None] < p[None, :])
    cm[:, 4, :] = (p[:, None] > p[None, :])
    cm[:, 5, :] = 1.0
    shared["cm"] = cm
    rope = _rope_tables()
    in_maps = []
    for r in range(8):
        b, j = r // 4, r % 4
        order = [(j + i) % 4 for i in range(4)]
        m = dict(shared)
        m["xp"] = x_prompt[2 * r:2 * r + 2].reshape(512, 1024)
        m["xs"] = np.ascontiguousarray(x_sample[b].reshape(4, 1024, 1024)[order].reshape(4096, 1024))
        m["rope"] = np.ascontiguousarray(rope.reshape(4, 1024, 128)[order].reshape(4096, 128))
        m["ck"] = f(cache_k)[b, 0].reshape(256, 256)
        m["cv"] = f(cache_v)[b, 0].reshape(256, 256)
        m["sC"] = f(state_C)[b, 0].reshape(8, 256, 256)
        m["sn"] = f(state_n)[b, 0].reshape(8, 256)
        m["sm"] = f(state_m)[b, 0].reshape(8)
        cvec = np.stack([f(c_ctx), f(c)[b]])
        m["cT"] = np.ascontiguousarray(cvec.reshape(2, 8, 128).transpose(2, 1, 0).reshape(128, 16))
        blk = np.zeros((128, 2, 24), np.float32)
        for tau in range(24):
            i = tau // 8 + 1
            vb = 1.0 if i <= 3 - j else 0.0
            blk[:, 0, tau] = 1.0 - vb
            blk[:, 1, tau] = vb
        m["blk"] = blk
        in_maps.append({k: np.ascontiguousarray(v, dtype=np.float32) for k, v in m.items()})
    key = (tuple(_dbg) if _dbg else None, _stop)
    if key not in _NC_CACHE:
        _NC_CACHE[key] = Builder(dbg=_dbg, stop=_stop).build()
    nc = _NC_CACHE[key]
    res = run_bass_kernel_spmd(nc, in_maps, core_ids=list(range(8)))
    R = res.results
    y_prompt = np.zeros((16, 256, 1024), np.float32)
    y_sample = np.zeros((2, 4096, 1024), np.float32)
    nk = np.zeros((16, 1, 256, 2, 128), np.float32)
    nv = np.zeros((16, 1, 256, 2, 128), np.float32)
    nC = np.zeros((16, 1, 2, 4, 256, 256), np.float32)
    nn = np.zeros((16, 1, 2, 4, 256), np.float32)
    nm = np.zeros((16, 1, 2, 4), np.float32)
    for r in range(8):
        b, j = r // 4, r % 4
        o = R[r]
        y_prompt[2 * r:2 * r + 2] = o["yp"].reshape(2, 256, 1024)
        y_sample[b, j * 1024:(j + 1) * 1024] = o["ys"]
        nk[2 * r:2 * r + 2, 0] = o["kc"].reshape(2, 256, 2, 128)
        nv[2 * r:2 * r + 2, 0] = o["vc"].reshape(2, 256, 2, 128)
        nC[2 * r:2 * r + 2, 0] = o["Cn"].reshape(2, 2, 4, 256, 256)
        nn[2 * r:2 * r + 2, 0] = o["nn"].reshape(2, 2, 4, 256)
        nm[2 * r:2 * r + 2, 0] = o["mn"].reshape(2, 2, 4)
    if _dbg:
        kernel._dbg_out = [R[r]["dbg"] for r in range(8)]
    return (y_prompt, y_sample, nk, nv, nC, nn, nm)
```

```python
import math
import numpy as np
import concourse.bass as bass
import concourse.mybir as mybir
from concourse.bass_utils import run_bass_kernel_spmd

F32 = mybir.dt.float32
BF16 = mybir.dt.bfloat16
AF = mybir.ActivationFunctionType
ALU = mybir.AluOpType
AX = mybir.AxisListType
PE, ACT, DVE, POOL, SP = "pe", "act", "dve", "pool", "sp"
N_DMA_SLOTS = 8
EPS = 1e-6
ALPHA = 2.0 ** 0.25
LN16 = math.log(16.0)
NEG = -30000.0


class Buf:
    __slots__ = ("name", "w", "r", "excl")

    def __init__(self, name, excl=False):
        self.name = name
        self.w = None
        self.r = {}
        self.excl = excl


class Sched:
    def __init__(self, nc):
        self.nc = nc
        self.lists = {e: [] for e in (PE, ACT, DVE, POOL, SP)}
        self.sig = {e: 0 for e in (PE, ACT, DVE, POOL)}
        self.pending = {e: False for e in (PE, ACT, DVE, POOL)}
        self.waited = {}
        self.dma_n = {}
        self.dma_rr = {SP: 0, POOL: 0, ACT: 0}
        self.out_tokens = []

    def _deps(self, reads, writes, eng=None):
        deps = {}

        def add(tok):
            if tok is None:
                return
            k, v = tok
            if deps.get(k, 0) < v:
                deps[k] = v
        for b in reads:
            add(b.w)
            if b.excl:
                for k, v in b.r.items():
                    if k != eng:
                        add((k, v))
        for b in writes:
            add(b.w)
            for k, v in b.r.items():
                add((k, v))
        return deps

    def _waits(self, eng, deps):
        waits = []
        for k, v in deps.items():
            if k == PE and eng == PE:
                continue
            if k in (PE, ACT, DVE, POOL):
                assert self.sig[k] >= v, f"dependency on unsignalled {k} instruction"
            if self.waited.get((eng, k), 0) >= v:
                continue
            self.waited[(eng, k)] = v
            waits.append((k, v))
        return waits

    def op(self, eng, fn, reads=(), writes=(), signal=True):
        deps = self._deps(reads, writes, eng)
        waits = self._waits(eng, deps)
        if signal:
            self.sig[eng] += 1
            tok = (eng, self.sig[eng])
            self.pending[eng] = False
        else:
            tok = (eng, self.sig[eng] + 1)
            self.pending[eng] = True
        self.lists[eng].append((fn, waits, (eng, 1) if signal else None))
        for b in reads:
            if b.r.get(eng, 0) < tok[1]:
                b.r[eng] = tok[1]
        for b in writes:
            b.w = tok
            b.r = {}
        return tok

    def dma(self, q, fn, reads=(), writes=(), is_output=False):
        slot = self.dma_rr[q]
        self.dma_rr[q] = (slot + 1) % N_DMA_SLOTS
        key = f"dma_{q}_{slot}"
        n = self.dma_n.get(key, 0)
        deps = self._deps(reads, writes)
        if n > 0 and deps.get(key, 0) < 16 * n:
            deps[key] = 16 * n
        waits = self._waits(q, deps)
        self.dma_n[key] = n + 1
        tok = (key, 16 * (n + 1))
        self.lists[q].append((fn, waits, (key, 16)))
        for b in reads:
            if b.r.get(key, 0) < tok[1]:
                b.r[key] = tok[1]
        for b in writes:
            b.w = tok
            b.r = {}
        if is_output:
            self.out_tokens.append(tok)
        return tok

    def barrier(self, mode="all"):
        for e in (PE, ACT, DVE, POOL):
            assert not self.pending[e]
        deps = {e: self.sig[e] for e in (PE, ACT, DVE, POOL) if self.sig[e] > 0}
        if mode != "nodma":
            for k, n in self.dma_n.items():
                deps[k] = 16 * n
        engs = (PE, ACT, DVE, POOL, SP)
        if mode == "nopool":
            engs = (PE, ACT, DVE, SP)
        if mode == "nosp":
            engs = (PE, ACT, DVE, POOL)
        for e in engs:
            d = dict(deps)
            waits = []
            for k, v in d.items():
                if self.waited.get((e, k), 0) >= v:
                    continue
                self.waited[(e, k)] = v
                waits.append((k, v))
            if waits:
                self.lists[e].append((None, waits, None))

    def finish(self):
        deps = {}
        for k, v in self.out_tokens:
            if deps.get(k, 0) < v:
                deps[k] = v
        waits = self._waits(SP, deps)
        self.lists[SP].append((None, waits, None))

    def emit(self):
        nc = self.nc
        keys = set()
        for e, lst in self.lists.items():
            for fn, waits, inc in lst:
                for k, v in waits:
                    keys.add(k)
                if inc is not None:
                    keys.add(inc[0])
        for e in (PE, ACT, DVE, POOL):
            assert not self.pending[e], f"{e} ends with unsignalled instruction"
        sems = {k: nc.alloc_semaphore(f"s_{k}") for k in sorted(keys)}
        lists = self.lists

        def run(engobj, lst):
            for fn, waits, inc in lst:
                for k, v in waits:
                    engobj.wait_ge(sems[k], v)
                if fn is None:
                    continue
                ins = fn(engobj)
                if inc is not None:
                    ins.then_inc(sems[inc[0]], inc[1])

        with nc.Block() as block:
            @block.tensor
            def _(e):
                run(e, lists[PE])

            @block.scalar
            def _(e):
                run(e, lists[ACT])

            @block.vector
            def _(e):
                run(e, lists[DVE])

            @block.gpsimd
            def _(e):
                run(e, lists[POOL])

            @block.sync
            def _(e):
                run(e, lists[SP])


class Group:
    def __init__(self, name, nseq, n_own, n_bnd, n_ctx, n_cache, mset, rope):
        self.name = name
        self.nseq = nseq
        self.n_own = n_own
        self.n_bnd = n_bnd
        self.n_ctx = n_ctx
        self.n_cache = n_cache
        self.mset = mset
        self.rope = rope
        self.is_p = (name == "P")
        self.T_own = n_own * 128
        self.nck = n_own


class StopBuild(Exception):
    pass


class Builder:
    def __init__(self, dbg=None, stop=None):
        self.stop = stop
        self.phase_id = 0
        self.nc = nc = bass.Bass("TRN2", target_bir_lowering=False)
        self.S = Sched(nc)
        self.dbg = dbg
        self.bufs = {}
        self.uid = 0
        self.base = ((nc.sbuf_base + 63) // 64) * 64
        self.lim = nc.sbuf_top
        self.ps = [nc.alloc_psum_tensor(f"psb{i}", [128, 512], F32) for i in range(8)]
        self.cur = self.base
        self.rr = 0

    def B(self, *key):
        b = self.bufs.get(key)
        if b is None:
            b = self.bufs[key] = Buf(str(key), excl=(key[0] == "ps"))
        return b

    def Bs(self, name, *ranges):
        out = [()]
        for r in ranges:
            r = [r] if isinstance(r, int) else list(r)
            out = [o + (i,) for o in out for i in r]
        return [self.B(name, *o) for o in out]

    def sb(self, name, shape, dtype, off=None):
        self.uid += 1
        esz = 4 if dtype == F32 else 2
        n = esz
        for s in shape[1:]:
            n *= s
        n = (n + 63) // 64 * 64
        if off is None:
            o = self.cur
            self.cur += n
        else:
            o = self.base + off
        assert o + n <= self.lim, f"SBUF overflow {name} {o + n - self.lim}"
        t = self.nc.alloc_sbuf_tensor_at(f"{name}_{self.uid}", list(shape), dtype, offset=o)
        return t

    def dram_in(self, name, shape):
        return self.nc.dram_tensor(name, list(shape), F32, kind="ExternalInput").ap()

    def dram_out(self, name, shape):
        return self.nc.dram_tensor(name, list(shape), F32, kind="ExternalOutput").ap()

    def op(self, eng, fn, reads=(), writes=(), signal=True):
        return self.S.op(eng, fn, reads, writes, signal)

    def load(self, out_ap, in_ap, writes, q=SP, reads=()):
        return self.S.dma(q, lambda e: e.dma_start(out=out_ap, in_=in_ap, allow_slow_non_contiguous=True), reads=reads, writes=writes)

    def store(self, out_ap, in_ap, reads):
        return self.S.dma(SP, lambda e: e.dma_start(out=out_ap, in_=in_ap, allow_slow_non_contiguous=True), reads=reads, is_output=True)

    def mm(self, out, lhsT, rhs, start, stop, reads, writes, signal=None, skip=False):
        if signal is None:
            signal = stop
        if skip:
            return self.S.op(PE, lambda e: e.matmul(out, lhsT=lhsT, rhs=rhs, start=start, stop=stop,
                                                    skip_group_check=True), reads, writes, signal)
        return self.S.op(PE, lambda e: e.matmul(out, lhsT=lhsT, rhs=rhs, start=start, stop=stop),
                         reads, writes, signal)

    def tr(self, out, in_, ident, reads, writes, signal=True):
        return self.S.op(PE, lambda e: e.transpose(out=out, in_=in_, identity=ident), reads, writes, signal)

    def act(self, out, in_, func, reads, writes, bias=None, scale=None, accum=None, eng=ACT):
        kw = {}
        if bias is not None:
            kw["bias"] = bias
        if scale is not None:
            kw["scale"] = scale
        if accum is not None:
            kw["accum_out"] = accum
        return self.S.op(ACT, lambda e: e.activation(out=out, in_=in_, func=func, **kw), reads, writes)

    def tt(self, out, in0, in1, op, reads, writes, eng=DVE):
        return self.S.op(eng, lambda e: e.tensor_tensor(out=out, in0=in0, in1=in1, op=op), reads, writes)

    def ts(self, out, in0, s1, s2, op0, op1, reads, writes, eng=DVE):
        if s2 is None:
            return self.S.op(eng, lambda e: e.tensor_scalar(out=out, in0=in0, scalar1=s1, scalar2=None, op0=op0),
                             reads, writes)
        return self.S.op(eng, lambda e: e.tensor_scalar(out=out, in0=in0, scalar1=s1, scalar2=s2, op0=op0, op1=op1),
                         reads, writes)

    def stt(self, out, in0, scalar, in1, op0, op1, reads, writes, eng=DVE):
        return self.S.op(eng, lambda e: e.scalar_tensor_tensor(out=out, in0=in0, scalar=scalar, in1=in1,
                                                               op0=op0, op1=op1), reads, writes)

    def cp(self, out, in_, reads, writes, eng=DVE):
        if eng == ACT:
            return self.S.op(ACT, lambda e: e.activation(out=out, in_=in_, func=AF.Copy), reads, writes)
        return self.S.op(eng, lambda e: e.tensor_copy(out=out, in_=in_), reads, writes)

    def red(self, out, in_, op, reads, writes, eng=DVE):
        return self.S.op(eng, lambda e: e.tensor_reduce(out=out, in_=in_, axis=AX.X, op=op), reads, writes)

    def memset(self, ap, val, writes, eng=DVE):
        return self.S.op(eng, lambda e: e.memset(ap, val), (), writes)

    def alt(self):
        self.rr += 1
        return ACT if (self.rr & 1) else DVE

    def evac_cast(self, out, in_, reads, writes, eng=None):
        eng = eng or self.alt()
        return self.cp(out, in_, reads, writes, eng=eng)

    def build(self):
        nc = self.nc
        I = {}
        I["xp"] = self.dram_in("xp", [512, 1024])
        I["xs"] = self.dram_in("xs", [4096, 1024])
        I["rope"] = self.dram_in("rope", [4096, 128])
        I["ck"] = self.dram_in("ck", [256, 256])
        I["cv"] = self.dram_in("cv", [256, 256])
        I["sC"] = self.dram_in("sC", [8, 256, 256])
        I["sn"] = self.dram_in("sn", [8, 256])
        I["sm"] = self.dram_in("sm", [8])
        I["cT"] = self.dram_in("cT", [128, 16])
        I["blk"] = self.dram_in("blk", [128, 2, 24])
        I["cm"] = self.dram_in("cm", [128, 6, 128])
        I["wmod"] = self.dram_in("wmod", [12, 128, 8 * 512])
        I["bmod"] = self.dram_in("bmod", [6144])
        I["wg"] = self.dram_in("wg", [128, 8 * 16])
        I["bg"] = self.dram_in("bg", [16])
        I["wml"] = self.dram_in("wml", [4, 4, 128, 8 * 256])
        I["wqa"] = self.dram_in("wqa", [2, 128, 8 * 512])
        I["wkva"] = self.dram_in("wkva", [128, 8 * 512])
        I["w5"] = self.dram_in("w5", [8, 128, 4 * 8 * 128])
        I["wout"] = self.dram_in("wout", [2, 128, 8 * 512])
        I["wup"] = self.dram_in("wup", [16, 128, 8 * 256])
        I["wdn"] = self.dram_in("wdn", [4, 128, 8 * 1024])
        I["mng"] = self.dram_in("mng", [1024])
        I["qg"] = self.dram_in("qg", [128])
        I["kg"] = self.dram_in("kg", [128])
        for nme in ("ln1g", "ln1b", "ln2g", "ln2b"):
            I[nme] = self.dram_in(nme, [1024])
        O = {}
        O["yp"] = self.dram_out("yp", [512, 1024])
        O["ys"] = self.dram_out("ys", [1024, 1024])
        O["kc"] = self.dram_out("kc", [512, 256])
        O["vc"] = self.dram_out("vc", [512, 256])
        O["Cn"] = self.dram_out("Cn", [2, 8, 256, 256])
        O["nn"] = self.dram_out("nn", [2, 8, 256])
        O["mn"] = self.dram_out("mn", [2, 8])
        if self.dbg:
            O["dbg"] = self.dram_out("dbg", list(self.dbg))
        self.I, self.O = I, O

        self.cm = self.sb("cm", [128, 6, 128], F32)
        self.identb = self.sb("identb", [128, 128], BF16)
        self.onesb = self.sb("onesb", [128, 2], BF16)
        self.onesb128 = self.sb("onesb128", [128, 128], BF16)
        self.COL = self.sb("COL", [128, 2, 4, 8], F32)
        self.G1 = self.sb("G1", [128, 2, 1024], F32)
        self.G2 = self.sb("G2", [128, 2, 1024], F32)
        self.QG = self.sb("QG", [128, 128], F32)
        self.KG = self.sb("KG", [128, 128], F32)
        self.BG = self.sb("BG", [128, 16], F32)
        self.wg = self.sb("wg", [128, 8, 16], BF16)
        self.arena0 = self.cur - self.base
        bc = self.B("consts")
        self.load(self.cm[:], I["cm"], [bc])
        self.load(self.QG[:], I["qg"].partition_broadcast(128), [bc])
        self.load(self.KG[:], I["kg"].partition_broadcast(128), [bc])
        self.load(self.BG[:], I["bg"].partition_broadcast(128), [bc])
        self.load(self.wg[:], I["wg"].rearrange("p (c n) -> p c n", c=8), [bc], q=POOL)
        self.cp(self.identb[:], self.cm[:, 0, :], [bc], [self.B("identb")])
        self.memset(self.onesb[:], 1.0, [self.B("onesb")])
        self.memset(self.onesb128[:], 1.0, [self.B("onesb128")])
        self.ident = self.cm[:, 0, :]

        try:
            self.build_body()
        except StopBuild:
            pass
        self.S.finish()
        self.S.emit()
        return nc

    def chk(self, label):
        self.phase_id += 1
        if self.stop is not None and self.phase_id >= self.stop:
            print("STOP at", self.phase_id, label)
            raise StopBuild()

    def build_body(self):
        self.phase0()
        import os
        if os.environ.get("DBG_BAR"):
            self.S.barrier(os.environ.get("DBG_BAR"))
        self.chk("phase0")
        gP = Group("P", nseq=2, n_own=2, n_bnd=2, n_ctx=2, n_cache=0, mset=0, rope=False)
        gS = Group("S", nseq=1, n_own=8, n_bnd=24, n_ctx=32, n_cache=2, mset=1, rope=True)
        for g in (gP, gS):
            self.S.barrier()
            self.run_group(g)

    def phase0(self):
        I = self.I
        self.cur = self.base + self.arena0
        MOD = self.sb("MOD", [128, 2, 6144], F32)
        bmod = self.sb("bmodr", [128, 6144], F32)
        cT = self.sb("cT", [128, 8, 2], F32)
        sg = self.sb("sg", [128, 8, 2], F32)
        srep = self.sb("srep", [128, 8, 2, 128], BF16)
        wsl = [self.sb(f"wmods{i}", [128, 8, 512], BF16) for i in range(4)]
        wstage = [self.sb(f"wstage{i}", [128, 8, 512], F32) for i in range(2)]
        l1g = self.sb("l1g", [128, 1024], F32)
        l1b = self.sb("l1b", [128, 1024], F32)
        tmp = self.sb("tmp0", [128, 1024], F32)
        tmp2 = self.sb("tmp02", [128, 8, 128], F32)
        b0 = self.B("p0")
        self.load(bmod[:], I["bmod"].partition_broadcast(128), [self.B("bmod")])
        self.load(cT[:], I["cT"].rearrange("p (c s) -> p c s", s=2), [self.B("cT")])
        self.load(l1g[:], I["ln1g"].partition_broadcast(128), [self.B("l1g")])
        self.load(l1b[:], I["ln1b"].partition_broadcast(128), [self.B("l1b")])
        self.act(sg[:], cT[:], AF.Sigmoid, [self.B("cT")], [self.B("sg")])
        self.tt(sg[:], sg[:], cT[:], ALU.mult, [self.B("cT"), self.B("sg")], [self.B("sg")])
        self.cp(srep[:].rearrange("p c s n -> p (c s) n"),
                sg[:].rearrange("p c s -> p (c s)").unsqueeze(2).to_broadcast([128, 16, 128]),
                [self.B("sg")], [self.B("srep")])
        for j in range(12):
            w = wsl[j % 4]
            bw = self.B("wmods", j % 4)
            if j % 2 == 0:
                self.load(w[:], I["wmod"][j].rearrange("p (c n) -> p c n", c=8), [bw], q=POOL)
            else:
                stg = wstage[(j // 2) % 2]
                bst = self.B("wstage", (j // 2) % 2)
                self.load(stg[:], I["wmod"][j].rearrange("p (c n) -> p c n", c=8), [bst], q=SP)
                self.cp(w[:], stg[:], [bst], [bw])
            for s in range(2):
                pb = self.B("ps", (2 * j + s) % 4)
                pt = self.ps[(2 * j + s) % 4]
                for c in range(8):
                    self.mm(pt[:], srep[:, c, s, :], w[:, c, :], c == 0, c == 7, [bw, self.B("srep")], [pb])
                self.tt(MOD[:, s, j * 512:(j + 1) * 512], pt[:], bmod[:, j * 512:(j + 1) * 512], ALU.add,
                        [pb, self.B("bmod")], [self.B("MOD", s, j)])
        for s in range(2):
            def row(k):
                return MOD[:, s, k * 1024:(k + 1) * 1024]

            def rb(k):
                return [self.B("MOD", s, 2 * k), self.B("MOD", s, 2 * k + 1)]
            self.cp(self.G1[:, s, :], row(2), rb(2), [self.B("G1", s)], eng=ACT)
            self.cp(self.G2[:, s, :], row(5), rb(5), [self.B("G2", s)], eng=ACT)
            bt, bt2 = self.B("p0tmp"), self.B("p0tmp2")
            identB = self.cm[:, 0, :].unsqueeze(1).to_broadcast([128, 8, 128])

            def diag(dst, src_ap, reads):
                self.tt(tmp2[:], src_ap.rearrange("p (c n) -> p c n", c=8), identB, ALU.mult,
                        reads + [self.B("consts")], [bt2])
                self.red(dst, tmp2[:], ALU.add, [bt2], [self.B("COL", s)])
            self.ts(tmp[:], row(1), 1.0, None, ALU.add, None, rb(1), [bt])
            diag(self.COL[:, s, 0, :], tmp[:], [bt])
            diag(self.COL[:, s, 1, :], row(0), rb(0))
            self.ts(tmp[:], row(4), 1.0, None, ALU.add, None, rb(4), [bt])
            self.tt(row(4), tmp[:], l1g[:], ALU.mult, [bt, self.B("l1g")], rb(4))
            diag(self.COL[:, s, 2, :], row(4), rb(4))
            self.tt(tmp[:], tmp[:], l1b[:], ALU.mult, [bt, self.B("l1b")], [bt])
            self.tt(tmp[:], tmp[:], row(3), ALU.add, [bt] + rb(3), [bt])
            diag(self.COL[:, s, 3, :], tmp[:], [bt])

    def run_group(self, g):
        I, O = self.I, self.O
        KB = 1024
        a0 = self.arena0
        nseq, n_own, nck, T_own = g.nseq, g.n_own, g.nck, g.T_own
        TT = nseq * T_own
        T_ctx = g.n_ctx * 128
        T_keys = (g.n_ctx + g.n_cache) * 128
        nkt = g.n_ctx + g.n_cache
        s = g.mset
        hT = self.sb("hT", [128, 8, TT], BF16, off=a0)
        hmT = self.sb("hmT", [128, 8, TT], BF16, off=a0 + 16 * KB)
        haT = self.sb("haT", [128, 8, TT], BF16, off=a0 + 32 * KB)
        self.cur = self.base + a0 + 48 * KB
        KaT = self.sb("KaT", [128, nseq, 2, T_keys], BF16)
        Va = self.sb("Va", [128, nseq, nkt, 2, 129], BF16)
        G128 = self.sb("G128", [128, nseq, g.n_bnd, 16], F32)
        G64 = self.sb("G64", [128, nseq, nck, 16], F32)
        W8 = self.sb("W8", [128, nseq, g.n_bnd, 8], F32)
        U64 = self.sb("U64", [128, nseq, nck, 8], F32)
        E64 = self.sb("E64", [128, nseq, nck, 8], F32)
        Z64 = self.sb("Z64", [128, nseq, nck, 8], F32)
        DEC = self.sb("DEC", [128, nseq, nck, 8], F32)
        SCI = self.sb("SCI", [128, 8], F32)
        C0 = self.sb("C0", [128, nseq, 8, 2, 257], F32)
        mark_common = self.cur

        def own_cols(q, t):
            o = q * T_own + t * 128
            return slice(o, o + 128)

        hTo = self.sb("hTo", [128, 8, nseq * (g.n_ctx - n_own) * 128 if not g.is_p else 16], BF16) if not g.is_p else None
        mark2 = self.cur
        xin = [self.sb(f"xin{i}", [128, 1024], F32) for i in range(2)]
        xsrc = I["xp"] if g.is_p else I["xs"]
        sc1 = self.COL[:, s, 0, :]
        sh1 = self.COL[:, s, 1, :]
        bcol = self.B("COL", s)
        cnt = 0
        gcnt = 0
        PG = self.ps[7]
        hview = {}
        import os
        DBG_NT = int(os.environ.get("DBG_NT", "999"))
        DBG_NOG = int(os.environ.get("DBG_NOG", "0"))
        for q in range(nseq):
            for t in range(min(g.n_ctx, DBG_NT)):
                xi = xin[cnt % 2]
                bx = self.B("xin", cnt % 2)
                row0 = (q * g.n_ctx + t) * 128
                self.load(xi[:], xsrc[row0:row0 + 128, :], [bx])
                pbank = [self.ps[(cnt % 2) * 2], self.ps[(cnt % 2) * 2 + 1]]
                pb = [self.B("ps", (cnt % 2) * 2), self.B("ps", (cnt % 2) * 2 + 1)]
                if t < n_own:
                    dst, off = hT, q * T_own + t * 128
                else:
                    dst, off = hTo, (q * (g.n_ctx - n_own) + (t - n_own)) * 128
                hview[(q, t)] = (dst, off)
                for c in range(8):
                    self.tr(pbank[c // 4][:, (c % 4) * 128:(c % 4 + 1) * 128], xi[:, c * 128:(c + 1) * 128],
                            self.ident, [bx, self.B("consts")], [pb[c // 4]], signal=(c % 4 == 3))
                for c in range(8):
                    src = pbank[c // 4][:, (c % 4) * 128:(c % 4 + 1) * 128]
                    o_ap = dst[:, c, off:off + 128]
                    bh = self.B("hT", g.name, q, t, c)
                    if c // 4 == 0 or os.environ.get("DBG_ACTONLY"):
                        self.S.op(ACT, lambda e, o_ap=o_ap, src=src, c=c: e.activation(
                            out=o_ap, in_=src, func=AF.Identity, scale=sc1[:, c:c + 1], bias=sh1[:, c:c + 1]),
                            [pb[c // 4], bcol], [bh])
                    else:
                        self.stt(o_ap, src, sc1[:, c:c + 1], sh1[:, c:c + 1].to_broadcast([128, 128]), ALU.mult, ALU.add,
                                 [pb[c // 4], bcol], [bh])
                cnt += 1
                bidx = t if g.is_p else (t - n_own)
                if DBG_NOG:
                    continue
                if 0 <= bidx < g.n_bnd:
                    sl = 0
                    PG = self.ps[4 + gcnt % 4]
                    pgb = self.B("ps", 4 + gcnt % 4)
                    gcnt += 1
                    for c in range(8):
                        self.mm(PG[:, sl * 16:(sl + 1) * 16], dst[:, c, off:off + 128], self.wg[:, c, :],
                                c == 0, c == 7, [self.B("hT", g.name, q, t, c), self.B("consts")], [pgb])
                    self.tt(G128[:, q, bidx, :], PG[:, sl * 16:(sl + 1) * 16], self.BG[:], ALU.add,
                            [pgb, self.B("consts")], [self.B("G128", q)])
                if t < n_own:
                    sl = 0
                    PG = self.ps[4 + gcnt % 4]
                    pgb = self.B("ps", 4 + gcnt % 4)
                    gcnt += 1
                    for c in range(8):
                        self.mm(PG[:, 0:16], dst[:, c, off:off + 128], self.wg[:, c, :], c == 0, c == 7,
                                [self.B("hT", g.name, q, t, c), self.B("consts")], [pgb])
                    self.tt(G64[:, q, t, :], PG[:, 0:16], self.BG[:], ALU.add,
                            [pgb, self.B("consts")], [self.B("G64", q)])

        def hT_bufs(q, t):
            return [self.B("hT", g.name, q, t, c) for c in range(8)]

        self.chk(g.name + " phase1")
        reuse = not g.is_p
        if reuse:
            self.S.barrier()
            self.cur = mark2
        for q in range(nseq):
            self.gates(g, q, G128, G64, W8, U64, E64, Z64, DEC, SCI)
        if reuse:
            self.S.barrier()
            self.cur = mark2

        self.chk(g.name + " gates")
        self.kv_proj(g, hview, hT_bufs, KaT, Va)
        if reuse:
            self.S.barrier()
            self.cur = mark2

        self.chk(g.name + " kv")
        self.boundary(g, hview, hT_bufs, W8, SCI, C0)
        self.S.barrier()
        self.cur = mark_common

        self.chk(g.name + " boundary")
        self.mlstm_own(g, hT, hT_bufs, U64, E64, Z64, DEC, C0, hmT)
        self.S.barrier()
        self.cur = mark_common

        self.chk(g.name + " mlstm")
        self.attention(g, hT, hT_bufs, KaT, Va, haT)

        self.chk(g.name + " attention")
        self.S.barrier()
        self.cur = self.base + a0 + 48 * KB
        self.post(g, hT, hmT, haT)
        self.chk(g.name + " post")

    def gates(self, g, q, G128, G64, W8, U64, E64, Z64, DEC, SCI):
        I, O = self.I, self.O
        n = g.n_bnd
        nck = g.nck
        cmB = self.B("consts")
        Lst, Ust, ONES = self.cm[:, 3, :], self.cm[:, 4, :], self.cm[:, 5, :]
        Uf, Ub = self.cm[:, 1, :], self.cm[:, 2, :]
        NLF = self.sb("NLF", [128, n, 8], F32)
        TTs = self.sb("TTs", [128, n, 8], F32)
        OFF = self.sb("OFF", [128, n, 8], F32)
        X = self.sb("X", [128, n, 8], F32)
        bG = self.B("G128", q)
        bN, bT, bO, bX = self.B("NLF", q), self.B("TTs", q), self.B("OFF", q), self.B("X", q)
        self.act(NLF[:], G128[:, q, :, 8:16], AF.Exp, [bG], [bN], scale=-1.0)
        self.act(NLF[:], NLF[:], AF.Ln, [bN], [bN], bias=1.0)
        pt = self.ps[4]
        pb = self.B("ps", 4)
        ecf = pt[:, 0:n * 4].rearrange("p (t h) -> p t h", h=4)
        ecb = pt[:, n * 4:n * 8].rearrange("p (t h) -> p t h", h=4)
        ttv = pt[:, n * 8:n * 16].rearrange("p (t h) -> p t h", h=8)
        self.mm(ecf, Ust, NLF[:, :, 0:4], True, True, [bN, cmB], [pb])
        self.mm(ecb, Lst, NLF[:, :, 4:8], True, True, [bN, cmB], [pb])
        self.mm(ttv, ONES, NLF[:], True, True, [bN, cmB], [pb])
        self.cp(TTs[:], ttv, [pb], [bT])
        self.memset(OFF[:], 0.0, [bO])
        for t in range(n - 2, -1, -1):
            self.tt(OFF[:, t, 0:4], OFF[:, t + 1, 0:4], TTs[:, t + 1, 0:4], ALU.add, [bO, bT], [bO])
        for t in range(1, n):
            self.tt(OFF[:, t, 4:8], OFF[:, t - 1, 4:8], TTs[:, t - 1, 4:8], ALU.add, [bO, bT], [bO])
        self.tt(X[:, :, 0:4], G128[:, q, :, 0:4], ecf, ALU.subtract, [bG, pb], [bX])
        self.tt(X[:, :, 4:8], G128[:, q, :, 4:8], ecb, ALU.subtract, [bG, pb], [bX])
        self.tt(X[:], X[:], OFF[:], ALU.subtract, [bX, bO], [bX])
        if not g.is_p:
            blk = self.sb("blk", [128, 2, 24], F32)
            bB = self.B("blk")
            self.load(blk[:], I["blk"], [bB])
            bW = self.B("W8", q)
            self.act(W8[:, q, :, :], X[:], AF.Exp, [bX], [bW], bias=-LN16)
            self.tt(W8[:, q, :, 0:4], W8[:, q, :, 0:4], blk[:, 0, :].unsqueeze(2).to_broadcast([128, n, 4]), ALU.mult,
                    [bW, bB], [bW])
            self.tt(W8[:, q, :, 4:8], W8[:, q, :, 4:8], blk[:, 1, :].unsqueeze(2).to_broadcast([128, n, 4]), ALU.mult,
                    [bW, bB], [bW])
            tmpd = self.sb("tmpd", [128, 8, n], F32)
            sm = self.sb("smr", [128, 8], F32)
            bD = self.B("tmpd")
            self.load(sm[:], I["sm"].partition_broadcast(128), [self.B("smr")])
            self.tt(tmpd[:, 0:4, :], TTs[:, :, 0:4].rearrange("p t h -> p h t"),
                    blk[:, 0, :].unsqueeze(1).to_broadcast([128, 4, n]), ALU.mult, [bT, bB], [bD])
            self.tt(tmpd[:, 4:8, :], TTs[:, :, 4:8].rearrange("p t h -> p h t"),
                    blk[:, 1, :].unsqueeze(1).to_broadcast([128, 4, n]), ALU.mult, [bT, bB], [bD])
            self.red(SCI[:], tmpd[:], ALU.add, [bD], [self.B("SCI")])
            self.tt(SCI[:], sm[:], SCI[:], ALU.subtract, [self.B("SCI"), self.B("smr")], [self.B("SCI")])
            self.act(SCI[:], SCI[:], AF.Exp, [self.B("SCI")], [self.B("SCI")])
        else:
            assert n == 2
            pt2 = self.ps[5]
            pb2 = self.B("ps", 5)
            mx = self.sb("mx", [16, 1], F32)
            mxb = self.sb("mxb", [16, 128], F32)
            MF = self.sb("MF", [128, 8], F32)
            GT = self.sb("GT", [128, 8], F32)
            bM = self.B("mfin", q)
            self.tr(pt2[0:16, 0:128], X[:].rearrange("p t h -> p (t h)"), self.ident, [bX, cmB], [pb2])
            self.red(mx[:], pt2[0:16, 0:128], ALU.max, [pb2], [bM])
            self.cp(mxb[:], mx[:].to_broadcast([16, 128]), [bM], [bM])
            self.tr(pt2[:, 128:144], mxb[:], self.cm[0:16, 0, 0:16], [bM, cmB], [pb2])
            MX = self.sb("MX", [128, 16], F32)
            self.cp(MX[:], pt2[:, 128:144], [pb2], [bM])
            self.tt(MF[:], MX[:, 0:8], MX[:, 8:16], ALU.max, [bM], [bM])
            self.tt(GT[:], TTs[:, 0, :], TTs[:, 1, :], ALU.add, [bT], [bM])
            self.stt(MF[:], GT[:], -1.0, MF[:], ALU.mult, ALU.max, [bM], [bM])
            self.tt(X[:], X[:], MF[:].unsqueeze(1).to_broadcast([128, 2, 8]), ALU.subtract, [bX, bM], [bX])
            self.act(W8[:, q, :, :], X[:], AF.Exp, [bX], [self.B("W8", q)], bias=-LN16)
            self.store(O["mn"][q:q + 1, :], MF[0:1, :], [bM])
        N64 = self.sb("N64", [128, nck, 8], F32)
        A64 = self.sb("A64", [128, nck, 8], F32)
        bG6 = self.B("G64", q)
        bN6, bA6 = self.B("N64", q), self.B("A64", q)
        self.act(N64[:], G64[:, q, :, 8:16], AF.Exp, [bG6], [bN6], scale=-1.0)
        self.act(N64[:], N64[:], AF.Ln, [bN6], [bN6], bias=1.0)
        pt3 = self.ps[6]
        pb3 = self.B("ps", 6)
        bcf = pt3[:, 0:nck * 4].rearrange("p (t h) -> p t h", h=4)
        bcb = pt3[:, nck * 4:nck * 8].rearrange("p (t h) -> p t h", h=4)
        t64 = pt3[:, nck * 8:nck * 16].rearrange("p (t h) -> p t h", h=8)
        t128 = pt3[:, nck * 16:nck * 24].rearrange("p (t h) -> p t h", h=8)
        self.mm(bcf, Uf, N64[:, :, 0:4], True, True, [bN6, cmB], [pb3])
        self.mm(bcb, Ub, N64[:, :, 4:8], True, True, [bN6, cmB], [pb3])
        self.mm(t64, ONES, N64[:], True, True, [bN6, cmB], [pb3])
        self.mm(t128, ONES, N64[:], True, True, [bN6, cmB], [pb3])
        self.tt(A64[:, :, 0:4], G64[:, q, :, 0:4], bcf, ALU.add, [bG6, pb3], [bA6])
        self.tt(A64[:, :, 4:8], G64[:, q, :, 4:8], bcb, ALU.add, [bG6, pb3], [bA6])
        self.act(U64[:, q, :, :], A64[:], AF.Exp, [bA6], [self.B("U64", q)], bias=-LN16)
        self.tt(A64[:], A64[:], t64, ALU.subtract, [bA6, pb3], [bA6])
        self.act(Z64[:, q, :, :], A64[:], AF.Exp, [bA6], [self.B("Z64", q)], bias=-LN16)
        self.act(E64[:, q, :, 0:4], bcf, AF.Exp, [pb3], [self.B("E64", q)], scale=-1.0)
        self.act(E64[:, q, :, 4:8], bcb, AF.Exp, [pb3], [self.B("E64", q)], scale=-1.0)
        self.act(DEC[:, q, :, :], t128, AF.Exp, [pb3], [self.B("DEC", q)], scale=-1.0)

    def interleave(self, gens, width, stagger=0):
        active = []
        it = iter(gens)
        since = stagger
        done = False
        while True:
            if not done and len(active) < width and since >= stagger:
                try:
                    active.append(next(it))
                    since = 0
                except StopIteration:
                    done = True
            if not active:
                if done:
                    break
                since = stagger
                continue
            since += 1
            for gq in list(active):
                try:
                    next(gq)
                except StopIteration:
                    active.remove(gq)

    def rope_apply(self, xt, H, rt, bR, bx, tmp1, tmp2, btmp):
        cosv = rt[:, 0:64].rearrange("p (a f) -> p a f", a=2).unsqueeze(1).to_broadcast([128, H, 2, 32])
        sinv = rt[:, 64:128].rearrange("p (a f) -> p a f", a=2).unsqueeze(1).to_broadcast([128, H, 2, 32])
        xv = xt.rearrange("p h (a k f) -> p h a k f", a=2, k=2)
        t1 = tmp1.rearrange("p h (a k f) -> p h a k f", a=2, k=2)
        t2 = tmp2.rearrange("p h (a k f) -> p h a k f", a=2, k=2)
        for k in range(2):
            self.tt(t1[:, :, :, k, :], xv[:, :, :, k, :], cosv, ALU.mult, [bx, bR], [btmp])
            yield
            self.tt(t2[:, :, :, k, :], xv[:, :, :, 1 - k, :], sinv, ALU.mult, [bx, bR], [btmp])
            yield
        self.tt(xv[:, :, :, 0, :], t1[:, :, :, 0, :], t2[:, :, :, 0, :], ALU.subtract, [btmp], [bx])
        yield
        self.tt(xv[:, :, :, 1, :], t1[:, :, :, 1, :], t2[:, :, :, 1, :], ALU.add, [btmp], [bx])
        yield

    def rms_heads(self, src_ps, H, gain, dst, pbs, bdst, scr, ss, bss):
        for h in range(H):
            self.S.op(ACT, lambda e, h=h: e.activation(out=scr[:, h, :], in_=src_ps[:, h, :], func=AF.Square,
                                                       accum_out=ss[:, h:h + 1]), pbs, [bss])
        yield
        self.act(ss[:, 0:H], ss[:, 0:H], AF.Sqrt, [bss], [bss], bias=EPS, scale=1.0 / 128.0)
        yield
        self.S.op(DVE, lambda e: e.reciprocal(out=ss[:, 0:H], in_=ss[:, 0:H]), [bss], [bss])
        yield
        self.tt(dst, src_ps, ss[:, 0:H].unsqueeze(2).to_broadcast([128, H, 128]), ALU.mult, pbs + [bss], [bdst])
        yield
        self.tt(dst, dst, gain.unsqueeze(1).to_broadcast([128, H, 128]), ALU.mult, [bdst, self.B("consts")], [bdst])
        yield

    def kv_proj(self, g, hview, hT_bufs, KaT, Va):
        I, O = self.I, self.O
        w = self.sb("wkva", [128, 8, 512], BF16)
        bw = self.B("wkva")
        self.load(w[:], I["wkva"].rearrange("p (c n) -> p c n", c=8), [bw], q=POOL)
        NS = 3
        kn = [self.sb(f"kn{i}", [128, 2, 128], F32) for i in range(NS)]
        vf = [self.sb(f"vf{i}", [128, 256], F32) for i in range(NS)]
        kb = [self.sb(f"kb{i}", [128, 2, 128], BF16) for i in range(NS)]
        scr = [self.sb(f"kscr{i}", [128, 2, 128], F32) for i in range(NS)]
        ss = [self.sb(f"kss{i}", [128, 2], F32) for i in range(NS)]
        t1 = [self.sb(f"rt1{i}", [128, 2, 128], F32) for i in range(NS)]
        t2 = [self.sb(f"rt2{i}", [128, 2, 128], F32) for i in range(NS)]
        rts = [self.sb(f"rts{i}", [128, 128], F32) for i in range(NS)]
        self.memset(Va[:, :, :, :, 128:129], 1.0, [self.B("Vaones")])

        def tile_job(q, t, i2):
            dst, off = hview[(q, t)]
            pt = self.ps[i2]
            pb = self.B("ps", i2)
            if g.rope:
                self.load(rts[i2][:], I["rope"][t * 128:(t + 1) * 128, :], [self.B("rts", i2)])
            for c in range(8):
                self.mm(pt[:], dst[:, c, off:off + 128], w[:, c, :], c == 0, c == 7, hT_bufs(q, t) + [bw], [pb])
            yield
            kps = pt[:, 0:256].rearrange("p (h d) -> p h d", h=2)
            bk, bv, bs = self.B("kn", i2), self.B("vf", i2), self.B("kss", i2)
            yield from self.rms_heads(kps, 2, self.KG[:], kn[i2][:], [pb], bk, scr[i2], ss[i2], bs)
            if g.is_p:
                self.cp(vf[i2][:], pt[:, 256:512], [pb], [bv], eng=ACT)
                r0 = q * 256 + t * 128
                self.store(O["kc"][r0:r0 + 128, :], kn[i2][:].rearrange("p h d -> p (h d)"), [bk])
                self.store(O["vc"][r0:r0 + 128, :], vf[i2][:], [bv])
            self.cp(Va[:, q, t, :, 0:128], pt[:, 256:512].rearrange("p (h d) -> p h d", h=2),
                    [pb], [self.B("Va", q, t), self.B("Vaones")], eng=ACT)
            yield
            if g.rope:
                yield from self.rope_apply(kn[i2][:], 2, rts[i2], self.B("rts", i2), bk, t1[i2][:], t2[i2][:],
                                           self.B("rtmp", i2))
            bkb = self.B("kb", i2)
            self.cp(kb[i2][:], kn[i2][:], [bk], [bkb])
            yield
            ptr = self.ps[3 + i2]
            pbt = self.B("ps", 3 + i2)
            ptv = ptr[:].bitcast(BF16)
            for h in range(2):
                self.tr(ptv[:, h * 128:(h + 1) * 128], kb[i2][:, h, :], self.identb[:],
                        [bkb, self.B("identb")], [pbt], signal=(h == 1))
            yield
            self.cp(KaT[:, q, :, t * 128:(t + 1) * 128], ptv[:, 0:256].rearrange("p (h n) -> p h n", h=2),
                    [pbt], [self.B("KaT", q, t)])
            yield

        def jobs():
            cnt = 0
            for q in range(g.nseq):
                for t in range(g.n_ctx):
                    yield tile_job(q, t, cnt % NS)
                    cnt += 1
        self.interleave(jobs(), NS, stagger=5)
        for q in range(g.nseq):
            if g.n_cache:
                ckf = self.sb("ckf", [128, 2, 256], F32)
                cvf = self.sb("cvf", [128, 2, 256], F32)
                ckb = self.sb("ckb", [128, 2, 256], BF16)
                self.load(ckf[:], I["ck"].rearrange("(t p) n -> p t n", p=128), [self.B("ckf")])
                self.load(cvf[:], I["cv"].rearrange("(t p) n -> p t n", p=128), [self.B("cvf")])
                self.cp(ckb[:], ckf[:], [self.B("ckf")], [self.B("ckb")])
                for tc in range(g.n_cache):
                    t = g.n_ctx + tc
                    self.cp(Va[:, q, t, :, 0:128], cvf[:, tc, :].rearrange("p (h d) -> p h d", h=2),
                            [self.B("cvf")], [self.B("Va", q, t), self.B("Vaones")])
                    ptr = self.ps[6 + tc % 2]
                    pbt = self.B("ps", 6 + tc % 2)
                    ptv = ptr[:].bitcast(BF16)
                    for h in range(2):
                        self.tr(ptv[:, h * 128:(h + 1) * 128], ckb[:, tc, h * 128:(h + 1) * 128], self.identb[:],
                                [self.B("ckb"), self.B("identb")], [pbt], signal=(h == 1))
                    self.cp(KaT[:, q, :, t * 128:(t + 1) * 128], ptv[:, 0:256].rearrange("p (h n) -> p h n", h=2),
                            [pbt], [self.B("KaT", q, t)])

    def boundary(self, g, hview, hT_bufs, W8, SCI, C0):
        I, O = self.I, self.O
        n = g.n_bnd
        n_own = g.n_own
        wk = [self.sb(f"bwk{i}", [128, 8, 256], BF16) for i in range(2)]
        wv = [self.sb(f"bwv{i}", [128, 8, 256], BF16) for i in range(2)]
        vb = [self.sb(f"bvb{i}", [128, 256], BF16) for i in range(3)]
        kp = [[self.sb(f"bkp{d}{i}", [128, 256], BF16) for i in range(3)] for d in range(2)]
        nsb = self.sb("bnsb", [128, 2, 2], F32)
        cinit = [self.sb(f"cinit{i}", [128, 2, 256], F32) for i in range(2)]
        ninit = [self.sb(f"ninit{i}", [128, 2], F32) for i in range(2)]
        cout = [self.sb(f"cout{i}", [128, 2, 256], F32) for i in range(2)]
        NS = 3
        ptb = [0, 1, 5]
        tiles = []
        for h in range(4):
            for q in range(g.nseq):
                for bi in range(n):
                    tiles.append((h, q, bi))
        state = {"ci": 0}

        def front(k):
            h, q, bi = tiles[k]
            i2 = h % 2
            bwk, bwv = self.B("bwk", i2), self.B("bwv", i2)
            if q == 0 and bi == 0:
                self.load(wk[i2][:], I["wml"][h, 1].rearrange("p (c n) -> p c n", c=8), [bwk], q=POOL)
                self.load(wv[i2][:], I["wml"][h, 2].rearrange("p (c n) -> p c n", c=8), [bwv], q=POOL)
            t = bi if g.is_p else n_own + bi
            dst, off = hview[(q, t)]
            j2 = k % NS
            pt = self.ps[ptb[j2]]
            pb = self.B("ps", ptb[j2])
            for c in range(8):
                self.mm(pt[:, 0:256], dst[:, c, off:off + 128], wk[i2][:, c, :], c == 0, c == 7,
                        hT_bufs(q, t) + [bwk], [pb], signal=False)
            for c in range(8):
                self.mm(pt[:, 256:512], dst[:, c, off:off + 128], wv[i2][:, c, :], c == 0, c == 7,
                        hT_bufs(q, t) + [bwv], [pb])
            self.cp(vb[j2][:], pt[:, 256:512], [pb], [self.B("bvb", j2)], eng=ACT)
            for d in range(2):
                self.ts(kp[d][j2][:], pt[:, 0:256], W8[:, q, bi, d * 4 + h:d * 4 + h + 1], None, ALU.mult, None,
                        [pb, self.B("W8", q)], [self.B("bkp", d, j2)])

        def back(k):
            h, q, bi = tiles[k]
            j2 = k % NS
            par = (h * g.nseq + q) % 2
            cacc = [self.ps[2 + par * 4], self.ps[3 + par * 4]]
            pbc = [self.B("ps", 2 + par * 4), self.B("ps", 3 + par * 4)]
            nacc = self.ps[4]
            pbn = self.B("ps", 4)
            no = par * 4
            bvb = self.B("bvb", j2)
            for d in range(2):
                bkp = self.B("bkp", d, j2)
                for dc in range(2):
                    self.mm(cacc[d][:, dc * 256:(dc + 1) * 256], kp[d][j2][:, dc * 128:(dc + 1) * 128],
                            vb[j2][:], bi == 0 and dc == 0, bi == n - 1, [bkp, bvb], [pbc[d]], signal=False,
                            skip=True)
                    self.mm(nacc[:, no + d * 2 + dc:no + d * 2 + dc + 1], kp[d][j2][:, dc * 128:(dc + 1) * 128],
                            self.onesb[:, 0:1], bi == 0 and dc == 0 and d == 0, bi == n - 1,
                            [bkp, self.B("onesb")], [pbn], signal=(d == 1 and dc == 1), skip=True)
            if bi < n - 1:
                return
            for d in range(2):
                hd = d * 4 + h
                bC0 = self.B("C0", q, hd)
                cv = cacc[d][:].rearrange("p (c e) -> p c e", c=2)
                ci = state["ci"]
                state["ci"] += 1
                if g.is_p:
                    co = cout[ci % 2]
                    bco = self.B("cout", ci % 2)
                    self.cp(co[:], cv, [pbc[d]], [bco], eng=ACT)
                    self.store(O["Cn"][q, hd].rearrange("(c p) e -> p c e", p=128), co[:], [bco])
                    bns = self.B("bnsb", ci % 2)
                    self.cp(nsb[:, ci % 2, 0:2], nacc[:, no + d * 2:no + d * 2 + 2], [pbn], [bns])
                    self.store(O["nn"][q, hd].rearrange("(c p) -> p c", p=128), nsb[:, ci % 2, 0:2], [bns])
                    self.memset(C0[:, q, hd, :, :], 0.0, [bC0])
                else:
                    cn = cinit[ci % 2]
                    nn_ = ninit[ci % 2]
                    bci = self.B("cinit", ci % 2)
                    self.load(cn[:], I["sC"][hd].rearrange("(c p) e -> p c e", p=128), [bci])
                    self.load(nn_[:], I["sn"][hd].rearrange("(c p) -> p c", p=128), [bci])
                    self.stt(C0[:, q, hd, :, 0:256], cn[:], SCI[:, hd:hd + 1], cv, ALU.mult, ALU.add,
                             [bci, self.B("SCI"), pbc[d]], [bC0])
                    self.stt(C0[:, q, hd, :, 256], nn_[:], SCI[:, hd:hd + 1], nacc[:, no + d * 2:no + d * 2 + 2],
                             ALU.mult, ALU.add, [bci, self.B("SCI"), pbn], [bC0])

        LA = 2
        for k in range(min(LA, len(tiles))):
            front(k)
        for k in range(len(tiles)):
            if k + LA < len(tiles):
                front(k + LA)
            back(k)

    def mlstm_own(self, g, hT, hT_bufs, U64, E64, Z64, DEC, C0, hmT):
        I, O = self.I, self.O
        nseq, n_own, nck, T_own = g.nseq, g.n_own, g.nck, g.T_own
        TT = nseq * T_own
        NC = nseq * nck
        NW = 8 if g.is_p else 7
        wpr = [self.sb(f"mw{i}", [128, 8, 256], BF16) for i in range(NW)]
        qT = self.sb("qT", [128, 2, TT], BF16)
        kT = self.sb("kT", [128, 2, TT], BF16)
        ktok = self.sb("ktok", [128, NC, 256], BF16)
        vext = self.sb("vext", [128, NC, 257], BF16)
        sigo = self.sb("sigo", [128, NC, 256], BF16)
        hraw = [self.sb(f"hraw{d}", [128, NC, 257], F32) for d in range(2)]
        hm = hraw[0]
        Ecp = self.sb("Ecp", [128, 2, NC], F32)
        Rd = self.sb("Rd", [128, 2, NC], F32)
        Rd2 = self.sb("Rd2", [128, 2, NC], F32)
        Cst = [[self.sb(f"Cst{q}{d}", [128, 2, 257], F32) for d in range(2)] for q in range(nseq)]
        Cbf = [[[self.sb(f"Cbf{q}{d}{i}", [128, 2, 257], BF16) for i in range(2)] for d in range(2)] for q in range(nseq)]
        PTall = self.sb("PTall", [128, 2, NC, 128], BF16)
        kpr = [[[self.sb(f"kpr{q}{d}{i}", [128, 256], BF16) for i in range(2)] for d in range(2)] for q in range(nseq)]
        sm = [[self.sb(f"sm{d}{i}", [128, 4], F32) for i in range(2)] for d in range(2)]
        mng = self.sb("mng", [128, 256], F32)
        hsq = self.sb("hsq", [128, 256], F32)
        ssq = self.sb("ssq", [128, NC], F32)
        hn = [self.sb(f"hn{i}", [128, 256], BF16) for i in range(2)]
        self.memset(vext[:, :, 256:257], 1.0, [self.B("vones")])
        Uf, Ub = self.cm[:, 1, :], self.cm[:, 2, :]
        bvo = self.B("vones")
        loaded = set()
        for h in range(4):
            slots = [(h * 4 + pc) % NW for pc in range(4)]
            bw = [self.B("mw", sl_) for sl_ in slots]
            wp = [wpr[sl_] for sl_ in slots]
            self.load(mng[:], I["mng"][h * 256:(h + 1) * 256].partition_broadcast(128), [self.B("mng")])
            for pc in range(4):
                if (h, pc) not in loaded:
                    self.load(wp[pc][:], I["wml"][h, pc].rearrange("p (c n) -> p c n", c=8), [bw[pc]], q=POOL)
                    loaded.add((h, pc))
            if h + 1 < 4:
                for pc in range(NW - 4):
                    sl_ = ((h + 1) * 4 + pc) % NW
                    self.load(wpr[sl_][:], I["wml"][h + 1, pc].rearrange("p (c n) -> p c n", c=8),
                              [self.B("mw", sl_)], q=POOL)
                    loaded.add((h + 1, pc))
            wq_, wk_, wv_, wo_ = wp
            blk = 512 if TT >= 512 else TT
            cnt = 0
            for (wsrc, bws, dstT, nm) in ((wq_, bw[0], qT, "qT"), (wk_, bw[1], kT, "kT")):
                for dc in range(2):
                    for b0 in range(0, TT, blk):
                        pt = self.ps[cnt % 2]
                        pb = self.B("ps", cnt % 2)
                        cnt += 1
                        tl = range(b0 // 128, (b0 + blk) // 128)
                        rd = [self.B("hT", g.name, tt_ // n_own, tt_ % n_own, c) for tt_ in tl for c in range(8)]
                        for c in range(8):
                            self.mm(pt[:, 0:blk], wsrc[:, c, dc * 128:(dc + 1) * 128], hT[:, c, b0:b0 + blk],
                                    c == 0, c == 7, rd + [bws], [pb])
                        self.evac_cast(dstT[:, dc, b0:b0 + blk], pt[:, 0:blk], [pb],
                                       [self.B(nm, dc, tt_) for tt_ in tl])
            for ck in range(NC):
                q, cl = ck // nck, ck % nck
                t = cl
                cols = slice(q * T_own + cl * 128, q * T_own + cl * 128 + 128)
                rd = hT_bufs(q, t)
                pt = self.ps[2 + ck % 2]
                pb = self.B("ps", 2 + ck % 2)
                for c in range(8):
                    self.mm(pt[:, 0:256], hT[:, c, cols], wk_[:, c, :], c == 0, c == 7, rd + [bw[1]], [pb],
                            signal=False)
                for c in range(8):
                    self.mm(pt[:, 256:512], hT[:, c, cols], wv_[:, c, :], c == 0, c == 7, rd + [bw[2]], [pb])
                self.cp(ktok[:, ck, :], pt[:, 0:256], [pb], [self.B("ktok", ck)], eng=ACT)
                self.cp(vext[:, ck, 0:256], pt[:, 256:512], [pb], [self.B("vext", ck), bvo])
                pt2 = self.ps[4 + ck % 2]
                pb2 = self.B("ps", 4 + ck % 2)
                for c in range(8):
                    self.mm(pt2[:, 0:256], hT[:, c, cols], wo_[:, c, :], c == 0, c == 7, rd + [bw[3]], [pb2])
                self.act(sigo[:, ck, :], pt2[:, 0:256], AF.Sigmoid, [pb2], [self.B("sigo", ck)])
            cntA = 0
            for q in range(nseq):
                for cl in range(nck):
                    for d in range(2):
                        hd = d * 4 + h
                        ck = q * nck + cl
                        t = cl
                        cols = slice(q * T_own + cl * 128, q * T_own + cl * 128 + 128)
                        bank = 2 + cntA % 4
                        cntA += 1
                        pS = self.ps[bank][:, 0:128]
                        bS = self.B("ps", bank)
                        for dc in range(2):
                            self.mm(pS, kT[:, dc, cols], qT[:, dc, cols], dc == 0, dc == 1,
                                    [self.B("kT", dc, (q * T_own) // 128 + t), self.B("qT", dc, (q * T_own) // 128 + t)],
                                    [bS])
                        self.stt(PTall[:, d, ck, :], pS, U64[:, q, cl, hd:hd + 1], (Uf if d == 0 else Ub),
                                 ALU.mult, ALU.mult, [bS, self.B("U64", q), self.B("consts")], [self.B("PT", d, ck)])
            def idx(q, d, step):
                cl = step if d == 0 else nck - 1 - step
                return cl, q * nck + cl

            def emit_kpr(q, d, step):
                hd = d * 4 + h
                cl, ck = idx(q, d, step)
                kdst = kpr[q][d][step % 2]
                self.S.op(ACT, lambda e, kdst=kdst, ck=ck, q=q, cl=cl, hd=hd: e.activation(
                    out=kdst[:], in_=ktok[:, ck, :], func=AF.Copy, scale=Z64[:, q, cl, hd:hd + 1]),
                    [self.B("ktok", ck), self.B("Z64", q)], [self.B("kpr", q, d, step % 2)])

            def emit_U(q, d, step):
                cl, ck = idx(q, d, step)
                par = step % 2
                slot = par if nseq == 1 else q
                bkp = self.B("kpr", q, d, par)
                pUC = self.ps[2 + d * 2 + slot]
                bUC = self.B("ps", 2 + d * 2 + slot)
                for dc in range(2):
                    self.mm(pUC[:, dc * 256:(dc + 1) * 256], kpr[q][d][par][:, dc * 128:(dc + 1) * 128],
                            vext[:, ck, 0:256], True, True, [bkp, self.B("vext", ck)], [bUC])
                n0 = (d * 2 + slot) * 2
                for dc in range(2):
                    self.mm(self.ps[6][:, n0 + dc:n0 + dc + 1], kpr[q][d][par][:, dc * 128:(dc + 1) * 128],
                            vext[:, ck, 256:257], True, True, [bkp, bvo], [self.B("ps", 6)])

            for q in range(nseq):
                for d in range(2):
                    hd = d * 4 + h
                    self.cp(Cst[q][d][:], C0[:, q, hd, :, :], [self.B("C0", q, hd)], [self.B("Cst", q, d)])
                    self.cp(Cbf[q][d][0][:], C0[:, q, hd, :, :], [self.B("C0", q, hd)], [self.B("Cbf", q, d, 0)], eng=ACT)
                    if nck > 1:
                        emit_kpr(q, d, 0)
                        emit_U(q, d, 0)
                    if nck > 2:
                        emit_kpr(q, d, 1)
            for step in range(nck):
                cur, nxt = step % 2, (step + 1) % 2
                for q in range(nseq):
                    for d in range(2):
                        hd = d * 4 + h
                        cl, ck = idx(q, d, step)
                        t = cl
                        cols = slice(q * T_own + cl * 128, q * T_own + cl * 128 + 128)
                        bCs = self.B("Cst", q, d)
                        if step < nck - 1:
                            slot = cur if nseq == 1 else q
                            pUC = self.ps[2 + d * 2 + slot]
                            bUC = self.B("ps", 2 + d * 2 + slot)
                            n0 = (d * 2 + slot) * 2
                            self.stt(Cst[q][d][:, :, 0:256], Cst[q][d][:, :, 0:256], DEC[:, q, cl, hd:hd + 1],
                                     pUC[:].rearrange("p (c e) -> p c e", c=2), ALU.mult, ALU.add,
                                     [bCs, self.B("DEC", q), bUC], [bCs])
                            self.stt(Cst[q][d][:, :, 256], Cst[q][d][:, :, 256], DEC[:, q, cl, hd:hd + 1],
                                     self.ps[6][:, n0:n0 + 2], ALU.mult, ALU.add,
                                     [bCs, self.B("DEC", q), self.B("ps", 6)], [bCs])
                            self.cp(Cbf[q][d][nxt][:], Cst[q][d][:], [bCs], [self.B("Cbf", q, d, nxt)], eng=ACT)
                            if step + 1 < nck - 1:
                                emit_U(q, d, step + 1)
                            if step + 2 < nck - 1:
                                emit_kpr(q, d, step + 2)
                        pO = self.ps[d][:, 0:257]
                        bO = self.B("ps", d)
                        self.mm(pO, PTall[:, d, ck, :], vext[:, ck, :], True, False,
                                [self.B("PT", d, ck), self.B("vext", ck), bvo], [bO], signal=False)
                        for dc in range(2):
                            self.mm(pO, qT[:, dc, cols], Cbf[q][d][cur][:, dc, :], False, dc == 1,
                                    [self.B("qT", dc, (q * T_own) // 128 + t), self.B("Cbf", q, d, cur)], [bO])
                        self.cp(hraw[d][:, ck, :], pO[:, 0:257], [bO], [self.B("hraw", d, ck)], eng=ACT)
            bR = self.B("Rden")
            rdall = [self.B("hraw", d, ck) for d in range(2) for ck in range(NC)]
            for d in range(2):
                self.cp(Ecp[:, d, :].rearrange("p (q c) -> p q c", q=nseq), E64[:, :, :, d * 4 + h],
                        [self.B("E64", q) for q in range(nseq)], [bR])
                self.tt(Rd[:, d, :], hraw[d][:, :, 256], Ecp[:, d, :], ALU.mult, rdall + [bR], [bR])
            self.stt(Rd2[:], Rd[:], -1.0, Rd[:], ALU.mult, ALU.max, [bR], [bR])
            self.ts(Rd2[:], Rd2[:], 1.0, None, ALU.max, None, [bR], [bR])
            self.S.op(DVE, lambda e: e.reciprocal(out=Rd2[:], in_=Rd2[:]), [bR], [bR])
            self.tt(Rd[:], Ecp[:], Rd2[:], ALU.mult, [bR], [bR])
            for d in range(2):
                self.tt(hraw[d][:, :, 0:256], hraw[d][:, :, 0:256],
                        Rd[:, d, :].unsqueeze(2).to_broadcast([128, NC, 256]), ALU.mult,
                        [self.B("hraw", d, ck) for ck in range(NC)] + [bR], [self.B("hraw", d, ck) for ck in range(NC)])
            for ck in range(NC):
                self.tt(hm[:, ck, 0:256], hraw[0][:, ck, 0:256], hraw[1][:, ck, 0:256], ALU.add,
                        [self.B("hraw", 0, ck), self.B("hraw", 1, ck)], [self.B("hm", ck)])
            bss = self.B("ssq")
            for ck in range(NC):
                self.S.op(ACT, lambda e, ck=ck: e.activation(out=hsq[:], in_=hm[:, ck, 0:256], func=AF.Square,
                                                             accum_out=ssq[:, ck:ck + 1]),
                          [self.B("hm", ck)], [bss, self.B("hsq")])
            self.act(ssq[:], ssq[:], AF.Sqrt, [bss], [bss], bias=EPS, scale=1.0 / 256.0)
            self.S.op(DVE, lambda e: e.reciprocal(out=ssq[:], in_=ssq[:]), [bss], [bss])
            for ck in range(NC):
                q, cl = ck // nck, ck % nck
                i2 = ck % 2
                bhm = self.B("hm", ck)
                bhn = self.B("hn", i2)
                self.stt(hm[:, ck, 0:256], hm[:, ck, 0:256], ssq[:, ck:ck + 1], mng[:], ALU.mult, ALU.mult,
                         [bhm, bss, self.B("mng")], [bhm])
                self.tt(hn[i2][:], hm[:, ck, 0:256], sigo[:, ck, :], ALU.mult, [bhm, self.B("sigo", ck)], [bhn])
                ptr = self.ps[7]
                pbt = self.B("ps", 7)
                ptv = ptr[:].bitcast(BF16)
                for dc in range(2):
                    self.tr(ptv[:, i2 * 256 + dc * 128:i2 * 256 + dc * 128 + 128], hn[i2][:, dc * 128:(dc + 1) * 128],
                            self.identb[:], [bhn, self.B("identb")], [pbt], signal=(dc == 1))
                c0 = q * T_own + cl * 128
                self.cp(hmT[:, 2 * h:2 * h + 2, c0:c0 + 128],
                        ptv[:, i2 * 256:i2 * 256 + 256].rearrange("p (c n) -> p c n", c=2),
                        [pbt], [self.B("hmT", h, ck)], eng=ACT)

    def attention(self, g, hT, hT_bufs, KaT, Va, haT):
        I, O = self.I, self.O
        nseq, n_own, T_own = g.nseq, g.n_own, g.T_own
        TT = nseq * T_own
        nkt = g.n_ctx + g.n_cache
        QT = self.sb("QT", [128, 8, TT], BF16)
        wq = [self.sb(f"wqa{i}", [128, 8, 512], BF16) for i in range(2)]
        for i in range(2):
            self.load(wq[i][:], I["wqa"][i].rearrange("p (c n) -> p c n", c=8), [self.B("wqa", i)], q=POOL)
        qn = [self.sb(f"qn{i}", [128, 8, 128], F32) for i in range(2)]
        qb = [self.sb(f"qb{i}", [128, 8, 128], BF16) for i in range(2)]
        scr1 = self.sb("qscr", [128, 8, 128], F32)
        scr = [scr1, scr1]
        ss = [self.sb(f"qss{i}", [128, 8], F32) for i in range(2)]
        t1 = [self.sb(f"qt1{i}", [128, 8, 128], F32) for i in range(2)]
        t2 = [self.sb(f"qt2{i}", [128, 8, 128], F32) for i in range(2)]
        if g.rope:
            ropeT = self.sb("ropeT", [128, n_own, 128], F32)
            self.load(ropeT[:], I["rope"][0:n_own * 128, :].rearrange("(t p) n -> p t n", p=128), [self.B("ropeT")])
        def q_job(q, t):
            tg = q * n_own + t
            i2 = tg % 2
            cols = slice(tg * 128, tg * 128 + 128)
            pts = [self.ps[2 * i2], self.ps[2 * i2 + 1]]
            pbs = [self.B("ps", 2 * i2), self.B("ps", 2 * i2 + 1)]
            for hf in range(2):
                for c in range(8):
                    self.mm(pts[hf][:], hT[:, c, cols], wq[hf][:, c, :], c == 0, c == 7,
                            hT_bufs(q, t) + [self.B("wqa", hf)], [pbs[hf]])
            yield
            bqn, bss = self.B("qn", i2), self.B("qss", i2)
            for hf in range(2):
                yield from self.rms_heads(pts[hf][:].rearrange("p (h d) -> p h d", h=4), 4, self.QG[:],
                                          qn[i2][:, hf * 4:(hf + 1) * 4, :], [pbs[hf]], bqn,
                                          scr[i2][:, hf * 4:(hf + 1) * 4, :], ss[i2][:, hf * 4:(hf + 1) * 4], bss)
            if g.rope:
                yield from self.rope_apply(qn[i2][:], 8, ropeT[:, t, :], self.B("ropeT"), bqn, t1[i2][:], t2[i2][:],
                                           self.B("qrtmp", i2))
            bqb = self.B("qb", i2)
            self.cp(qb[i2][:], qn[i2][:], [bqn], [bqb], eng=ACT)
            yield
            for hf in range(2):
                ptr = self.ps[4 + 2 * i2 + hf]
                pbt = self.B("ps", 4 + 2 * i2 + hf)
                ptv = ptr[:].bitcast(BF16)
                for hh in range(4):
                    self.tr(ptv[:, hh * 128:(hh + 1) * 128], qb[i2][:, hf * 4 + hh, :], self.identb[:],
                            [bqb, self.B("identb")], [pbt], signal=(hh == 3))
                yield
                self.cp(QT[:, hf * 4:(hf + 1) * 4, cols], ptv[:, 0:512].rearrange("p (h n) -> p h n", h=4),
                        [pbt], [self.B("QT", hf, tg)], eng=(ACT if hf else DVE))
                yield

        self.interleave((q_job(q, t) for q in range(nseq) for t in range(n_own)), 2, stagger=10)
        PTs = [self.sb(f"aPT{i}", [128, 512], BF16) for i in range(5)]
        DACC = [self.sb(f"dacc{i}", [128, 512], F32) for i in range(2)]
        rden = [self.sb(f"rden{i}", [128, 512], F32) for i in range(2)]
        qblk = 512 if T_own >= 512 else T_own
        scale = 128.0 ** -0.5
        ONES = self.cm[:, 5, :]
        its = []
        ob = 0
        for q in range(nseq):
            for hq in range(8):
                for b0 in range(0, T_own, qblk):
                    for kt in range(nkt):
                        its.append((q, hq, b0, kt, ob))
                    ob += 1
        use_pool = nkt >= 8

        def issue_st(i):
            q, hq, b0, kt, ob_ = its[i]
            kvh = hq // 4
            c0 = q * T_own + b0
            tgl = range(c0 // 128, (c0 + qblk) // 128)
            pS = self.ps[i % 5]
            bS = self.B("ps", i % 5)
            self.mm(pS[:, 0:qblk], KaT[:, q, kvh, kt * 128:(kt + 1) * 128], QT[:, hq, c0:c0 + qblk],
                    True, True, [self.B("KaT", q, kt)] + [self.B("QT", hq // 4, tg) for tg in tgl], [bS])

        LA = 4
        for i in range(min(LA, len(its))):
            issue_st(i)
        for i in range(len(its)):
            q, hq, b0, kt, ob_ = its[i]
            kvh = hq // 4
            c0 = q * T_own + b0
            tgl = range(c0 // 128, (c0 + qblk) // 128)
            par = ob_ % 2
            pO = self.ps[5 + par]
            bO = self.B("ps", 5 + par)
            pS = self.ps[i % 5]
            bS = self.B("ps", i % 5)
            pt_ = PTs[i % 5]
            bP = self.B("aPT", i % 5)
            self.act(pt_[:, 0:qblk], pS[:, 0:qblk], AF.Exp, [bS], [bP], scale=scale)
            if i + LA < len(its):
                issue_st(i + LA)
            self.mm(pO[:, 0:qblk], Va[:, q, kt, kvh, 0:128], pt_[:, 0:qblk], kt == 0, kt == nkt - 1,
                    [bP, self.B("Va", q, kt)], [bO])
            pD = self.ps[7]
            bD = self.B("ps", 7)
            if kt % 2 == 1:
                self.mm(pD[:, 0:qblk], self.onesb128[:], pt_[:, 0:qblk], kt == 1, False,
                        [bP, self.B("onesb128")], [bD], signal=False)
            else:
                acc = DACC[par]
                bacc = self.B("dacc", par)
                if kt == 0:
                    self.cp(acc[:, 0:qblk], pt_[:, 0:qblk], [bP], [bacc])
                else:
                    self.tt(acc[:, 0:qblk], acc[:, 0:qblk], pt_[:, 0:qblk], ALU.add, [bacc, bP], [bacc])
            if kt == nkt - 1:
                self.mm(pD[:, 0:qblk], ONES, DACC[par][:, 0:qblk], nkt == 1, True,
                        [self.B("dacc", par), self.B("consts")], [bD])
                brd = self.B("rden", par)
                self.S.op(DVE, lambda e, par=par, pD=pD: e.reciprocal(out=rden[par][:, 0:qblk], in_=pD[:, 0:qblk]),
                          [bD], [brd])
                self.tt(haT[:, hq, c0:c0 + qblk], pO[:, 0:qblk], rden[par][:, 0:qblk], ALU.mult, [bO, brd],
                        [self.B("haT", hq, tg) for tg in tgl])

    def post(self, g, hT, hmT, haT):
        I, O = self.I, self.O
        nseq, n_own, T_own = g.nseq, g.n_own, g.T_own
        TT = nseq * T_own
        ntile = TT // 128
        s = g.mset
        xsrc = I["xp"] if g.is_p else I["xs"]
        ydst = O["yp"] if g.is_p else O["ys"]

        def xrow(tg):
            q, t = tg // n_own, tg % n_own
            return (q * g.n_ctx + t) * 128
        x1a = self.sb("x1a", [128, ntile, 1024], F32)
        h2T = self.sb("h2T", [128, 8, TT], BF16)
        mark = self.cur
        w5 = [self.sb(f"w5{i}", [128, 4, 8, 128], BF16) for i in range(3)]
        wo = [self.sb(f"wo{i}", [128, 8, 512], BF16) for i in range(2)]
        mT = self.sb("mT", [128, 8, 512], BF16)
        xin = [self.sb(f"xin5{i}", [128, 1024], F32) for i in range(2)]
        ytmp = [self.sb(f"ytmp{i}", [128, 1024], F32) for i in range(2)]
        rows = self.sb("rows5", [128, 2, 1024], F32)
        sgm = [self.sb(f"sgm{i}", [128, 512], F32) for i in range(2)]
        sga = [self.sb(f"sga{i}", [128, 512], F32) for i in range(2)]
        m1 = [self.sb(f"m1{i}", [128, 512], F32) for i in range(2)]
        st = self.sb("st5", [128, 2, 12], F32)
        mv = self.sb("mv5", [128, 2, 4], F32)
        brow = self.B("rows5")
        self.load(rows[:, 0, :], I["ln1g"].partition_broadcast(128), [brow])
        self.load(rows[:, 1, :], I["ln1b"].partition_broadcast(128), [brow])
        self.ts(rows[:], rows[:], ALPHA, None, ALU.mult, None, [brow], [brow])
        S2 = self.COL[:, s, 2, :]
        B2 = self.COL[:, s, 3, :]
        bcol = self.B("COL", s)
        nblk = TT // 512
        xc = 0
        for blk in range(nblk):
            b0 = blk * 512
            tgl = list(range(b0 // 128, b0 // 128 + 4))
            rd_h = [self.B("hT", g.name, tg // n_own, tg % n_own, c) for tg in tgl for c in range(8)]
            rd_hm = [self.B("hmT", h, ck) for h in range(4) for ck in range(b0 // 128, b0 // 128 + 4)]
            rd_ha = [self.B("haT", hq, tg) for hq in range(8) for tg in tgl]
            for fc in range(8):
                wi = (blk * 8 + fc) % 3
                w = w5[wi]
                bw = self.B("w5", wi)
                self.load(w[:], I["w5"][fc].rearrange("p (k c n) -> p k c n", k=4, c=8), [bw], q=POOL)
                if blk == 0 and fc in (2, 3):
                    self.load(wo[fc - 2][:], I["wout"][fc - 2].rearrange("p (c n) -> p c n", c=8),
                              [self.B("wo", fc - 2)], q=POOL)
                srcs = (hmT, haT, hT, hT)
                rds = (rd_hm, rd_ha, rd_h, rd_h)
                pof = 4 * (fc % 2)
                pss = [self.ps[pof + k] for k in range(4)]
                pbs = [self.B("ps", pof + k) for k in range(4)]
                for k in (2, 3, 0, 1):
                    for c in range(8):
                        self.mm(pss[k][:], w[:, k, c, :], srcs[k][:, c, b0:b0 + 512], c == 0, c == 7,
                                rds[k] + [bw], [pbs[k]])
                f2 = fc % 2
                bsg, bsa, bm1 = self.B("sgm", f2), self.B("sga", f2), self.B("m1", f2)
                self.act(sgm[f2][:], pss[2][:], AF.Sigmoid, [pbs[2]], [bsg])
                self.act(sga[f2][:], pss[3][:], AF.Sigmoid, [pbs[3]], [bsa])
                self.tt(m1[f2][:], pss[0][:], sgm[f2][:], ALU.mult, [pbs[0], bsg], [bm1])
                self.tt(sga[f2][:], pss[1][:], sga[f2][:], ALU.mult, [pbs[1], bsa], [bsa])
                self.tt(mT[:, fc, :], m1[f2][:], sga[f2][:], ALU.add, [bm1, bsa], [self.B("mT", fc)])
            rd_m = [self.B("mT", fc) for fc in range(8)]

            def tile_front(ti, tg):
                nonlocal xc
                i2 = xc % 2
                xc += 1
                bx = self.B("xin5", i2)
                self.load(xin[i2][:], xsrc[xrow(tg):xrow(tg) + 128, :], [bx])
                pm = [self.ps[4 + (ti % 2) * 2], self.ps[5 + (ti % 2) * 2]]
                bpm = [self.B("ps", 4 + (ti % 2) * 2), self.B("ps", 5 + (ti % 2) * 2)]
                for hf in range(2):
                    for c in range(8):
                        self.mm(pm[hf][:], mT[:, c, ti * 128:(ti + 1) * 128], wo[hf][:, c, :], c == 0, c == 7,
                                rd_m + [self.B("wo", hf)], [bpm[hf]])
                y = ytmp[i2]
                by = self.B("ytmp", i2)
                for hf in range(2):
                    self.tt(y[:, hf * 512:(hf + 1) * 512], pm[hf][:], self.G1[:, s, hf * 512:(hf + 1) * 512], ALU.mult,
                            [bpm[hf], self.B("G1", s)], [by])
                self.stt(y[:], xin[i2][:], ALPHA, y[:], ALU.mult, ALU.add, [bx, by], [by])
                self.layernorm(y, by, st, mv, i2)
                bxa = self.B("x1a", tg)
                self.tt(x1a[:, tg, :], y[:], rows[:, 0, :], ALU.mult, [by, brow], [bxa])
                self.tt(x1a[:, tg, :], x1a[:, tg, :], rows[:, 1, :], ALU.add, [bxa, brow], [bxa])
                return i2

            def tile_back(ti, tg, i2):
                y = ytmp[i2]
                by = self.B("ytmp", i2)
                pbank = [self.ps[0 + (ti % 2) * 2], self.ps[1 + (ti % 2) * 2]]
                pbb = [self.B("ps", 0 + (ti % 2) * 2), self.B("ps", 1 + (ti % 2) * 2)]
                for c in range(8):
                    self.tr(pbank[c // 4][:, (c % 4) * 128:(c % 4 + 1) * 128], y[:, c * 128:(c + 1) * 128],
                            self.ident, [by, self.B("consts")], [pbb[c // 4]], signal=(c % 4 == 3))
                for c in range(8):
                    src = pbank[c // 4][:, (c % 4) * 128:(c % 4 + 1) * 128]
                    o_ap = h2T[:, c, tg * 128:(tg + 1) * 128]
                    bh = self.B("h2T", tg, c)
                    if c // 4 == 0:
                        self.S.op(ACT, lambda e, o_ap=o_ap, src=src, c=c: e.activation(
                            out=o_ap, in_=src, func=AF.Identity, scale=S2[:, c:c + 1], bias=B2[:, c:c + 1]),
                            [pbb[c // 4], bcol], [bh])
                    else:
                        self.ts(o_ap, src, S2[:, c:c + 1], B2[:, c:c + 1], ALU.mult, ALU.add, [pbb[c // 4], bcol], [bh])

            prev = None
            for ti, tg in enumerate(tgl):
                i2 = tile_front(ti, tg)
                if prev is not None:
                    tile_back(*prev)
                prev = (ti, tg, i2)
            tile_back(*prev)
        self.S.barrier()
        self.cur = self.base + self.arena0
        uT = self.sb("uT", [128, 32, 512], BF16)
        rows6 = self.sb("rows6", [128, 2, 1024], F32)
        y2 = [self.sb(f"y2{i}", [128, 1024], F32) for i in range(2)]
        assert self.cur <= self.base + self.arena0 + 48 * 1024
        self.cur = mark
        wd = self.sb("wd", [128, 32, 1024], BF16)
        wu = [self.sb(f"wu{i}", [128, 8, 256], BF16) for i in range(4)]
        ur = [self.sb(f"ur{i}", [128, 512], F32) for i in range(2)]
        st = self.sb("st6", [128, 2, 12], F32)
        mv = self.sb("mv6", [128, 2, 4], F32)
        br6 = self.B("rows6")
        self.load(rows6[:, 0, :], I["ln2g"].partition_broadcast(128), [br6])
        self.load(rows6[:, 1, :], I["ln2b"].partition_broadcast(128), [br6])
        wc = 0
        two_path = g.is_p
        pcs = 0
        if two_path:
            wstg = [self.sb(f"wdstg{i}", [128, 2, 1024], F32) for i in range(2)]
        for blk in range(nblk):
            b0 = blk * 512
            tgl = list(range(b0 // 128, b0 // 128 + 4))
            rd_h2 = [self.B("h2T", tg, c) for tg in tgl for c in range(8)]
            for sl in range(16):
                wi = wc % 4
                wc += 1
                bw = self.B("wu", wi)
                self.load(wu[wi][:], I["wup"][sl].rearrange("p (c n) -> p c n", c=8), [bw], q=POOL)
                if two_path and sl % 4 == 0:
                    qd = sl // 4
                    for piece in range(4):
                        stg = wstg[pcs % 2]
                        bst = self.B("wdstg", pcs % 2)
                        pcs += 1
                        self.load(stg[:], I["wdn"][qd][:, piece * 2048:(piece + 1) * 2048].rearrange(
                            "p (c n) -> p c n", c=2), [bst], q=SP)
                        fc0 = qd * 8 + piece * 2
                        self.cp(wd[:, fc0:fc0 + 2, :], stg[:], [bst], [self.B("wd", qd)])
                if (not two_path) and sl % 4 == 3:
                    qd = sl // 4
                    self.load(wd[:, qd * 8:(qd + 1) * 8, :], I["wdn"][qd].rearrange("p (c n) -> p c n", c=8),
                              [self.B("wd", qd)], q=POOL)
                for f4 in range(2):
                    fc = sl * 2 + f4
                    pt = self.ps[fc % 4]
                    pb = self.B("ps", fc % 4)
                    for c in range(8):
                        self.mm(pt[:], wu[wi][:, c, f4 * 128:(f4 + 1) * 128], h2T[:, c, b0:b0 + 512], c == 0, c == 7,
                                rd_h2 + [bw], [pb])
                    bur = self.B("ur", fc % 2)
                    self.act(ur[fc % 2][:], pt[:], AF.Relu, [pb], [bur])
                    self.tt(uT[:, fc, :], ur[fc % 2][:], ur[fc % 2][:], ALU.mult, [bur], [self.B("uT", fc)])
            rd_u = [self.B("uT", fc) for fc in range(32)]
            for ti, tg in enumerate(tgl):
                i2 = ti % 2
                pm = [self.ps[4 + i2 * 2], self.ps[5 + i2 * 2]]
                bpm = [self.B("ps", 4 + i2 * 2), self.B("ps", 5 + i2 * 2)]
                for hf in range(2):
                    for fc in range(32):
                        self.mm(pm[hf][:], uT[:, fc, ti * 128:(ti + 1) * 128], wd[:, fc, hf * 512:(hf + 1) * 512],
                                fc == 0, fc == 31, rd_u + [self.B("wd", fc // 8)], [bpm[hf]])
                y = y2[i2]
                by = self.B("y2", i2)
                for hf in range(2):
                    self.tt(y[:, hf * 512:(hf + 1) * 512], pm[hf][:], self.G2[:, s, hf * 512:(hf + 1) * 512], ALU.mult,
                            [bpm[hf], self.B("G2", s)], [by])
                self.tt(y[:], y[:], x1a[:, tg, :], ALU.add, [by, self.B("x1a", tg)], [by])
                self.layernorm(y, by, st, mv, i2)
                self.tt(y[:], y[:], rows6[:, 0, :], ALU.mult, [by, br6], [by])
                self.tt(y[:], y[:], rows6[:, 1, :], ALU.add, [by, br6], [by])
                self.store(ydst[tg * 128:(tg + 1) * 128, :], y[:], [by])

    def layernorm(self, y, by, st, mv, i2):
        bst = self.B("lnst", i2)
        self.S.op(DVE, lambda e: e.bn_stats(out=st[:, i2, 0:6], in_=y[:, 0:512]), [by], [bst])
        self.S.op(DVE, lambda e: e.bn_stats(out=st[:, i2, 6:12], in_=y[:, 512:1024]), [by], [bst])
        self.S.op(DVE, lambda e: e.bn_aggr(out=mv[:, i2, 0:2], in_=st[:, i2, :]), [bst], [bst])
        self.act(mv[:, i2, 2:3], mv[:, i2, 1:2], AF.Sqrt, [bst], [bst], bias=EPS, scale=1.0)
        self.S.op(DVE, lambda e: e.reciprocal(out=mv[:, i2, 2:3], in_=mv[:, i2, 2:3]), [bst], [bst])
        self.ts(y[:], y[:], mv[:, i2, 0:1], mv[:, i2, 2:3], ALU.subtract, ALU.mult, [by, bst], [by])


def _lay(w):
    n = w.shape[1]
    return np.ascontiguousarray(w.reshape(8, 128, n).transpose(1, 0, 2).reshape(128, 8 * n))


def _rope_tables():
    T, GW, NF = 4096, 64, 32
    rows = T // GW
    row = np.repeat(np.arange(rows), GW)
    col = np.tile(np.arange(GW), rows)
    inv = (np.float32(10000.0) ** (-np.arange(NF, dtype=np.float32) / np.float32(NF))).astype(np.float32)
    ang = np.stack([row, col], -1).astype(np.float32)[..., None] * inv
    return np.concatenate([np.cos(ang).reshape(T, 64), np.sin(ang).reshape(T, 64)], axis=1).astype(np.float32)


_NC_CACHE = {}


def kernel(x_prompt, x_sample, cache_k, cache_v, state_C, state_n, state_m, c, c_ctx,
           w_mod, b_mod, w_in, b_gates, mlstm_norm_g, q_norm_g, k_norm_g, w_bm, w_ba, w_out,
           ln1_g, ln1_b, w_up, w_down, ln2_g, ln2_b, _dbg=None, _stop=None):
    f = lambda a: np.ascontiguousarray(np.asarray(a, dtype=np.float32))
    x_prompt, x_sample, w_in0 = f(x_prompt), f(x_sample), f(w_in)[0]
    w_mod0, w_bm0, w_ba0, w_out0, w_up0, w_down0 = f(w_mod)[0], f(w_bm)[0], f(w_ba)[0], f(w_out)[0], f(w_up)[0], f(w_down)[0]
    perm = np.array([0, 1, 2, 3, 8, 9, 10, 11, 4, 5, 6, 7, 12, 13, 14, 15])
    shared = {}
    shared["wmod"] = np.stack([_lay(w_mod0[:, j * 512:(j + 1) * 512]) for j in range(12)])
    shared["bmod"] = f(b_mod)[0]
    shared["wg"] = _lay(w_in0[:, 4096 + perm])
    shared["bg"] = f(b_gates)[0][perm].copy()
    shared["wml"] = np.stack([np.stack([_lay(w_in0[:, p * 1024 + h * 256:p * 1024 + (h + 1) * 256]) for p in range(4)])
                              for h in range(4)])
    shared["wqa"] = np.stack([_lay(w_in0[:, 4112 + i * 512:4112 + (i + 1) * 512]) for i in range(2)])
    shared["wkva"] = _lay(w_in0[:, 5136:5648])
    w5 = []
    for fc in range(8):
        sl = slice(fc * 128, (fc + 1) * 128)
        parts = [w_bm0[:, sl], w_ba0[:, sl], w_in0[:, 5648 + fc * 128:5648 + (fc + 1) * 128],
                 w_in0[:, 6672 + fc * 128:6672 + (fc + 1) * 128]]
        w5.append(np.concatenate([_lay(p) for p in parts], axis=1))
    shared["w5"] = np.stack(w5)
    shared["wout"] = np.stack([_lay(w_out0[:, i * 512:(i + 1) * 512]) for i in range(2)])
    shared["wup"] = np.stack([_lay(w_up0[:, i * 256:(i + 1) * 256]) for i in range(16)])
    shared["wdn"] = np.ascontiguousarray(w_down0.reshape(4, 8, 128, 1024).transpose(0, 2, 1, 3).reshape(4, 128, 8192))
    shared["mng"] = f(mlstm_norm_g)[0]
    shared["qg"] = f(q_norm_g)[0]
    shared["kg"] = f(k_norm_g)[0]
    shared["ln1g"], shared["ln1b"] = f(ln1_g)[0], f(ln1_b)[0]
    shared["ln2g"], shared["ln2b"] = f(ln2_g)[0], f(ln2_b)[0]
    p = np.arange(128)
    cm = np.zeros((128, 6, 128), np.float32)
    cm[:, 0, :] = (p[:, None] == p[None, :])
    cm[:, 1, :] = (p[:, None] <= p[None, :])
    cm[:, 2, :] = (p[:, None] >= p[None, :])
    cm[:, 3, :] = (p[:, None] < p[None, :])
    cm[:, 4, :] = (p[:, None] > p[None, :])
    cm[:, 5, :] = 1.0
    shared["cm"] = cm
    rope = _rope_tables()
    in_maps = []
    for r in range(8):
        b, j = r // 4, r % 4
        order = [(j + i) % 4 for i in range(4)]
        m = dict(shared)
        m["xp"] = x_prompt[2 * r:2 * r + 2].reshape(512, 1024)
        m["xs"] = np.ascontiguousarray(x_sample[b].reshape(4, 1024, 1024)[order].reshape(4096, 1024))
        m["rope"] = np.ascontiguousarray(rope.reshape(4, 1024, 128)[order].reshape(4096, 128))
        m["ck"] = f(cache_k)[b, 0].reshape(256, 256)
        m["cv"] = f(cache_v)[b, 0].reshape(256, 256)
        m["sC"] = f(state_C)[b, 0].reshape(8, 256, 256)
        m["sn"] = f(state_n)[b, 0].reshape(8, 256)
        m["sm"] = f(state_m)[b, 0].reshape(8)
        cvec = np.stack([f(c_ctx), f(c)[b]])
        m["cT"] = np.ascontiguousarray(cvec.reshape(2, 8, 128).transpose(2, 1, 0).reshape(128, 16))
        blk = np.zeros((128, 2, 24), np.float32)
        for tau in range(24):
            i = tau // 8 + 1
            vb = 1.0 if i <= 3 - j else 0.0
            blk[:, 0, tau] = 1.0 - vb
            blk[:, 1, tau] = vb
        m["blk"] = blk
        in_maps.append({k: np.ascontiguousarray(v, dtype=np.float32) for k, v in m.items()})
    key = (tuple(_dbg) if _dbg else None, _stop)
    if key not in _NC_CACHE:
        _NC_CACHE[key] = Builder(dbg=_dbg, stop=_stop).build()
    nc = _NC_CACHE[key]
    res = run_bass_kernel_spmd(nc, in_maps, core_ids=list(range(8)))
    R = res.results
    y_prompt = np.zeros((16, 256, 1024), np.float32)
    y_sample = np.zeros((2, 4096, 1024), np.float32)
    nk = np.zeros((16, 1, 256, 2, 128), np.float32)
    nv = np.zeros((16, 1, 256, 2, 128), np.float32)
    nC = np.zeros((16, 1, 2, 4, 256, 256), np.float32)
    nn = np.zeros((16, 1, 2, 4, 256), np.float32)
    nm = np.zeros((16, 1, 2, 4), np.float32)
    for r in range(8):
        b, j = r // 4, r % 4
        o = R[r]
        y_prompt[2 * r:2 * r + 2] = o["yp"].reshape(2, 256, 1024)
        y_sample[b, j * 1024:(j + 1) * 1024] = o["ys"]
        nk[2 * r:2 * r + 2, 0] = o["kc"].reshape(2, 256, 2, 128)
        nv[2 * r:2 * r + 2, 0] = o["vc"].reshape(2, 256, 2, 128)
        nC[2 * r:2 * r + 2, 0] = o["Cn"].reshape(2, 2, 4, 256, 256)
        nn[2 * r:2 * r + 2, 0] = o["nn"].reshape(2, 2, 4, 256)
        nm[2 * r:2 * r + 2, 0] = o["mn"].reshape(2, 2, 4)
    if _dbg:
        kernel._dbg_out = [R[r]["dbg"] for r in range(8)]
    return (y_prompt, y_sample, nk, nv, nC, nn, nm)
```
